# Optimizing a Trainium2 kernel written in Bass

```python
import math
import jax, jax.numpy as jnp
from jax import lax
import numpy as np

D_MODEL = 1024
BATCH = 4
SEQ = 4096
DEPTH = 2
DEC_BATCH = 8
DEC_SEQ = 8192
PAST_LEN = 128

N_EVEN = (DEPTH + 1) // 2
N_ODD = DEPTH // 2
SSM_WIDTH = D_MODEL // 2
SSM_GROUP = 16
SSM_GROUPS = SSM_WIDTH // SSM_GROUP
SSM_STATE = 64
DT_MIN = 1e-3
DT_MAX = 1e-1
GM_WIDTH = D_MODEL // 2
GM_HEADS = 4
GM_HEAD_DIM = GM_WIDTH // GM_HEADS
GM_CHUNK = 128
EVEN_IN = SSM_WIDTH + 2 * GM_WIDTH
HEAD_DIM = 64
N_HEADS = D_MODEL // HEAD_DIM
N_KV_HEADS = 4
Q_PER_KV = N_HEADS // N_KV_HEADS
WINDOW = 128
ATT_BLOCK = 128
KEY_SPAN = ATT_BLOCK + 2 * WINDOW
ODD_IN = (N_HEADS + 2 * N_KV_HEADS) * HEAD_DIM
REL_BUCKETS = 32
REL_MAX_DIST = 128
D_FF = int(math.ceil(8 * D_MODEL / 3 / 256)) * 256
EPS = 1e-6
NEG_INF = -1e30

kernel_name = "hybrid_s5_gmlp_swa_encoder"


def _rmsnorm(x, g):
    xf = x.astype(jnp.float32)
    y = xf * lax.rsqrt(jnp.mean(xf * xf, axis=-1, keepdims=True) + EPS)
    return (y * g.astype(jnp.float32)).astype(x.dtype)


def _cscan_op(e1, e2):
    a1r, a1i, b1r, b1i = e1
    a2r, a2i, b2r, b2i = e2
    return (a2r * a1r - a2i * a1i,
            a2r * a1i + a2i * a1r,
            a2r * b1r - a2i * b1i + b2r,
            a2r * b1i + a2i * b1r + b2i)


def _s5_mixer(u, lam_re, lam_im, log_dt, b_re, b_im, c_re, c_im, d_skip, glu_w, glu_b):
    bsz, L, _ = u.shape
    uf = u.astype(jnp.float32).reshape(bsz, L, SSM_GROUPS, SSM_GROUP)
    y = uf * d_skip.astype(jnp.float32).reshape(SSM_GROUPS, SSM_GROUP)
    for direction in range(2):
        lr = lam_re[direction].astype(jnp.float32)
        li = lam_im[direction].astype(jnp.float32)
        dt = jnp.exp(log_dt[direction].astype(jnp.float32))[:, None]
        mag = jnp.exp(lr * dt)
        ar = mag * jnp.cos(li * dt)
        ai = mag * jnp.sin(li * dt)
        den = lr * lr + li * li
        zr = ((ar - 1.0) * lr + ai * li) / den
        zi = (ai * lr - (ar - 1.0) * li) / den
        br = b_re[direction].astype(jnp.float32)
        bi = b_im[direction].astype(jnp.float32)
        bbr = zr[..., None] * br - zi[..., None] * bi
        bbi = zr[..., None] * bi + zi[..., None] * br
        xr = jnp.einsum('blgh,gph->blgp', uf, bbr)
        xi = jnp.einsum('blgh,gph->blgp', uf, bbi)
        a_r = jnp.broadcast_to(ar[None, None], (1, L, SSM_GROUPS, SSM_STATE))
        a_i = jnp.broadcast_to(ai[None, None], (1, L, SSM_GROUPS, SSM_STATE))
        _, _, sr, si = lax.associative_scan(_cscan_op, (a_r, a_i, xr, xi), reverse=(direction == 1), axis=1)
        y = y + jnp.einsum('blgp,ghp->blgh', sr, c_re[direction].astype(jnp.float32)) \
              - jnp.einsum('blgp,ghp->blgh', si, c_im[direction].astype(jnp.float32))
    y = jax.nn.gelu(y.reshape(bsz, L, SSM_WIDTH))
    y = y * jax.nn.sigmoid(y @ glu_w.astype(jnp.float32) + glu_b.astype(jnp.float32))
    return y.astype(u.dtype)


def _spatial_gating(z_u, z_v, gm_norm, gm_w_s, gm_b_s):
    bsz, L, _ = z_u.shape
    nc = L // GM_CHUNK
    v = _rmsnorm(z_v.reshape(bsz, nc, GM_CHUNK, GM_HEADS, GM_HEAD_DIM), gm_norm)
    mixed = jnp.einsum('hij,bnjhd->bnihd', gm_w_s, v) + gm_b_s.T[None, None, :, :, None]
    return z_u * mixed.reshape(bsz, L, GM_WIDTH).astype(z_u.dtype)


def _rel_bucket(rel):
    half = REL_BUCKETS // 2
    max_exact = half // 2
    ret = (rel > 0).astype(jnp.int32) * half
    n = jnp.abs(rel)
    nf = jnp.maximum(n, 1).astype(jnp.float32)
    large = max_exact + (jnp.log(nf / max_exact) / math.log(REL_MAX_DIST / max_exact)
                         * (half - max_exact)).astype(jnp.int32)
    large = jnp.minimum(large, half - 1)
    return ret + jnp.where(n < max_exact, n, large)


def _window_attention(h, w_qkv, q_norm, k_norm, attn_sink, rel_table, w_o):
    bsz, L, _ = h.shape
    qkv = h @ w_qkv
    q = qkv[..., :N_HEADS * HEAD_DIM].reshape(bsz, L, N_HEADS, HEAD_DIM)
    k = qkv[..., N_HEADS * HEAD_DIM:(N_HEADS + N_KV_HEADS) * HEAD_DIM].reshape(bsz, L, N_KV_HEADS, HEAD_DIM)
    v = qkv[..., (N_HEADS + N_KV_HEADS) * HEAD_DIM:].reshape(bsz, L, N_KV_HEADS, HEAD_DIM)
    q = _rmsnorm(q, q_norm) * (HEAD_DIM ** -0.5)
    k = _rmsnorm(k, k_norm)
    pad = ((0, 0), (WINDOW, WINDOW), (0, 0), (0, 0))
    kp = jnp.pad(k, pad)
    vp = jnp.pad(v, pad)
    nb = L // ATT_BLOCK
    qi = jnp.arange(ATT_BLOCK)[:, None]
    kj = jnp.arange(KEY_SPAN)[None, :]
    rel = kj - WINDOW - qi
    band = jnp.abs(rel) <= WINDOW
    bias = rel_table[_rel_bucket(rel)].transpose(2, 0, 1).astype(jnp.float32)
    sink = attn_sink.astype(jnp.float32)[None, :, None]

    def block(b):
        start = b * ATT_BLOCK
        qb = lax.dynamic_slice_in_dim(q, start, ATT_BLOCK, axis=1)
        qb = qb.reshape(bsz, ATT_BLOCK, N_KV_HEADS, Q_PER_KV, HEAD_DIM)
        kb = lax.dynamic_slice_in_dim(kp, start, KEY_SPAN, axis=1)
        vb = lax.dynamic_slice_in_dim(vp, start, KEY_SPAN, axis=1)
        s = jnp.einsum('bqkgd,bskd->bkgqs', qb, kb).astype(jnp.float32)
        s = s.reshape(bsz, N_HEADS, ATT_BLOCK, KEY_SPAN) + bias
        kpos = start - WINDOW + jnp.arange(KEY_SPAN)
        valid = band & ((kpos >= 0) & (kpos < L))[None, :]
        s = jnp.where(valid, s, NEG_INF)
        m = jnp.maximum(jnp.max(s, axis=-1), sink)
        p = jnp.exp(s - m[..., None])
        p = p / (jnp.sum(p, axis=-1) + jnp.exp(sink - m))[..., None]
        p = p.reshape(bsz, N_KV_HEADS, Q_PER_KV, ATT_BLOCK, KEY_SPAN).astype(vb.dtype)
        o = jnp.einsum('bkgqs,bskd->bqkgd', p, vb)
        return o.reshape(bsz, ATT_BLOCK, D_MODEL)

    out = lax.map(block, jnp.arange(nb))
    out = out.transpose(1, 0, 2, 3).reshape(bsz, L, D_MODEL)
    return out @ w_o


def _swiglu(h, w_gate, w_up, w_down):
    return (jax.nn.silu(h @ w_gate) * (h @ w_up)) @ w_down


def _trunk(x, norm_mix, norm_ffn, w_in_even, ssm_lam_re, ssm_lam_im, ssm_log_dt, ssm_b_re, ssm_b_im,
           ssm_c_re, ssm_c_im, ssm_d, glu_w, glu_b, gm_norm, gm_w_s, gm_b_s, w_out_even,
           w_qkv, q_norm, k_norm, attn_sink, w_o, rel_table, ffn_w_gate, ffn_w_up, ffn_w_down):
    e = 0
    o = 0
    for layer in range(DEPTH):
        h = _rmsnorm(x, norm_mix[layer])
        if layer % 2 == 0:
            z = h @ w_in_even[e]
            u_a = z[..., :SSM_WIDTH]
            u_b = jax.nn.gelu(z[..., SSM_WIDTH:SSM_WIDTH + GM_WIDTH])
            v_b = jax.nn.gelu(z[..., SSM_WIDTH + GM_WIDTH:])
            y_a = _s5_mixer(u_a, ssm_lam_re[e], ssm_lam_im[e], ssm_log_dt[e], ssm_b_re[e], ssm_b_im[e],
                            ssm_c_re[e], ssm_c_im[e], ssm_d[e], glu_w[e], glu_b[e])
            y_b = _spatial_gating(u_b, v_b, gm_norm[e], gm_w_s[e], gm_b_s[e])
            mix = jnp.concatenate([y_a, y_b], axis=-1) @ w_out_even[e]
            e += 1
        else:
            mix = _window_attention(h, w_qkv[o], q_norm[o], k_norm[o], attn_sink[o], rel_table, w_o[o])
            o += 1
        x = x + mix.astype(x.dtype)
        x = x + _swiglu(_rmsnorm(x, norm_ffn[layer]), ffn_w_gate[layer], ffn_w_up[layer],
                        ffn_w_down[layer]).astype(x.dtype)
    return x


def setup_inputs(seed: int = 0) -> dict:
    key = jax.random.key(seed)
    ks = jax.random.split(key, 32)
    f32 = jnp.float32
    nrm = lambda k, shape, s: jax.random.normal(k, shape, f32) * s
    P, G, H = SSM_STATE, SSM_GROUPS, SSM_GROUP
    lam_im_base = jnp.broadcast_to(jnp.pi * jnp.arange(P, dtype=f32), (N_EVEN, 2, G, P))
    return {
        "x_prompt": jax.random.normal(ks[0], (BATCH, SEQ, D_MODEL), f32),
        "x_sample": jax.random.normal(ks[1], (DEC_BATCH, DEC_SEQ, D_MODEL), f32),
        "norm_mix": 1.0 + nrm(ks[2], (DEPTH, D_MODEL), 0.01),
        "norm_ffn": 1.0 + nrm(ks[3], (DEPTH, D_MODEL), 0.01),
        "w_in_even": nrm(ks[4], (N_EVEN, D_MODEL, EVEN_IN), D_MODEL ** -0.5),
        "ssm_lam_re": -0.5 + nrm(ks[5], (N_EVEN, 2, G, P), 0.01),
        "ssm_lam_im": lam_im_base + nrm(ks[6], (N_EVEN, 2, G, P), 0.01),
        "ssm_log_dt": jax.random.uniform(ks[7], (N_EVEN, 2, G), f32, math.log(DT_MIN), math.log(DT_MAX)),
        "ssm_b_re": nrm(ks[8], (N_EVEN, 2, G, P, H), (2.0 * H) ** -0.5),
        "ssm_b_im": nrm(ks[9], (N_EVEN, 2, G, P, H), (2.0 * H) ** -0.5),
        "ssm_c_re": nrm(ks[10], (N_EVEN, 2, G, H, P), (2.0 / P) ** 0.5),
        "ssm_c_im": nrm(ks[11], (N_EVEN, 2, G, H, P), (2.0 / P) ** 0.5),
        "ssm_d": nrm(ks[12], (N_EVEN, SSM_WIDTH), 1.0),
        "glu_w": nrm(ks[13], (N_EVEN, SSM_WIDTH, SSM_WIDTH), SSM_WIDTH ** -0.5),
        "glu_b": nrm(ks[14], (N_EVEN, SSM_WIDTH), 0.01),
        "gm_norm": 1.0 + nrm(ks[15], (N_EVEN, GM_HEADS, GM_HEAD_DIM), 0.01),
        "gm_w_s": nrm(ks[16], (N_EVEN, GM_HEADS, GM_CHUNK, GM_CHUNK), GM_CHUNK ** -0.5),
        "gm_b_s": 1.0 + nrm(ks[17], (N_EVEN, GM_HEADS, GM_CHUNK), 0.1),
        "w_out_even": nrm(ks[18], (N_EVEN, D_MODEL, D_MODEL), D_MODEL ** -0.5),
        "w_qkv": nrm(ks[19], (N_ODD, D_MODEL, ODD_IN), D_MODEL ** -0.5),
        "q_norm": 1.0 + nrm(ks[20], (N_ODD, HEAD_DIM), 0.01),
        "k_norm": 1.0 + nrm(ks[21], (N_ODD, HEAD_DIM), 0.01),
        "attn_sink": nrm(ks[22], (N_ODD, N_HEADS), 0.5),
        "w_o": nrm(ks[23], (N_ODD, D_MODEL, D_MODEL), D_MODEL ** -0.5),
        "rel_table": nrm(ks[24], (REL_BUCKETS, N_HEADS), 0.5),
        "ffn_w_gate": nrm(ks[25], (DEPTH, D_MODEL, D_FF), D_MODEL ** -0.5),
        "ffn_w_up": nrm(ks[26], (DEPTH, D_MODEL, D_FF), D_MODEL ** -0.5),
        "ffn_w_down": nrm(ks[27], (DEPTH, D_FF, D_MODEL), D_FF ** -0.5),
    }


def reference(x_prompt, x_sample, norm_mix, norm_ffn, w_in_even, ssm_lam_re, ssm_lam_im, ssm_log_dt,
              ssm_b_re, ssm_b_im, ssm_c_re, ssm_c_im, ssm_d, glu_w, glu_b, gm_norm, gm_w_s, gm_b_s,
              w_out_even, w_qkv, q_norm, k_norm, attn_sink, w_o, rel_table, ffn_w_gate, ffn_w_up,
              ffn_w_down):
    y_prompt = _trunk(x_prompt, norm_mix, norm_ffn, w_in_even, ssm_lam_re, ssm_lam_im, ssm_log_dt,
                      ssm_b_re, ssm_b_im, ssm_c_re, ssm_c_im, ssm_d, glu_w, glu_b, gm_norm, gm_w_s, gm_b_s,
                      w_out_even, w_qkv, q_norm, k_norm, attn_sink, w_o, rel_table, ffn_w_gate, ffn_w_up,
                      ffn_w_down)
    y_sample = _trunk(x_sample, norm_mix, norm_ffn, w_in_even, ssm_lam_re, ssm_lam_im, ssm_log_dt,
                      ssm_b_re, ssm_b_im, ssm_c_re, ssm_c_im, ssm_d, glu_w, glu_b, gm_norm, gm_w_s, gm_b_s,
                      w_out_even, w_qkv, q_norm, k_norm, attn_sink, w_o, rel_table, ffn_w_gate, ffn_w_up,
                      ffn_w_down)
    return (y_prompt, y_sample)
```

```python
import contextlib
import numpy as np
import ml_dtypes
import concourse.bass as bass
import concourse.mybir as mybir
from concourse.bass_utils import run_bass_kernel_spmd

F32 = mybir.dt.float32
BF16 = mybir.dt.bfloat16
AF = mybir.ActivationFunctionType
ALU = mybir.AluOpType
AX = mybir.AxisListType

D = 1024
FF = 2816
NJ = FF // 128
EPS = 1e-6
MAGIC = 12582912.0
TWO_PI = float(2 * np.pi)
ENGS = ['pe', 'act', 'dve', 'pool', 'sp']
NSLOT = 27


class Prog:
    def __init__(self, nc, stack):
        self.nc = nc
        self.stack = stack
        self.ops = {e: [] for e in ENGS}
        self.cnt = {e: 0 for e in ENGS}
        self.seen = {e: {} for e in ENGS}
        self.reg = {}
        self.dmasem = {}
        self.sems = {}

    def sem(self, key):
        if key not in self.sems:
            self.sems[key] = self.stack.enter_context(self.nc.semaphore("s_" + str(key)))
        return self.sems[key]

    def _deps(self, eng, reads, writes):
        need = {}

        def add(k, v):
            if v > need.get(k, 0):
                need[k] = v
        for r in reads:
            st = self.reg.get(r)
            if st and st['w']:
                add(*st['w'])
        for r in writes:
            st = self.reg.get(r)
            if st:
                if st['w']:
                    add(*st['w'])
                for k, v in st['r'].items():
                    add(k, v)
        seen = self.seen[eng]
        waits = []
        for k, v in need.items():
            if eng == 'pe' and k == 'pe':
                continue
            if k.startswith('d_'):
                v = self.dmasem[k]
            if seen.get(k, 0) < v:
                seen[k] = v
                waits.append((k, v))
        return waits

    def _commit(self, reads, writes, tok):
        k, v = tok
        for r in reads:
            st = self.reg.setdefault(r, {'w': None, 'r': {}})
            if st['r'].get(k, 0) < v:
                st['r'][k] = v
        for r in writes:
            self.reg[r] = {'w': tok, 'r': {}}

    def op(self, eng, fn, reads=(), writes=()):
        waits = self._deps(eng, reads, writes)
        self.cnt[eng] += 1
        tok = (eng, self.cnt[eng])
        self.ops[eng].append((waits, fn, (eng, 1)))
        self._commit(reads, writes, tok)

    def dma(self, eng, fn, semname, reads=(), writes=()):
        waits = self._deps(eng, reads, writes)
        key = 'd_' + semname
        c = self.dmasem.get(key, 0) + 16
        self.dmasem[key] = c
        self.ops[eng].append((waits, fn, (key, 16)))
        self._commit(reads, writes, (key, c))

    def alias(self, new, olds):
        merged = {}
        for o in olds:
            st = self.reg.get(o)
            if not st:
                continue
            if st['w']:
                k, v = st['w']
                merged[k] = max(merged.get(k, 0), v)
            for k, v in st['r'].items():
                merged[k] = max(merged.get(k, 0), v)
        for n in new:
            st = self.reg.get(n)
            m2 = dict(merged)
            if st:
                if st['w']:
                    k, v = st['w']
                    m2[k] = max(m2.get(k, 0), v)
                for k, v in st['r'].items():
                    m2[k] = max(m2.get(k, 0), v)
            self.reg[n] = {'w': None, 'r': m2}

    def barrier_all(self):
        for e in ENGS:
            waits = []
            for k, v in list(self.cnt.items()):
                if v > 0 and self.seen[e].get(k, 0) < v and not (e == 'pe' and k == 'pe'):
                    self.seen[e][k] = v
                    waits.append((k, v))
            for k, v in self.dmasem.items():
                if self.seen[e].get(k, 0) < v:
                    self.seen[e][k] = v
                    waits.append((k, v))
            if waits:
                self.ops[e].append((waits, None, None))
        self.reg = {}

    def emit(self, block):
        engobj = {'pe': 'tensor', 'act': 'scalar', 'dve': 'vector', 'pool': 'gpsimd', 'sp': 'sync'}
        for e in ENGS:
            self.sem(e)
        for k in self.dmasem:
            self.sem(k)
        for e in ENGS:
            ops = self.ops[e]

            def body(eng, ops=ops):
                for waits, fn, inc in ops:
                    for k, v in waits:
                        eng.wait_ge(self.sem(k), v)
                    if fn is not None:
                        ins = fn(eng)
                        ins.then_inc(self.sem(inc[0]), inc[1])
            getattr(block, engobj[e])(body)


class Arena:
    def __init__(self, nc, lo, hi):
        self.nc, self.lo, self.hi, self.cur = nc, lo, hi, lo
        self.n = 0

    def alloc(self, name, shape, dt):
        size = int(np.prod(shape[1:])) * (4 if dt == F32 else 2)
        off = (self.cur + 63) // 64 * 64
        assert off + size <= self.hi, f"SBUF arena overflow at {name}: need {off + size - self.hi} more bytes"
        self.cur = off + size
        self.n += 1
        return self.nc.alloc_sbuf_tensor_at(f"{name}_{self.n}_{off}", list(shape), dt, offset=off).ap()

    def sub(self):
        return Arena(self.nc, self.cur, self.hi)


class K:
    def __init__(self, P):
        self.P = P

    def tt(self, eng, out, a, b, op, r, w):
        self.P.op(eng, lambda e: e.tensor_tensor(out=out, in0=a, in1=b, op=op), r, w)

    def ts(self, eng, out, a, s1, op0, r, w, s2=None, op1=None):
        if op1 is None:
            self.P.op(eng, lambda e: e.tensor_scalar(out=out, in0=a, scalar1=s1, scalar2=None, op0=op0), r, w)
        else:
            self.P.op(eng, lambda e: e.tensor_scalar(out=out, in0=a, scalar1=s1, scalar2=s2, op0=op0, op1=op1), r, w)

    def stt(self, eng, out, a, scalar, b, op0, op1, r, w):
        self.P.op(eng, lambda e: e.scalar_tensor_tensor(out=out, in0=a, scalar=scalar, in1=b, op0=op0, op1=op1), r, w)

    def act(self, out, in_, func, r, w, scale=1.0, bias=None, accum=None):
        def f(e):
            kw = {}
            if bias is not None:
                kw['bias'] = bias
            if accum is not None:
                kw['accum_out'] = accum
            return e.activation(out=out, in_=in_, func=func, scale=scale, **kw)
        self.P.op('act', f, r, w)

    def cp(self, eng, out, in_, r, w):
        if eng == 'act':
            self.P.op('act', lambda e: e.activation(out=out, in_=in_, func=AF.Copy), r, w)
        else:
            self.P.op(eng, lambda e: e.tensor_copy(out=out, in_=in_), r, w)

    def mm(self, out, lhsT, rhs, start, stop, r, w):
        self.P.op('pe', lambda e: e.matmul(out, lhsT=lhsT, rhs=rhs, start=start, stop=stop), r, w)

    def dma(self, q, out, in_, sem, r, w, slow=False):
        if sem == 'c0':
            sem = 'c_' + w[0]
        if slow:
            self.P.dma(q, lambda e: e.dma_start(out=out, in_=in_, allow_slow_non_contiguous=True), sem, r, w)
        else:
            self.P.dma(q, lambda e: e.dma_start(out=out, in_=in_), sem, r, w)

    def memset(self, eng, out, val, w):
        self.P.op(eng, lambda e: e.memset(out, val), (), w)

    def recip(self, out, in_, r, w):
        self.P.op('dve', lambda e: e.reciprocal(out=out, in_=in_), r, w)

    def reduce(self, out, in_, r, w):
        self.P.op('dve', lambda e: e.tensor_reduce(out=out, in_=in_, axis=AX.X, op=ALU.add), r, w)

    def scan(self, out, d0, d1, r, w):
        self.P.op('dve', lambda e: e.tensor_tensor_scan(out=out, data0=d0, data1=d1, initial=0.0,
                                                       op0=ALU.mult, op1=ALU.add), r, w)


WPIECES = [
    ("w_in", 1024, 1536, "nm0"), ("glu_w", 512, 512, None), ("w_out", 1024, 1024, None),
    ("w_qkv", 1024, 1536, "nm1"), ("w_o", 1024, 1024, None),
    ("wg0", 1024, FF, "nf0"), ("wu0", 1024, FF, "nf0"), ("wg1", 1024, FF, "nf1"), ("wu1", 1024, FF, "nf1"),
]


def build(LS, LP, OWNP, debug=False):
    assert LS % 1024 == 0 and LP % 1024 == 0 and OWNP % 512 == 0
    nc = bass.Bass("TRN2", target_bir_lowering=False)
    stack = contextlib.ExitStack()
    P = Prog(nc, stack)
    k = K(P)

    def din(name, shape):
        return nc.dram_tensor(name, list(shape), F32, kind="ExternalInput").ap()

    def dscr(name, shape, dt=BF16):
        return nc.dram_tensor(name, list(shape), dt).ap()

    I = {}
    for nm, shp in [("xs", [LS, D]), ("xp", [LP, D]), ("norm_mix", [2, D]), ("norm_ffn", [2, D]),
                    ("w_in", [D, 1536]), ("glu_w", [512, 512]), ("glu_b", [512]), ("w_out", [D, D]),
                    ("w_qkv", [D, 1536]), ("w_o", [D, D]),
                    ("wg0", [D, FF]), ("wu0", [D, FF]), ("wd0", [FF, D]),
                    ("wg1", [D, FF]), ("wu1", [D, FF]), ("wd1", [FF, D]),
                    ("lam_re", [2, 32, 64]), ("lam_im", [2, 32, 64]), ("log_dt", [2, 32]),
                    ("b_re", [2, 32, 64, 16]), ("b_im", [2, 32, 64, 16]),
                    ("c_re", [2, 32, 16, 64]), ("c_im", [2, 32, 16, 64]), ("ssm_d", [512]),
                    ("gm_norm", [4, 128]), ("gm_w_s", [4, 128, 128]), ("gm_b_s", [4, 128]),
                    ("q_norm", [64]), ("k_norm", [64]), ("attn_sink", [16]),
                    ("biasg", [128, 16, 3, 128]), ("maskc", [128, 3, 128]),
                    ("ident", [128, 128]), ("maskf", [128, 128]), ("maskb", [128, 128]),
                    ("kexp", [64 * NSLOT]), ("iota32", [32]), ("sgn1", [128]), ("sgn2", [128])]:
        I[nm] = din(nm, shp)
    ys = nc.dram_tensor("ys", [LS, D], F32, kind="ExternalOutput").ap()
    yp = nc.dram_tensor("yp", [OWNP, D], F32, kind="ExternalOutput").ap()
    dbg = {}

    WS = {}
    for nm, Kd, N, g in WPIECES:
        WS[nm] = dscr("s_" + nm, [(N + 511) // 512, 128, Kd // 128, 512])
    WS["wd0"] = dscr("s_wd0", [8, 128, NJ, 128])
    WS["wd1"] = dscr("s_wd1", [8, 128, NJ, 128])
    ssmw = dscr("s_ssmw", [32, 128, 9, 128])
    Lmax = max(LS, LP)
    if debug:
        gTs = nc.dram_tensor("dbg_gT", [4, 128, Lmax], BF16, kind="ExternalOutput").ap()
        dbg_x2 = nc.dram_tensor("dbg_x2", [Lmax // 512, 128, 8, 512], F32, kind="ExternalOutput").ap()
        dbg_x3 = nc.dram_tensor("dbg_x3", [Lmax // 512, 128, 8, 512], F32, kind="ExternalOutput").ap()
        dbg_oT = nc.dram_tensor("dbg_oT", [Lmax // 512, 128, 8, 512], BF16, kind="ExternalOutput").ap()
        dbg_qT = nc.dram_tensor("dbg_qT", [128, 8, 512], BF16, kind="ExternalOutput").ap()
        dbg_kT = nc.dram_tensor("dbg_kT", [128, 2, 12 * 128], BF16, kind="ExternalOutput").ap()
        dbg_V = nc.dram_tensor("dbg_V", [128, 12, 4, 65], BF16, kind="ExternalOutput").ap()
        dbg_bias = nc.dram_tensor("dbg_bias", [128, 16, 3, 128], BF16, kind="ExternalOutput").ap()
        dbg_x1 = nc.dram_tensor("dbg_x1", [Lmax // 512, 128, 8, 512], F32, kind="ExternalOutput").ap()
    else:
        gTs = dscr("s_gT", [4, 128, Lmax])

    base = (nc.sbuf_base + 63) // 64 * 64
    top = nc.sbuf_top // 64 * 64
    A0 = Arena(nc, base, top)
    psum = nc.alloc_psum_tensor("ps", [128, 4096], F32).ap()

    def PS(b, n=512):
        return psum[:, b * 512:b * 512 + n]

    def PSR(b):
        return f"ps{b}"

    ident = A0.alloc("ident", [128, 128], F32)
    identb = A0.alloc("identb", [128, 128], BF16)
    ones_col = A0.alloc("ones_col", [128, 1], BF16)
    ones_row = A0.alloc("ones_row", [1, 128], F32)
    rho8 = A0.alloc("rho8", [128, 64], F32)
    phi = A0.alloc("phi", [128, 64], F32)
    f1 = A0.alloc("f1", [128, 64], F32)
    iota = A0.alloc("iota", [128, 32], F32)
    glub = A0.alloc("glub", [128, 4], F32)
    gmn = A0.alloc("gmn", [128, 512], F32)
    bsT = A0.alloc("bsT", [128, 4, 128], F32)
    wsT = A0.alloc("wsT", [128, 4, 128], BF16)
    esink = A0.alloc("esink", [128, 16], F32)
    gqk = A0.alloc("gqk", [128, 1], F32)
    sgn1 = A0.alloc("sgn1", [128, 1], F32)
    sgn2 = A0.alloc("sgn2", [128, 1], F32)
    epsc = A0.alloc("epsc", [128, 1], F32)
    ccol = A0.alloc("ccol", [128, 3], F32)

    k.dma('sp', ident, I["ident"], 'c0', [], ['ident'])
    k.cp('dve', identb, ident, ['ident'], ['identb'])
    k.memset('dve', ones_col, 1.0, ['ones_col'])
    k.memset('dve', ones_row, 1.0, ['ones_row'])
    k.memset('dve', epsc, EPS, ['epsc'])
    k.memset('dve', ccol[:, 0:1], MAGIC, ['ccol'])
    k.memset('dve', ccol[:, 1:2], -MAGIC, ['ccol'])
    k.memset('dve', ccol[:, 2:3], 0.25, ['ccol'])
    k.dma('sp', iota, I["iota32"].partition_broadcast(128), 'c0', [], ['iota'])
    k.dma('sp', sgn1, I["sgn1"].rearrange("(p o) -> p o", o=1), 'c0', [], ['sgn1'])
    k.dma('sp', sgn2, I["sgn2"].rearrange("(p o) -> p o", o=1), 'c0', [], ['sgn2'])
    k.dma('sp', glub, I["glu_b"].rearrange("(c p) -> p c", p=128), 'c0', [], ['glub'], slow=True)
    k.dma('sp', gmn, I["gm_norm"].rearrange("h d -> (h d)").partition_broadcast(128), 'c0', [], ['gmn'])
    k.dma('sp', bsT, I["gm_b_s"].rearrange("h i -> (h i)").partition_broadcast(128), 'c0', [], ['bsT'])
    k.dma('sp', esink, I["attn_sink"].partition_broadcast(128), 'c0', [], ['esink'])
    k.act(esink, esink, AF.Exp, ['esink'], ['esink'])

    gains = A0.alloc("gains", [128, 4, 8], F32)
    S1 = A0.sub()
    tq = S1.alloc("tq", [128, 2], F32)
    for half in range(2):
        k.dma('sp', tq[half * 64:(half + 1) * 64, 0:1], I["q_norm"].rearrange("(p o) -> p o", o=1), 'c0', [], ['tq'])
        k.dma('sp', tq[half * 64:(half + 1) * 64, 1:2], I["k_norm"].rearrange("(p o) -> p o", o=1), 'c0', [], ['tq'])
    k.stt('dve', gqk, tq[:, 0:1], 0.125, tq[:, 1:2], ALU.mult, ALU.mult, ['tq'], ['gqk'])
    wst = S1.alloc("wst", [128, 4, 128], F32)
    k.dma('sp', wst, I["gm_w_s"].rearrange("h i j -> i h j"), 'c0', [], ['wst'])
    for h in range(4):
        k.mm(PS(h, 128), wst[:, h, :], ident, True, True, ['wst', 'ident'], [PSR(h)])
        k.cp('dve', wsT[:, h, :], PS(h, 128), [PSR(h)], ['wsT'])

    for gi, (src, row) in enumerate([("norm_mix", 0), ("norm_mix", 1), ("norm_ffn", 0), ("norm_ffn", 1)]):
        k.dma('sp', gains[:, gi, :], I[src][row].rearrange("(c p) -> p c", p=128), 'c0', [], ['gains'], slow=True)
    gidx = {"nm0": 0, "nm1": 1, "nf0": 2, "nf1": 3}
    top_hi = top
    DEF_BYTES = 2 * (8 * 512 * 4) + 2 * (8 * 512 * 2) + 256
    DA = Arena(nc, (top_hi - DEF_BYTES) // 64 * 64, top_hi)
    stg = [DA.alloc(f"stg{i}", [128, 8, 512], F32) for i in range(2)]
    stb = [DA.alloc(f"stb{i}", [128, 8, 512], BF16) for i in range(2)]
    cnt = [0]
    cast_jobs = []

    def mk_piece_job(nm, Kd, N, g, pc, engs):
        def job():
            KC = Kd // 128
            src = I[nm].rearrange("(c p) n -> p c n", p=128)
            n0 = pc * 512
            nn = min(512, N - n0)
            s_ = cnt[0] % 2
            cnt[0] += 1
            k.dma('sp', stg[s_][:, 0:KC, 0:nn], src[:, :, n0:n0 + nn], f'stg{s_}', [], [f'stg{s_}'])
            for kc in range(KC):
                eng = engs[(cnt[0] + kc) % len(engs)]
                if g is None:
                    k.cp(eng, stb[s_][:, kc, 0:nn], stg[s_][:, kc, 0:nn], [f'stg{s_}'], [f'stb{s_}'])
                elif eng == 'act':
                    k.act(stb[s_][:, kc, 0:nn], stg[s_][:, kc, 0:nn], AF.Copy, [f'stg{s_}', 'gains'], [f'stb{s_}'],
                          scale=gains[:, gidx[g], kc:kc + 1])
                else:
                    k.ts(eng, stb[s_][:, kc, 0:nn], stg[s_][:, kc, 0:nn], gains[:, gidx[g], kc:kc + 1], ALU.mult,
                         [f'stg{s_}', 'gains'], [f'stb{s_}'])
            k.dma('pool', WS[nm][pc][:, :, 0:nn], stb[s_][:, 0:KC, 0:nn], f'stb{s_}', [f'stb{s_}'], ['WS_' + nm])
        return job

    def mk_wd_job(l, fc, jh, engs):
        def job():
            src = I[f"wd{l}"].rearrange("(j p) n -> p j n", p=128)
            s_ = cnt[0] % 2
            cnt[0] += 1
            stgv = stg[s_].rearrange("p a b -> p (a b)")[:, 0:11 * 128].rearrange("p (j n) -> p j n", n=128)
            stbv = stb[s_].rearrange("p a b -> p (a b)")[:, 0:11 * 128].rearrange("p (j n) -> p j n", n=128)
            k.dma('sp', stgv, src[:, jh * 11:(jh + 1) * 11, fc * 128:(fc + 1) * 128], f'stg{s_}', [], [f'stg{s_}'])
            k.cp(engs[cnt[0] % len(engs)], stbv, stgv, [f'stg{s_}'], [f'stb{s_}'])
            k.dma('pool', WS[f"wd{l}"][fc][:, jh * 11:(jh + 1) * 11, :], stbv, f'stb{s_}', [f'stb{s_}'], [f'WS_wd{l}'])
        return job

    for nm, Kd, N, g in WPIECES:
        for pc in range((N + 511) // 512):
            if nm == "w_in":
                mk_piece_job(nm, Kd, N, g, pc, ['dve', 'act'])()
            else:
                cast_jobs.append(mk_piece_job(nm, Kd, N, g, pc, ['act']))
    for l in range(2):
        for fc in range(8):
            for jh in range(2):
                cast_jobs.append(mk_wd_job(l, fc, jh, ['act']))

    def bc(ap, shape):
        return ap.to_broadcast(shape)

    kx = S1.alloc("kx", [128, 64, NSLOT], F32)
    k.dma('sp', kx, I["kexp"].partition_broadcast(128), 'c0', [], ['kx'])
    lr = S1.alloc("lr", [128, 64], F32)
    li = S1.alloc("li", [128, 64], F32)
    dtb = S1.alloc("dtb", [128, 64], F32)
    for half in range(2):
        k.dma('sp', lr[half * 64:(half + 1) * 64, :], I["lam_re"].rearrange("d g p -> p (d g)"), 'c0', [], ['lr'], slow=True)
        k.dma('sp', li[half * 64:(half + 1) * 64, :], I["lam_im"].rearrange("d g p -> p (d g)"), 'c0', [], ['li'], slow=True)
    k.dma('sp', dtb, I["log_dt"].rearrange("d g -> (d g)").partition_broadcast(128), 'c0', [], ['dtb'])
    k.act(dtb, dtb, AF.Exp, ['dtb'], ['dtb'])
    lrdt = S1.alloc("lrdt", [128, 64], F32)
    lidt = S1.alloc("lidt", [128, 64], F32)
    k.tt('dve', lrdt, lr, dtb, ALU.mult, ['lr', 'dtb'], ['lrdt'])
    k.stt('dve', lidt, li, 1.0 / TWO_PI, dtb, ALU.mult, ALU.mult, ['li', 'dtb'], ['lidt'])
    MAG = S1.alloc("MAG", [128, 64, NSLOT], F32)
    Wt = S1.alloc("Wt", [128, 64, NSLOT], F32)
    Rt = S1.alloc("Rt", [128, 64, NSLOT], F32)
    SINT = S1.alloc("SINT", [128, 64, NSLOT], F32)
    COST = S1.alloc("COST", [128, 64, NSLOT], F32)
    sh3 = [128, 64, NSLOT]
    k.tt('dve', MAG, kx, bc(lrdt.unsqueeze(2), sh3), ALU.mult, ['kx', 'lrdt'], ['MAG'])
    k.act(MAG, MAG, AF.Exp, ['MAG'], ['MAG'])
    k.tt('dve', Wt, kx, bc(lidt.unsqueeze(2), sh3), ALU.mult, ['kx', 'lidt'], ['Wt'])
    k.ts('dve', Rt, Wt, MAGIC, ALU.add, ['Wt'], ['Rt'], s2=MAGIC, op1=ALU.subtract)
    k.tt('dve', SINT, Wt, Rt, ALU.subtract, ['Wt', 'Rt'], ['SINT'])
    k.cp('dve', phi, SINT[:, :, 25], ['SINT'], ['phi'])
    k.ts('dve', f1, phi, 32.0, ALU.mult, ['phi'], ['f1'])
    f1r = S1.alloc("f1r", [128, 64], F32)
    k.ts('dve', f1r, f1, MAGIC, ALU.add, ['f1'], ['f1r'], s2=MAGIC, op1=ALU.subtract)
    k.tt('dve', f1, f1, f1r, ALU.subtract, ['f1', 'f1r'], ['f1'])
    k.act(SINT, SINT, AF.Sin, ['SINT'], ['SINT'], scale=TWO_PI)
    k.ts('dve', Wt, Wt, 0.25, ALU.add, ['Wt'], ['Wt'])
    k.ts('dve', Rt, Wt, MAGIC, ALU.add, ['Wt'], ['Rt'], s2=MAGIC, op1=ALU.subtract)
    k.tt('dve', COST, Wt, Rt, ALU.subtract, ['Wt', 'Rt'], ['COST'])
    k.act(COST, COST, AF.Sin, ['COST'], ['COST'], scale=TWO_PI)
    AR, AI = COST, SINT
    k.tt('dve', AR, MAG, COST, ALU.mult, ['MAG', 'COST'], ['COST'])
    k.tt('dve', AI, MAG, SINT, ALU.mult, ['MAG', 'SINT'], ['SINT'])
    k.cp('dve', rho8, MAG[:, :, 25], ['MAG'], ['rho8'])
    den = S1.alloc("den", [128, 64], F32)
    t1 = S1.alloc("t1", [128, 64], F32)
    t2 = S1.alloc("t2", [128, 64], F32)
    arm1 = S1.alloc("arm1", [128, 64], F32)
    zr = S1.alloc("zr", [128, 64], F32)
    zi = S1.alloc("zi", [128, 64], F32)
    k.tt('dve', den, lr, lr, ALU.mult, ['lr'], ['den'])
    k.tt('dve', t1, li, li, ALU.mult, ['li'], ['t1'])
    k.tt('dve', den, den, t1, ALU.add, ['den', 't1'], ['den'])
    k.recip(den, den, ['den'], ['den'])
    k.ts('dve', arm1, AR[:, :, 24], -1.0, ALU.add, ['COST'], ['arm1'])
    k.tt('dve', t1, arm1, lr, ALU.mult, ['arm1', 'lr'], ['t1'])
    k.tt('dve', t2, AI[:, :, 24], li, ALU.mult, ['SINT', 'li'], ['t2'])
    k.tt('dve', t1, t1, t2, ALU.add, ['t1', 't2'], ['t1'])
    k.tt('dve', zr, t1, den, ALU.mult, ['t1', 'den'], ['zr'])
    k.tt('dve', t1, AI[:, :, 24], lr, ALU.mult, ['SINT', 'lr'], ['t1'])
    k.tt('dve', t2, arm1, li, ALU.mult, ['arm1', 'li'], ['t2'])
    k.tt('dve', t1, t1, t2, ALU.subtract, ['t1', 't2'], ['t1'])
    k.tt('dve', zi, t1, den, ALU.mult, ['t1', 'den'], ['zi'])
    P0 = S1.alloc("P0", [128, 64, 16], F32)
    Q0 = S1.alloc("Q0", [128, 64, 16], F32)
    Pm = S1.alloc("Pm", [128, 64, 16], F32)
    Qm = S1.alloc("Qm", [128, 64, 16], F32)
    tb = S1.alloc("tb", [128, 64, 16], F32)
    bre = I["b_re"].rearrange("d g p h -> p (d g) h")
    bim = I["b_im"].rearrange("d g p h -> p (d g) h")
    k.dma('sp', P0[0:64], bre, 'c0', [], ['P0'])
    k.dma('sp', P0[64:128], bim, 'c0', [], ['P0'])
    k.dma('sp', Q0[0:64], bim, 'c0', [], ['Q0'])
    k.dma('sp', Q0[64:128], bre, 'c0', [], ['Q0'])
    sh = [128, 64, 16]
    zrb, zib = bc(zr.unsqueeze(2), sh), bc(zi.unsqueeze(2), sh)
    k.tt('dve', tb, Q0, zib, ALU.mult, ['Q0', 'zi'], ['tb'])
    k.tt('dve', Pm, P0, zrb, ALU.mult, ['P0', 'zr'], ['Pm'])
    k.stt('dve', Pm, tb, sgn1[:, 0:1], Pm, ALU.mult, ALU.add, ['tb', 'Pm', 'sgn1'], ['Pm'])
    k.tt('dve', tb, P0, zib, ALU.mult, ['P0', 'zi'], ['tb'])
    k.tt('dve', Qm, Q0, zrb, ALU.mult, ['Q0', 'zr'], ['Qm'])
    k.stt('dve', Qm, tb, sgn2[:, 0:1], Qm, ALU.mult, ALU.add, ['tb', 'Qm', 'sgn2'], ['Qm'])
    CT1 = S1.alloc("CT1", [128, 64, 16], F32)
    CT2 = S1.alloc("CT2", [128, 64, 16], F32)
    Cst = S1.alloc("Cst", [128, 8, 2, 64], F32)
    cre = I["c_re"].rearrange("d (gq g8) h p -> (g8 h) (d gq) p", g8=8)
    cim = I["c_im"].rearrange("d (gq g8) h p -> (g8 h) (d gq) p", g8=8)
    for which, CT in enumerate([CT1, CT2]):
        a_, b_ = (cre, cim) if which == 0 else (cim, cre)
        k.dma('sp', Cst[:, :, 0, :], a_, 'c0', [], ['Cst'])
        k.dma('sp', Cst[:, :, 1, :], b_, 'c0', [], ['Cst'])
        for blk in range(8):
            k.mm(PS(blk, 128), Cst[:, blk, :, :].rearrange("p a b -> p (a b)"), ident, True, True,
                 ['Cst', 'ident'], [PSR(blk)])
            k.cp('dve', CT[:, blk * 8:(blk + 1) * 8, :].rearrange("p a b -> p (a b)"), PS(blk, 128),
                 [PSR(blk)], ['CT%d' % which])
    dcol = S1.alloc("dcol", [128, 32], F32)
    for tau in range(8):
        k.dma('sp', dcol[tau * 16:(tau + 1) * 16, :], I["ssm_d"].rearrange("(g h) -> h g", h=16), 'c0', [], ['dcol'], slow=True)
    mkf = S1.alloc("mkf", [128, 128], F32)
    mkb = S1.alloc("mkb", [128, 128], F32)
    k.dma('sp', mkf, I["maskf"], 'c0', [], ['mkf'])
    k.dma('sp', mkb, I["maskb"], 'c0', [], ['mkb'])

    GB = 8
    sh4 = [128, GB, 8, 16]
    TW = {}
    for nm in ["WBt", "WBs", "Vt", "WC", "WC2"]:
        for d in range(2):
            TW[nm, d] = S1.alloc(f"{nm}{d}", sh4, F32)
    ta = S1.alloc("ta", sh4, F32)
    tb4 = S1.alloc("tb4", sh4, F32)
    wtile = [S1.alloc(f"wtile{i}", [128, 9, 128], BF16) for i in range(2)]
    kt1 = S1.alloc("kt1", [128, 128], F32)
    kt2 = S1.alloc("kt2", [128, 128], F32)
    for gb in range(32 // GB):
        for d in range(2):
            gs = slice(d * 32 + gb * GB, d * 32 + (gb + 1) * GB)

            def ER(s0):
                return bc(AR[:, gs, s0:s0 + 8].unsqueeze(3), sh4), bc(AI[:, gs, s0:s0 + 8].unsqueeze(3), sh4)

            def HB(t):
                return bc(t[:, gs, :].unsqueeze(2), sh4)
            rr = ['COST', 'SINT', 'Pm', 'Qm', 'CT0', 'CT1', 'sgn1', 'sgn2']
            er, ei = ER(0)
            o = TW["WBt", d]
            k.tt('dve', ta, er, HB(Pm), ALU.mult, rr, ['ta'])
            k.tt('dve', tb4, ei, HB(Qm), ALU.mult, rr, ['tb4'])
            k.stt('dve', o, tb4, sgn1[:, 0:1], ta, ALU.mult, ALU.add, ['ta', 'tb4'] + rr, ['WBt%d' % d])
            o = TW["WBs", d]
            k.tt('dve', ta, er, HB(Qm), ALU.mult, rr, ['ta'])
            k.tt('dve', tb4, ei, HB(Pm), ALU.mult, rr, ['tb4'])
            if d == 0:
                k.stt('dve', o, ta, sgn2[:, 0:1], tb4, ALU.mult, ALU.add, ['ta', 'tb4'] + rr, ['WBs%d' % d])
            else:
                k.stt('dve', o, ta, sgn1[:, 0:1], tb4, ALU.mult, ALU.subtract, ['ta', 'tb4'] + rr, ['WBs%d' % d])
            er, ei = ER(8)
            o = TW["Vt", d]
            k.tt('dve', ta, er, HB(Pm), ALU.mult, rr, ['ta'])
            k.tt('dve', tb4, ei, HB(Qm), ALU.mult, rr, ['tb4'])
            k.stt('dve', o, tb4, sgn1[:, 0:1], ta, ALU.mult, ALU.add, ['ta', 'tb4'] + rr, ['Vt%d' % d])
            er, ei = ER(16)
            o = TW["WC", d]
            k.tt('dve', ta, er, HB(CT1), ALU.mult, rr, ['ta'])
            k.tt('dve', tb4, ei, HB(CT2), ALU.mult, rr, ['tb4'])
            k.stt('dve', o, ta, sgn2[:, 0:1], tb4, ALU.mult, ALU.subtract, ['ta', 'tb4'] + rr, ['WC%d' % d])
            o = TW["WC2", d]
            k.tt('dve', ta, er, HB(CT2), ALU.mult, rr, ['ta'])
            k.tt('dve', tb4, ei, HB(CT1), ALU.mult, rr, ['tb4'])
            if d == 0:
                k.stt('dve', o, tb4, sgn1[:, 0:1], ta, ALU.mult, ALU.subtract, ['ta', 'tb4'] + rr, ['WC2%d' % d])
            else:
                k.stt('dve', o, tb4, sgn2[:, 0:1], ta, ALU.mult, ALU.add, ['ta', 'tb4'] + rr, ['WC2%d' % d])
        for gi in range(GB):
            g = gb * GB + gi
            wt = wtile[g % 2]
            wr = f'wtile{g % 2}'

            def V2(t):
                return t[:, gi, :, :].rearrange("p a b -> p (a b)")
            for d in range(2):
                k.mm(PS(0 + d, 128), V2(TW["WBt", d]), ident, True, True, ['WBt%d' % d, 'ident'], [PSR(0 + d)])
                k.cp('act', wt[:, d * 4 + 0, :], PS(0 + d, 128), [PSR(0 + d)], [wr])
                k.mm(PS(2 + d, 128), V2(TW["WBs", d]), ident, True, True, ['WBs%d' % d, 'ident'], [PSR(2 + d)])
                k.cp('act', wt[:, d * 4 + 1, :], PS(2 + d, 128), [PSR(2 + d)], [wr])
                k.cp('dve', wt[:, d * 4 + 2, :], V2(TW["WC", d]), ['WC%d' % d], [wr])
                k.cp('dve', wt[:, d * 4 + 3, :], V2(TW["WC2", d]), ['WC2%d' % d], [wr])
                k.mm(PS(4 + d, 128), V2(TW["Vt", d]), V2(TW["WC", d]), True, True, ['Vt%d' % d, 'WC%d' % d], [PSR(4 + d)])
            k.tt('dve', kt1, PS(4, 128), mkf, ALU.mult, [PSR(4), 'mkf'], ['kt1'])
            k.tt('dve', kt2, PS(5, 128), mkb, ALU.mult, [PSR(5), 'mkb'], ['kt2'])
            k.tt('dve', kt1, kt1, kt2, ALU.add, ['kt1', 'kt2'], ['kt1'])
            k.stt('dve', wt[:, 8, :], ident, dcol[:, g:g + 1], kt1, ALU.mult, ALU.add, ['kt1', 'ident', 'dcol'], [wr])
            k.dma('pool', ssmw[g], wt, wr, [wr], ['ssmw'])
    assert S1.cur <= DA.lo, (S1.cur, DA.lo)
    P.barrier_all()

    uid = [0]

    def XR(xr):
        return [f'{xr}.{c}' for c in range(8)]

    def ssq_step(xT, xr, sq, kc, n=512, bank_a=0):
        s_ = kc % 2
        if kc % 4 == 3:
            k.tt('dve', sq[s_][:, 0:n], xT[:, kc, 0:n], xT[:, kc, 0:n], ALU.mult, [f'{xr}.{kc}'], [f'sq{s_}'])
        else:
            k.act(sq[s_][:, 0:n], xT[:, kc, 0:n], AF.Square, [f'{xr}.{kc}'], [f'sq{s_}'])
        k.mm(PS(bank_a, n)[0:1, :], ones_col, sq[s_][:, 0:n], kc == 0, kc == 7, [f'sq{s_}', 'ones_col'], [PSR(bank_a)])

    def norm_to_hT(xT, xr, hT, hr, sq, rbc, rrow, n=512, bank_a=0, bank_b=1, fused=False):
        if not fused:
            for kc in range(8):
                ssq_step(xT, xr, sq, kc, n, bank_a)
        k.act(rrow[:, 0:n], PS(bank_a, n)[0:1, :], AF.Sqrt, [PSR(bank_a), 'epsc'], ['rrow'], scale=1.0 / D, bias=epsc[0:1, 0:1])
        k.recip(rrow[:, 0:n], rrow[:, 0:n], ['rrow'], ['rrow'])
        k.mm(PS(bank_b, n), ones_row, rrow[:, 0:n], True, True, ['rrow', 'ones_row'], [PSR(bank_b)])
        for kc in range(8):
            k.tt('dve', hT[:, kc, 0:n], xT[:, kc, 0:n], PS(bank_b, n), ALU.mult, [f'{xr}.{kc}', PSR(bank_b)], [f'{hr}{kc}'])

    def load_xT(xsrc, t0, xin, xT, xr, nblk=4):
        for b in range(nblk):
            s = b % len(xin)
            k.dma('sp', xin[s], xsrc[t0 + b * 128:t0 + (b + 1) * 128, :], f'xin{s}', [], [f'xin{s}'])
            for kc in range(8):
                k.mm(PS(kc)[:, b * 128:(b + 1) * 128], xin[s][:, kc * 128:(kc + 1) * 128], ident, True, True,
                     [f'xin{s}', 'ident'], [PSR(kc)])
        for kc in range(8):
            k.cp('act' if kc % 2 == 0 else 'dve', xT[:, kc, 0:nblk * 128], PS(kc, nblk * 128), [PSR(kc)], [f'{xr}.{kc}'])

    def ssm_phases(xsrc, L, LC, jobs=None, hi=None):
        nch = L // 8
        PA = Arena(nc, A0.cur, hi if hi is not None else A0.hi)
        U = PA.alloc("U", [128, 32, nch], BF16)
        AA = PA.sub()
        xin = [AA.alloc(f"xin{i}", [128, D], F32) for i in range(2)]
        xTs = [AA.alloc(f"xT{i}", [128, 8, 512], F32) for i in range(2)]
        sq = [AA.alloc(f"sq{i}", [128, 512], BF16) for i in range(2)]
        rbc = AA.alloc("rbc", [128, 512], F32)
        rrow = AA.alloc("rrow", [1, 512], F32)
        hT1k = AA.alloc("hT1k", [128, 8, 1024], BF16)
        wa = AA.alloc("wa", [128, 8, 512], BF16)
        Zc = AA.alloc("Zc", [128, 32, 8, 16], BF16)
        k.dma('sp', wa, WS["w_in"][0], 'wa', ['WS_w_in'], ['wa'])
        for t1k in range(L // 1024):
            for half in range(2):
                t0 = t1k * 1024 + half * 512
                load_xT(xsrc, t0, xin, xTs[half], f'xTa{half}')
            for half in range(2):
                norm_to_hT(xTs[half], f'xTa{half}', hT1k[:, :, half * 512:(half + 1) * 512], 'hT1k', sq, rbc, rrow)
            for tau in range(8):
                for kc in range(8):
                    k.mm(PS(tau), hT1k[:, kc, tau::8], wa[:, kc, :], kc == 0, kc == 7, [f'hT1k{kc}', 'wa'], [PSR(tau)])
                k.cp('act' if tau % 2 == 0 else 'dve', Zc[:, :, tau, :], PS(tau).rearrange("p (g h) -> p g h", h=16), [PSR(tau)], ['Zc'])
            for g in range(32):
                b = g // 4
                k.mm(PS(b)[:, (g % 4) * 128:(g % 4 + 1) * 128], Zc[:, g, :, :].rearrange("p a b -> p (a b)"), identb, True, True,
                     ['Zc', 'identb'], [PSR(b)])
                if g % 4 == 3:
                    k.cp('act' if b % 2 == 0 else 'dve', U[:, b * 4:(b + 1) * 4, t1k * 128:(t1k + 1) * 128],
                         PS(b).rearrange("p (g c) -> p g c", c=128), [PSR(b)], [f'U{b * 4 + q_}' for q_ in range(4)])
        P.barrier_all()
        BA = PA.sub()
        wset = [BA.alloc(f"wset{i}", [128, 9, 128], BF16) for i in range(2)]
        n1 = nch // 32
        tab = [BA.alloc(f"tab{i}", [128, 2, nch], F32) for i in range(2)]
        tw = BA.alloc("tw", [128, 2, nch], F32)
        tr = BA.alloc("tr", [128, 2, nch], F32)
        xt = [BA.alloc(f"xt{i}", [128, nch], F32) for i in range(2)]
        xt2 = BA.alloc("xt2", [128, nch], F32)
        St = BA.alloc("St", [128, nch], F32)
        P12 = [BA.alloc(f"P12{i}", [128, 2, nch], BF16) for i in range(2)]
        nb = (nch + 511) // 512
        units = [(g, d) for g in range(32) for d in range(2)]

        def wsel(g):
            return wset[g % 2], f'wset{g % 2}'

        def tabgen(g, d):
            gd = d * 32 + g
            w0 = tw[:, 0, :].rearrange("p (a b) -> p a b", b=32)
            k.ts('dve', w0, iota[:, :].unsqueeze(1).to_broadcast([128, n1, 32]), phi[:, gd:gd + 1], ALU.mult,
                 ['iota', 'phi'], ['tw'])
            k.stt('dve', w0, iota[:, 0:n1].unsqueeze(2).to_broadcast([128, n1, 32]), f1[:, gd:gd + 1], w0,
                  ALU.mult, ALU.add, ['iota', 'f1', 'tw'], ['tw'])
            k.ts('dve', tw[:, 1, :], tw[:, 0, :], 0.25, ALU.add, ['tw'], ['tw'])
            k.ts('dve', tr, tw, MAGIC, ALU.add, ['tw'], ['tr'], s2=MAGIC, op1=ALU.subtract)
            k.tt('dve', tr, tw, tr, ALU.subtract, ['tw', 'tr'], ['tr'])
            k.act(tab[d], tr, AF.Sin, ['tr'], [f'tab{d}'], scale=TWO_PI)

        def xmm(g, d):
            ws_, wsr = wsel(g)
            if d == 0:
                k.dma('sp', ws_, ssmw[g], wsr, ['ssmw'], [wsr])
            for bi in range(nb):
                cs = slice(bi * 512, min(nch, (bi + 1) * 512))
                n = cs.stop - cs.start
                k.mm(PS(bi, n), ws_[:, d * 4 + 0, :], U[:, g, cs], True, True, [wsr, f'U{g}'], [PSR(bi)])
                k.mm(PS(nb + bi, n), ws_[:, d * 4 + 1, :], U[:, g, cs], True, True, [wsr, f'U{g}'], [PSR(nb + bi)])

        def rot(g, d):
            tb_, tbr, x_, xr_ = tab[d], f'tab{d}', xt[d], f'xt{d}'
            for bi in range(nb):
                cs = slice(bi * 512, min(nch, (bi + 1) * 512))
                n = cs.stop - cs.start
                k.tt('dve', x_[:, cs], PS(bi, n), tb_[:, 1, cs], ALU.mult, [PSR(bi), tbr], [xr_])
                k.tt('dve', xt2[:, cs], PS(nb + bi, n), tb_[:, 0, cs], ALU.mult, [PSR(nb + bi), tbr], ['xt2'])
            k.tt('dve', x_, x_, xt2, ALU.add, [xr_, 'xt2'], [xr_])

        def scn(g, d):
            gd = d * 32 + g
            tb_, tbr, x_, xr_ = tab[d], f'tab{d}', xt[d], f'xt{d}'
            rb = rho8[:, gd:gd + 1].to_broadcast([128, nch])
            if d == 0:
                k.scan(St, rb, x_, [xr_, 'rho8'], ['St'])
            else:
                k.scan(St[:, ::-1], rb, x_[:, ::-1], [xr_, 'rho8'], ['St'])
            pp = P12[d]
            k.tt('dve', pp[:, 0, :], St, tb_[:, 1, :], ALU.mult, ['St', tbr], [f'P12{d}'])
            k.tt('dve', pp[:, 1, :], St, tb_[:, 0, :], ALU.mult, ['St', tbr], [f'P12{d}'])

        def ymm(g):
            ws_, wsr = wsel(g)
            for bi in range(nb):
                c0 = bi * 512
                c1 = min(nch, c0 + 512)
                yb = 4 + bi
                rdeps = [wsr, f'U{g}', 'P120', 'P121']
                k.mm(PS(yb, c1 - c0), ws_[:, 8, :], U[:, g, c0:c1], True, False, rdeps, [PSR(yb)])
                lo = max(c0, 1)
                for wi in (2, 3):
                    k.mm(psum[:, yb * 512 + (lo - c0):yb * 512 + (c1 - c0)], ws_[:, wi, :],
                         P12[0][:, wi - 2, lo - 1:c1 - 1], False, False, rdeps, [PSR(yb)])
                hi = min(c1, nch - 1)
                for wi in (6, 7):
                    k.mm(psum[:, yb * 512:yb * 512 + (hi - c0)], ws_[:, wi, :],
                         P12[1][:, wi - 6, c0 + 1:hi + 1], False, wi == 7, rdeps, [PSR(yb)])
                k.act(U[:, g, c0:c1], PS(yb, c1 - c0), AF.Gelu_apprx_tanh, [PSR(yb)], [f'U{g}'])

        tabgen(*units[0])
        for ui, (g, d) in enumerate(units):
            xmm(g, d)
            if ui + 1 < len(units):
                tabgen(*units[ui + 1])
            rot(g, d)
            scn(g, d)
            if d == 1:
                ymm(g)
            if jobs:
                jobs.pop(0)()
        while jobs:
            jobs.pop(0)()
        P.barrier_all()
        CA = PA.sub()
        Gc = CA.alloc("Gc", [128, 8, 512], BF16)
        gTt = [CA.alloc(f"gTt{i}", [128, 4, 1024], BF16) for i in range(2)]
        for t1k in range(LC // 1024):
            gt = gTt[t1k % 2]
            gtr = f'gTt{t1k % 2}'
            for g in range(32):
                b = g // 4
                k.mm(PS(b)[:, (g % 4) * 128:(g % 4 + 1) * 128], U[:, g, t1k * 128:(t1k + 1) * 128], identb, True, True,
                     [f'U{g}', 'identb'], [PSR(b)])
                if g % 4 == 3:
                    src = PS(b).rearrange("p (g t h) -> p g t h", g=4, t=8)
                    dst = Gc[:, :, b * 64:(b + 1) * 64].rearrange("p t (g h) -> p g t h", g=4)
                    k.cp('act' if b % 2 == 0 else 'dve', dst, src, [PSR(b)], ['Gc'])
            for j in range(4):
                for th in range(2):
                    b = j * 2 + th
                    for ti in range(4):
                        tau = th * 4 + ti
                        k.mm(PS(b)[:, ti * 128:(ti + 1) * 128], Gc[:, tau, j * 128:(j + 1) * 128], identb, True, True,
                             ['Gc', 'identb'], [PSR(b)])
                    src = PS(b).rearrange("p (t c) -> p t c", t=4)
                    dst = gt[:, j, :].rearrange("p (c t) -> p t c", t=8)[:, th * 4:(th + 1) * 4, :]
                    k.cp('act' if b % 2 == 0 else 'dve', dst, src, [PSR(b)], [gtr])
            k.dma('pool', gTs[:, :, t1k * 1024:(t1k + 1) * 1024].rearrange("j p t -> p j t"), gt, gtr, [gtr], ['gTs'])
        P.barrier_all()

    def main_seq(xsrc, ydst, L, OWN):
        MA = A0.sub()
        R = 5
        ring = [MA.alloc(f"ring{i}", [128, 4096], BF16) for i in range(R)]
        biasT = MA.alloc("biasT", [128, 16, 3, 128], BF16)
        xin = [MA.alloc(f"xin{i}", [128, D], F32) for i in range(4)]
        xouts = [xin[2], xin[3]]
        xb = [MA.alloc(f"xb{i}", [128, 8, 512], F32) for i in range(2)]
        hT = MA.alloc("hT", [128, 8, 512], BF16)
        sq = [MA.alloc(f"sq{i}", [128, 512], BF16) for i in range(2)]
        mst = MA.alloc("mst", [128, 3, 128], F32)
        rbc = None
        rrow = MA.alloc("rrow", [1, 512], F32)
        gT = MA.alloc("gT", [128, 4, 512], BF16)
        sgate = [MA.alloc(f"sgate{i}", [128, 512], F32) for i in range(2)]
        qT = [MA.alloc(f"qT{i}", [128, 8, 512], BF16) for i in range(2)]
        NR = 12
        kT = MA.alloc("kT", [128, 2, NR * 128], BF16)
        Vr = MA.alloc("Vr", [128, NR, 4, 65], BF16)
        stat = MA.alloc("stat", [128, 64], F32)
        GX = MA.sub()
        ubT = GX.alloc("ubT", [128, 4, 512], BF16)
        vgs = [GX.alloc(f"vg{i}", [128, 512], F32) for i in range(2)]
        vsqs = [GX.alloc(f"vsq{i}", [128, 512], F32) for i in range(2)]
        vns = [GX.alloc(f"vn{i}", [128, 512], BF16) for i in range(2)]
        ybT = GX.alloc("ybT", [128, 4, 512], BF16)
        sg = GX.alloc("sg", [128, 512], F32)
        yaT = GX.alloc("yaT", [128, 4, 512], BF16)
        GXq = MA.sub()
        qk = GXq.alloc("qk", [128, 1280], F32)
        qsq = GXq.alloc("qsq", [128, 1280], F32)
        qn = GXq.alloc("qn", [128, 1280], BF16)
        MA.cur = max(GX.cur, GXq.cur)
        GY = MA.sub()
        actT = GY.alloc("actT", [128, NJ, 512], BF16)
        GYa = MA.sub()
        PT = [GYa.alloc(f"PT{i}", [128, 3, 512], BF16) for i in range(2)]
        on = GYa.alloc("on", [128, 1024], BF16)
        oT = GYa.alloc("oT", [128, 8, 512], BF16)
        MA.cur = max(GY.cur, GYa.cur)
        L0N = ['ubT', 'vg0', 'vg1', 'vsq0', 'vsq1', 'vn0', 'vn1', 'ybT', 'sg', 'yaT']
        QKN = ['qk', 'qsq', 'qn']
        ATN = ['PT0', 'PT1', 'on', 'oT']
        NBseq = L // 128

        mstage = mst
        k.dma('sp', mstage, I["maskc"], 'mstd', [], ['mst'])
        bst = xb[0].rearrange("p a b -> p (a b)")[:, 0:2048].rearrange("p (h q) -> p h q", h=16)
        for r in range(3):
            k.dma('sp', bst, I["biasg"][:, :, r, :], 'bst', [], XR('xb0'))
            k.tt('dve', biasT[:, :, r, :], bst, mstage[:, r, :].unsqueeze(1).to_broadcast([128, 16, 128]), ALU.add,
                 XR('xb0') + ['mst'], ['biasT'])
        k.memset('dve', Vr[:, :, :, 64:65], 1.0, ['Vr'])

        ridx = [0]

        def wload(src, dreg, view):
            s = ridx[0] % R
            ridx[0] += 1
            tot = int(np.prod(src.shape[1:]))
            dst = ring[s][:, 0:tot]
            if view is not None:
                dst = dst.rearrange(view[0], **view[1])
            k.dma('sp', dst, src, f'ring{s}', [dreg], [f'ring{s}'])
            return dst, f'ring{s}'

        def wpiece(nm, pc, KC=8, ncol=512):
            src = WS[nm][pc]
            if ncol != 512:
                src = src[:, :, 0:ncol]
            return wload(src, 'WS_' + nm, ("p (c n) -> p c n", dict(n=ncol)))

        def ffn(l, xT, xr, n):
            P.alias(['actT'], ATN)
            norm_to_hT(xT, xr, hT, 'hT', sq, rbc, rrow, n=n, fused=True)
            jcount = 0
            for pc in range(6):
                ncol = 512 if pc < 5 else 256
                wg_, wgr = wpiece(f"wg{l}", pc, ncol=ncol)
                wu_, wur = wpiece(f"wu{l}", pc, ncol=ncol)
                for jj in range(ncol // 128):
                    j = pc * 4 + jj
                    ba, bb = (2, 3) if jcount % 2 == 0 else (4, 5)
                    s = jcount % 2
                    jcount += 1
                    for kc in range(8):
                        k.mm(PS(ba, n), wg_[:, kc, jj * 128:(jj + 1) * 128], hT[:, kc, 0:n], kc == 0, kc == 7, [wgr, f'hT{kc}'], [PSR(ba)])
                    for kc in range(8):
                        k.mm(PS(bb, n), wu_[:, kc, jj * 128:(jj + 1) * 128], hT[:, kc, 0:n], kc == 0, kc == 7, [wur, f'hT{kc}'], [PSR(bb)])
                    k.act(sgate[s][:, 0:n], PS(ba, n), AF.Silu, [PSR(ba)], [f'sgate{s}'])
                    k.tt('dve', actT[:, j, 0:n], sgate[s][:, 0:n], PS(bb, n), ALU.mult, [f'sgate{s}', PSR(bb)], ['actT'])
            for fc in range(8):
                wd_, wdr = wload(WS[f"wd{l}"][fc], f'WS_wd{l}', ("p (j n) -> p j n", dict(n=128)))
                bo = 6 + fc % 2
                for j in range(NJ):
                    k.mm(PS(bo, n), wd_[:, j, :], actT[:, j, 0:n], j == 0, j == NJ - 1, [wdr, 'actT'], [PSR(bo)])
                k.tt('dve', xT[:, fc, 0:n], xT[:, fc, 0:n], PS(bo, n), ALU.add, [f'{xr}.{fc}', PSR(bo)], [f'{xr}.{fc}'])
                if l == 0 and fc >= 1:
                    ssq_step(xT, xr, sq, fc - 1, n)
            if l == 0:
                ssq_step(xT, xr, sq, 7, n)

        def L0(i, xT, xr, n):
            t0 = i * 512
            nblk = n // 128
            P.alias(L0N, QKN)
            load_xT(xsrc, t0, xin, xT, xr, nblk=nblk)
            k.dma('sp', gT[:, :, 0:n], gTs[:, :, t0:t0 + n].rearrange("j p t -> p j t"), 'gT', ['gTs'], ['gT'])
            norm_to_hT(xT, xr, hT, 'hT', sq, rbc, rrow, n=n)
            w1, w1r = wpiece("w_in", 1)
            for jj in range(4):
                bk = 2 + jj % 2
                for kc in range(8):
                    k.mm(PS(bk, n), w1[:, kc, jj * 128:(jj + 1) * 128], hT[:, kc, 0:n], kc == 0, kc == 7, [w1r, f'hT{kc}'], [PSR(bk)])
                k.act(ubT[:, jj, 0:n], PS(bk, n), AF.Gelu_apprx_tanh, [PSR(bk)], ['ubT'])
            w2, w2r = wpiece("w_in", 2)
            for b in range(nblk):
                bk = 4 + b
                ts_ = slice(b * 128, (b + 1) * 128)
                for kc in range(8):
                    k.mm(PS(bk), hT[:, kc, ts_], w2[:, kc, :], kc == 0, kc == 7, [w2r, f'hT{kc}'], [PSR(bk)])

            def chainA(b):
                bk = 4 + b
                q_ = b % 2
                st_ = stat[:, 48 + 4 * b:52 + 4 * b]
                k.act(vgs[q_], PS(bk), AF.Gelu_apprx_tanh, [PSR(bk)], [f'vg{q_}'])
                k.act(vsqs[q_], vgs[q_], AF.Square, [f'vg{q_}'], [f'vsq{q_}'])
                k.reduce(st_, vsqs[q_].rearrange("p (h d) -> p h d", h=4), [f'vsq{q_}'], [f'vst{b}'])

            def chainB(b):
                q_ = b % 2
                st_ = stat[:, 48 + 4 * b:52 + 4 * b]
                k.act(st_, st_, AF.Sqrt, [f'vst{b}', 'epsc'], [f'vst{b}'], scale=1.0 / 128, bias=epsc[:, 0:1])
                k.recip(st_, st_, [f'vst{b}'], [f'vst{b}'])
                for h in range(4):
                    hs = slice(h * 128, (h + 1) * 128)
                    k.stt('dve', vns[q_][:, hs], vgs[q_][:, hs], st_[:, h:h + 1], gmn[:, hs], ALU.mult, ALU.mult,
                          [f'vg{q_}', f'vst{b}', 'gmn'], [f'vn{q_}'])

            def mixed(b):
                bm = 4 + b
                q_ = b % 2
                ts_ = slice(b * 128, (b + 1) * 128)
                for h in range(4):
                    hs = slice(h * 128, (h + 1) * 128)
                    k.mm(PS(bm)[:, hs], vns[q_][:, hs], wsT[:, h, :], True, True, [f'vn{q_}', 'wsT'], [PSR(bm)])
                v3 = vsqs[q_].rearrange("p (h d) -> p h d", h=4)
                k.tt('dve', v3, PS(bm).rearrange("p (h d) -> p h d", h=4), bsT, ALU.add, [PSR(bm), 'bsT'], [f'vsq{q_}'])
                k.tt('dve', ybT[:, :, ts_], v3, ubT[:, :, ts_], ALU.mult, [f'vsq{q_}', 'ubT'], ['ybT'])

            for b0 in range(0, nblk, 2):
                bs = [b for b in (b0, b0 + 1) if b < nblk]
                for b in bs:
                    chainA(b)
                for b in bs:
                    chainB(b)
                for b in bs:
                    mixed(b)
            wl, wlr = wload(WS["glu_w"][0], 'WS_glu_w', ("p (c n) -> p c n", dict(n=512)))
            for fo in range(4):
                bk = 2 + fo % 2
                for kc in range(4):
                    k.mm(PS(bk, n), wl[:, kc, fo * 128:(fo + 1) * 128], gT[:, kc, 0:n], kc == 0, kc == 3, [wlr, 'gT'], [PSR(bk)])
                k.act(sg[:, 0:n], PS(bk, n), AF.Sigmoid, [PSR(bk), 'glub'], ['sg'], bias=glub[:, fo:fo + 1])
                k.tt('dve', yaT[:, fo, 0:n], gT[:, fo, 0:n], sg[:, 0:n], ALU.mult, ['gT', 'sg'], ['yaT'])
            for pc in range(2):
                wo_, wor = wpiece("w_out", pc)
                for fci in range(4):
                    fc = pc * 4 + fci
                    bo = 6 + fc % 2
                    for kc in range(8):
                        rhs = yaT[:, kc, 0:n] if kc < 4 else ybT[:, kc - 4, 0:n]
                        k.mm(PS(bo, n), wo_[:, kc, fci * 128:(fci + 1) * 128], rhs, kc == 0, kc == 7, [wor, 'yaT', 'ybT'], [PSR(bo)])
                    k.tt('dve', xT[:, fc, 0:n], xT[:, fc, 0:n], PS(bo, n), ALU.add, [f'{xr}.{fc}', PSR(bo)], [f'{xr}.{fc}'])
                    if fc >= 1:
                        ssq_step(xT, xr, sq, fc - 1, n)
            ssq_step(xT, xr, sq, 7, n)
            if debug and n == 512:
                k.dma('pool', dbg_x1[i], xT, 'dbg', XR(xr), [])
            ffn(0, xT, xr, n)
            if debug and n == 512:
                k.dma('pool', dbg_x2[i], xT, 'dbg', XR(xr), [])

        def QKV(i, xT, xr, n, qTs, qTr):
            nblk = n // 128
            P.alias(QKN, L0N)
            norm_to_hT(xT, xr, hT, 'hT', sq, rbc, rrow, n=n, fused=True)
            wq = [wpiece("w_qkv", pc) for pc in range(3)]

            def qmm(b):
                ts_ = slice(b * 128, (b + 1) * 128)
                for pc in range(3):
                    for kc in range(8):
                        k.mm(PS(1 + pc), hT[:, kc, ts_], wq[pc][0][:, kc, :], kc == 0, kc == 7, [wq[pc][1], f'hT{kc}'], [PSR(1 + pc)])

            def qchain(b):
                B = i * 4 + b
                slot = B % NR
                k.cp('act', qk[:, 0:512], PS(1), [PSR(1)], ['qk'])
                k.cp('act', qk[:, 512:1024], PS(2), [PSR(2)], ['qk'])
                k.cp('act', qk[:, 1024:1280], PS(3)[:, 0:256], [PSR(3)], ['qk'])
                k.cp('act', Vr[:, slot, :, 0:64], PS(3)[:, 256:512].rearrange("p (v d) -> p v d", v=4), [PSR(3)], ['Vr'])
                k.act(qsq, qk, AF.Square, ['qk'], ['qsq'])
                k.reduce(stat[:, 8:28], qsq.rearrange("p (h d) -> p h d", d=64), ['qsq'], ['stat'])
                k.act(stat[:, 8:28], stat[:, 8:28], AF.Sqrt, ['stat', 'epsc'], ['stat'], scale=1.0 / 64, bias=epsc[:, 0:1])
                k.recip(stat[:, 8:28], stat[:, 8:28], ['stat'], ['stat'])
                for A in range(2):
                    i0_ = qk[:, A * 512:(A + 1) * 512].rearrange("p (b c d) -> p b c d", b=2, c=4)
                    i1_ = stat[:, 8 + A * 8:16 + A * 8].rearrange("p (b c) -> p b c", b=2).unsqueeze(3).to_broadcast([128, 2, 4, 64])
                    o_ = qn[:, A * 512:(A + 1) * 512].rearrange("p (c b d) -> p b c d", c=4, b=2)
                    k.tt('dve', o_, i0_, i1_, ALU.mult, ['qk', 'stat'], ['qn'])
                k.tt('dve', qn[:, 1024:1280].rearrange("p (h d) -> p h d", d=64), qk[:, 1024:1280].rearrange("p (h d) -> p h d", d=64),
                     stat[:, 24:28].unsqueeze(2).to_broadcast([128, 4, 64]), ALU.mult, ['qk', 'stat'], ['qn'])

            def qtr(b):
                B = i * 4 + b
                slot = B % NR
                ts_ = slice(b * 128, (b + 1) * 128)
                for half in range(2):
                    bk = 4 + half
                    for pi in range(4):
                        pr = half * 4 + pi
                        k.mm(PS(bk)[:, pi * 128:(pi + 1) * 128], qn[:, pr * 128:(pr + 1) * 128], identb, True, True, ['qn', 'identb'], [PSR(bk)])
                    k.cp('dve' if half == 0 else 'act', qTs[:, half * 4:(half + 1) * 4, ts_], PS(bk).rearrange("p (a t) -> p a t", a=4),
                         [PSR(bk)], [qTr])
                for kp in range(2):
                    k.mm(PS(6)[:, kp * 128:(kp + 1) * 128], qn[:, 1024 + kp * 128:1024 + (kp + 1) * 128], identb, True, True,
                         ['qn', 'identb'], [PSR(6)])
                k.ts('dve', kT[:, :, slot * 128:(slot + 1) * 128], PS(6)[:, 0:256].rearrange("p (a t) -> p a t", a=2), gqk[:, 0:1], ALU.mult,
                     [PSR(6), 'gqk'], ['kT'])

            qmm(0)
            for b in range(nblk):
                qchain(b)
                if b + 1 < nblk:
                    qmm(b + 1)
                qtr(b)

        def L1(i, xT, xr, qTs, qTr):
            P.alias(ATN, ['actT'])
            n = 512
            sbc = [0]

            def rs_of(qb):
                QB = i * 4 + qb
                return [r for r in range(3) if 0 <= QB - 1 + r < NBseq]

            def scores(qb, kv):
                QB = i * 4 + qb
                qs_ = slice(qb * 128, (qb + 1) * 128)
                hs = slice((kv % 2) * 64, (kv % 2) * 64 + 64)
                kp = kv // 2
                pt = PT[kv % 2]
                ptr = f'PT{kv % 2}'
                for r in rs_of(qb):
                    slot = (QB - 1 + r) % NR
                    sb = 1 + (sbc[0] % 3)
                    sbc[0] += 1
                    k.mm(PS(sb), kT[hs, kp, slot * 128:(slot + 1) * 128], qTs[hs, kp * 4:kp * 4 + 4, qs_], True, False,
                         ['kT', qTr], [PSR(sb)])
                    k.mm(PS(sb), identb, biasT[:, kv * 4:(kv + 1) * 4, r, :], False, True, ['identb', 'biasT'], [PSR(sb)])
                    k.act(pt[:, r, :], PS(sb), AF.Exp, [PSR(sb)], [ptr])

            def pv(qb, kv):
                QB = i * 4 + qb
                rs = rs_of(qb)
                pt = PT[kv % 2]
                ptr = f'PT{kv % 2}'
                for hh in range(4):
                    h = kv * 4 + hh
                    col = (h // 7) * 512 + (h % 7) * 65
                    for r in rs:
                        slot = (QB - 1 + r) % NR
                        k.mm(psum[:, 5 * 512 + col:5 * 512 + col + 65], pt[:, r, hh * 128:(hh + 1) * 128], Vr[:, slot, kv, :],
                             r == rs[0], r == rs[-1], [ptr, 'Vr'], [PSR(5 + h // 7)])

            def fin(qb):
                qs_ = slice(qb * 128, (qb + 1) * 128)
                for bg, (h0, nh) in enumerate([(0, 7), (7, 7), (14, 2)]):
                    ov = PS(5 + bg)[:, 0:nh * 65].rearrange("p (h e) -> p h e", e=65)
                    k.tt('dve', stat[:, 32 + h0:32 + h0 + nh].unsqueeze(2), ov[:, :, 64:65], esink[:, h0:h0 + nh].unsqueeze(2), ALU.add,
                         [PSR(5 + bg), 'esink'], ['stat2'])
                k.recip(stat[:, 32:48], stat[:, 32:48], ['stat2'], ['stat2'])
                for bg, (h0, nh) in enumerate([(0, 7), (7, 7), (14, 2)]):
                    ov = PS(5 + bg)[:, 0:nh * 65].rearrange("p (h e) -> p h e", e=65)
                    k.tt('dve', on[:, h0 * 64:(h0 + nh) * 64].rearrange("p (h d) -> p h d", d=64), ov[:, :, 0:64],
                         stat[:, 32 + h0:32 + h0 + nh].unsqueeze(2).to_broadcast([128, nh, 64]), ALU.mult, [PSR(5 + bg), 'stat2'], ['on'])
                for half in range(2):
                    bk = 0 if half == 0 else 4
                    for ci in range(4):
                        kc = half * 4 + ci
                        k.mm(PS(bk)[:, ci * 128:(ci + 1) * 128], on[:, kc * 128:(kc + 1) * 128], identb, True, True, ['on', 'identb'], [PSR(bk)])
                    k.cp('act' if half == 0 else 'dve', oT[:, half * 4:(half + 1) * 4, qs_], PS(bk).rearrange("p (a t) -> p a t", a=4),
                         [PSR(bk)], ['oT'])

            aunits = [(qb, kv) for qb in range(4) for kv in range(4)]
            scores(*aunits[0])
            for ui, (qb, kv) in enumerate(aunits):
                if ui + 1 < len(aunits):
                    scores(*aunits[ui + 1])
                pv(qb, kv)
                if kv == 3:
                    fin(qb)
            if debug:
                k.dma('pool', dbg_oT[i], oT, 'dbg', ['oT'], [])
            for pc in range(2):
                wo_, wor = wpiece("w_o", pc)
                for fci in range(4):
                    fc = pc * 4 + fci
                    bo = 6 + fc % 2
                    for kc in range(8):
                        k.mm(PS(bo, n), wo_[:, kc, fci * 128:(fci + 1) * 128], oT[:, kc, :], kc == 0, kc == 7, [wor, 'oT'], [PSR(bo)])
                    k.tt('dve', xT[:, fc, :], xT[:, fc, :], PS(bo, n), ALU.add, [f'{xr}.{fc}', PSR(bo)], [f'{xr}.{fc}'])
                    if fc >= 1:
                        ssq_step(xT, xr, sq, fc - 1, n)
            ssq_step(xT, xr, sq, 7, n)
            if debug:
                k.dma('pool', dbg_x3[i], xT, 'dbg', XR(xr), [])
            ffn(1, xT, xr, n)
            for b in range(4):
                ts_ = slice(b * 128, (b + 1) * 128)
                for kc in range(8):
                    bk = 1 + (b % 2) * 2 + kc // 4
                    k.mm(PS(bk)[:, (kc % 4) * 128:(kc % 4 + 1) * 128], xT[:, kc, ts_], ident, True, True, [f'{xr}.{kc}', 'ident'], [PSR(bk)])
                b0 = 1 + (b % 2) * 2
                xo, xor_ = xouts[b % 2], f'xin{2 + b % 2}'
                k.cp('act', xo[:, 0:512], PS(b0), [PSR(b0)], [xor_])
                k.cp('dve', xo[:, 512:1024], PS(b0 + 1), [PSR(b0 + 1)], [xor_])
                k.dma('pool', ydst[i * 512 + b * 128:i * 512 + (b + 1) * 128, :], xo, xor_, [xor_], [])

        nt = OWN // 512
        for i in range(nt):
            s = i % 2
            L0(i, xb[s], f'xb{s}', 512)
            QKV(i, xb[s], f'xb{s}', 512, qT[s], f'qT{s}')
            if i >= 1:
                L1(i - 1, xb[1 - s], f'xb{1 - s}', qT[1 - s], f'qT{1 - s}')
        if OWN < L:
            s = nt % 2
            L0(nt, xb[s], f'xb{s}', 128)
            QKV(nt, xb[s], f'xb{s}', 128, qT[s], f'qT{s}')
        s = (nt - 1) % 2
        L1(nt - 1, xb[s], f'xb{s}', qT[s], f'qT{s}')
        if debug:
            k.dma('pool', dbg_qT, qT[s], 'dbg', [f'qT{s}'], [])
            k.dma('pool', dbg_kT, kT, 'dbg', ['kT'], [])
            k.dma('pool', dbg_V, Vr, 'dbg', ['Vr'], [])
            k.dma('pool', dbg_bias, biasT, 'dbg', ['biasT'], [])
        P.barrier_all()

    ssm_phases(I["xs"], LS, LS, jobs=cast_jobs, hi=DA.lo)
    main_seq(I["xs"], ys, LS, LS)
    lc = ((OWNP + 128 + 1023) // 1024) * 1024 if OWNP < LP else LP
    ssm_phases(I["xp"], LP, min(lc, LP))
    main_seq(I["xp"], yp, LP, OWNP)
    block = stack.enter_context(nc.Block())
    P.emit(block)
    stack.close()
    return nc


def _rel_bucket_np(rel):
    half, max_exact = 16, 8
    ret = (rel > 0).astype(np.int32) * half
    n = np.abs(rel)
    nf = np.maximum(n, 1).astype(np.float32)
    large = max_exact + (np.log(nf / np.float32(max_exact)) / np.float32(np.log(128 / max_exact))
                         * np.float32(half - max_exact)).astype(np.int32)
    large = np.minimum(large, half - 1)
    return ret + np.where(n < max_exact, n, large)


def _consts():
    c = {}
    c["ident"] = np.eye(128, dtype=np.float32)
    tt = np.arange(128) // 16
    c["maskf"] = (tt[:, None] <= tt[None, :]).astype(np.float32)
    c["maskb"] = (tt[:, None] >= tt[None, :]).astype(np.float32)
    kexp = np.zeros((64, NSLOT), np.float32)
    for d in range(2):
        for tau in range(8):
            e = 7 - tau if d == 0 else tau
            kexp[d * 32:(d + 1) * 32, tau] = e
            kexp[d * 32:(d + 1) * 32, 8 + tau] = e - 8
            kexp[d * 32:(d + 1) * 32, 16 + tau] = tau + 1 if d == 0 else 8 - tau
    kexp[:, 24] = 1
    kexp[:, 25] = 8
    c["kexp"] = kexp.reshape(-1)
    c["iota32"] = np.arange(32, dtype=np.float32)
    c["sgn1"] = np.concatenate([-np.ones(64), np.ones(64)]).astype(np.float32)
    c["sgn2"] = -c["sgn1"]
    kk = np.arange(128)[:, None, None]
    r = np.arange(3)[None, :, None]
    qq = np.arange(128)[None, None, :]
    rel = (r - 1) * 128 + kk - qq
    c["maskc"] = np.where(np.abs(rel) <= 128, 0.0, -80.0).astype(np.float32)
    c["_rel"] = rel
    return c


def _core_inputs(inp, core, xs, xp, consts):
    rev = (core % 2 == 1)
    f = (lambda a: np.ascontiguousarray(a[::-1])) if rev else (lambda a: np.ascontiguousarray(a))
    m = {}
    m["xs"] = f(xs)
    m["xp"] = f(xp)
    m["norm_mix"] = inp["norm_mix"]
    m["norm_ffn"] = inp["norm_ffn"]
    m["w_in"] = inp["w_in_even"][0]
    m["glu_w"] = inp["glu_w"][0]
    m["glu_b"] = inp["glu_b"][0]
    m["w_out"] = inp["w_out_even"][0]
    m["w_qkv"] = inp["w_qkv"][0]
    m["w_o"] = inp["w_o"][0]
    for l in range(2):
        m[f"wg{l}"] = inp["ffn_w_gate"][l]
        m[f"wu{l}"] = inp["ffn_w_up"][l]
        m[f"wd{l}"] = inp["ffn_w_down"][l]
    dsw = (lambda a: np.ascontiguousarray(a[::-1])) if rev else (lambda a: a)
    m["lam_re"] = dsw(inp["ssm_lam_re"][0])
    m["lam_im"] = dsw(inp["ssm_lam_im"][0])
    m["log_dt"] = dsw(inp["ssm_log_dt"][0])
    m["b_re"] = dsw(inp["ssm_b_re"][0])
    m["b_im"] = dsw(inp["ssm_b_im"][0])
    m["c_re"] = dsw(inp["ssm_c_re"][0])
    m["c_im"] = dsw(inp["ssm_c_im"][0])
    m["ssm_d"] = inp["ssm_d"][0]
    m["gm_norm"] = inp["gm_norm"][0]
    ws, bs = inp["gm_w_s"][0], inp["gm_b_s"][0]
    if rev:
        ws, bs = ws[:, ::-1, ::-1], bs[:, ::-1]
    m["gm_w_s"] = np.ascontiguousarray(ws)
    m["gm_b_s"] = np.ascontiguousarray(bs)
    m["q_norm"] = inp["q_norm"][0]
    m["k_norm"] = inp["k_norm"][0]
    m["attn_sink"] = inp["attn_sink"][0]
    rel = consts["_rel"]
    bucket = _rel_bucket_np(-rel if rev else rel)
    bg = inp["rel_table"][bucket]
    m["biasg"] = np.ascontiguousarray(bg.transpose(0, 3, 1, 2))
    for kname in ["ident", "maskf", "maskb", "kexp", "iota32", "sgn1", "sgn2", "maskc"]:
        m[kname] = consts[kname]
    return {k_: np.ascontiguousarray(v, dtype=np.float32) for k_, v in m.items()}


_NC_CACHE = {}


def run_cores(inp, xs_list, xp_list, LS, LP, OWNP, ncores, debug=False):
    key = (LS, LP, OWNP, debug)
    if key not in _NC_CACHE:
        _NC_CACHE[key] = build(LS, LP, OWNP, debug=debug)
    nc = _NC_CACHE[key]
    consts = _consts()
    in_maps = [_core_inputs(inp, c, xs_list[c], xp_list[c], consts) for c in range(ncores)]
    res = run_bass_kernel_spmd(nc, in_maps, core_ids=list(range(ncores)))
    outs = []
    for c in range(ncores):
        r = res.results[c]
        ys_, yp_ = np.asarray(r["ys"]), np.asarray(r["yp"])
        if c % 2 == 1:
            ys_, yp_ = ys_[::-1], yp_[::-1]
        outs.append((ys_, yp_, r) if debug else (ys_, yp_))
    return outs


def kernel(**inputs):
    inp = {k_: np.asarray(v) for k_, v in inputs.items()}
    x_prompt, x_sample = inp["x_prompt"], inp["x_sample"]
    B, LP, _ = x_prompt.shape
    BS, LS, _ = x_sample.shape
    ncores = 8
    xs_list = [x_sample[c] for c in range(ncores)]
    xp_list = [x_prompt[c // 2] for c in range(ncores)]
    outs = run_cores(inp, xs_list, xp_list, LS, LP, LP // 2, ncores)
    y_sample = np.stack([outs[c][0] for c in range(ncores)], axis=0).astype(np.float32)
    y_prompt = np.zeros_like(x_prompt, dtype=np.float32)
    H = LP // 2
    for c in range(ncores):
        b = c // 2
        if c % 2 == 0:
            y_prompt[b, 0:H] = outs[c][1]
        else:
            y_prompt[b, H:LP] = outs[c][1]
    return (y_prompt, y_sample)
```

```python
import contextlib
import numpy as np
import ml_dtypes
import concourse.bass as bass
import concourse.mybir as mybir
from concourse.bass_utils import run_bass_kernel_spmd

F32 = mybir.dt.float32
BF16 = mybir.dt.bfloat16
AF = mybir.ActivationFunctionType
ALU = mybir.AluOpType
AX = mybir.AxisListType

D = 1024
FF = 2816
NJ = FF // 128
EPS = 1e-6
MAGIC = 12582912.0
TWO_PI = float(2 * np.pi)
ENGS = ['pe', 'act', 'dve', 'pool', 'sp']
NSLOT = 27


class Prog:
    def __init__(self, nc, stack):
        self.nc = nc
        self.stack = stack
        self.ops = {e: [] for e in ENGS}
        self.cnt = {e: 0 for e in ENGS}
        self.seen = {e: {} for e in ENGS}
        self.reg = {}
        self.dmasem = {}
        self.sems = {}

    def sem(self, key):
        if key not in self.sems:
            self.sems[key] = self.stack.enter_context(self.nc.semaphore("s_" + str(key)))
        return self.sems[key]

    def _deps(self, eng, reads, writes):
        need = {}

        def add(k, v):
            if v > need.get(k, 0):
                need[k] = v
        for r in reads:
            st = self.reg.get(r)
            if st and st['w']:
                add(*st['w'])
        for r in writes:
            st = self.reg.get(r)
            if st:
                if st['w']:
                    add(*st['w'])
                for k, v in st['r'].items():
                    add(k, v)
        seen = self.seen[eng]
        waits = []
        for k, v in need.items():
            if eng == 'pe' and k == 'pe':
                continue
            if k.startswith('d_'):
                v = self.dmasem[k]
            if seen.get(k, 0) < v:
                seen[k] = v
                waits.append((k, v))
        return waits

    def _commit(self, reads, writes, tok):
        k, v = tok
        for r in reads:
            st = self.reg.setdefault(r, {'w': None, 'r': {}})
            if st['r'].get(k, 0) < v:
                st['r'][k] = v
        for r in writes:
            self.reg[r] = {'w': tok, 'r': {}}

    def op(self, eng, fn, reads=(), writes=()):
        waits = self._deps(eng, reads, writes)
        self.cnt[eng] += 1
        tok = (eng, self.cnt[eng])
        self.ops[eng].append((waits, fn, (eng, 1)))
        self._commit(reads, writes, tok)

    def dma(self, eng, fn, semname, reads=(), writes=()):
        waits = self._deps(eng, reads, writes)
        key = 'd_' + semname
        c = self.dmasem.get(key, 0) + 16
        self.dmasem[key] = c
        self.ops[eng].append((waits, fn, (key, 16)))
        self._commit(reads, writes, (key, c))

    def alias(self, new, olds):
        merged = {}
        for o in olds:
            st = self.reg.get(o)
            if not st:
                continue
            if st['w']:
                k, v = st['w']
                merged[k] = max(merged.get(k, 0), v)
            for k, v in st['r'].items():
                merged[k] = max(merged.get(k, 0), v)
        for n in new:
            st = self.reg.get(n)
            m2 = dict(merged)
            if st:
                if st['w']:
                    k, v = st['w']
                    m2[k] = max(m2.get(k, 0), v)
                for k, v in st['r'].items():
                    m2[k] = max(m2.get(k, 0), v)
            self.reg[n] = {'w': None, 'r': m2}

    def barrier_all(self):
        for e in ENGS:
            waits = []
            for k, v in list(self.cnt.items()):
                if v > 0 and self.seen[e].get(k, 0) < v and not (e == 'pe' and k == 'pe'):
                    self.seen[e][k] = v
                    waits.append((k, v))
            for k, v in self.dmasem.items():
                if self.seen[e].get(k, 0) < v:
                    self.seen[e][k] = v
                    waits.append((k, v))
            if waits:
                self.ops[e].append((waits, None, None))
        self.reg = {}

    def emit(self, block):
        engobj = {'pe': 'tensor', 'act': 'scalar', 'dve': 'vector', 'pool': 'gpsimd', 'sp': 'sync'}
        for e in ENGS:
            self.sem(e)
        for k in self.dmasem:
            self.sem(k)
        for e in ENGS:
            ops = self.ops[e]

            def body(eng, ops=ops):
                for waits, fn, inc in ops:
                    for k, v in waits:
                        eng.wait_ge(self.sem(k), v)
                    if fn is not None:
                        ins = fn(eng)
                        ins.then_inc(self.sem(inc[0]), inc[1])
            getattr(block, engobj[e])(body)


class Arena:
    def __init__(self, nc, lo, hi):
        self.nc, self.lo, self.hi, self.cur = nc, lo, hi, lo
        self.n = 0

    def alloc(self, name, shape, dt):
        size = int(np.prod(shape[1:])) * (4 if dt == F32 else 2)
        off = (self.cur + 63) // 64 * 64
        assert off + size <= self.hi, f"SBUF arena overflow at {name}: need {off + size - self.hi} more bytes"
        self.cur = off + size
        self.n += 1
        return self.nc.alloc_sbuf_tensor_at(f"{name}_{self.n}_{off}", list(shape), dt, offset=off).ap()

    def sub(self):
        return Arena(self.nc, self.cur, self.hi)


class K:
    def __init__(self, P):
        self.P = P

    def tt(self, eng, out, a, b, op, r, w):
        self.P.op(eng, lambda e: e.tensor_tensor(out=out, in0=a, in1=b, op=op), r, w)

    def ts(self, eng, out, a, s1, op0, r, w, s2=None, op1=None):
        if op1 is None:
            self.P.op(eng, lambda e: e.tensor_scalar(out=out, in0=a, scalar1=s1, scalar2=None, op0=op0), r, w)
        else:
            self.P.op(eng, lambda e: e.tensor_scalar(out=out, in0=a, scalar1=s1, scalar2=s2, op0=op0, op1=op1), r, w)

    def stt(self, eng, out, a, scalar, b, op0, op1, r, w):
        self.P.op(eng, lambda e: e.scalar_tensor_tensor(out=out, in0=a, scalar=scalar, in1=b, op0=op0, op1=op1), r, w)

    def act(self, out, in_, func, r, w, scale=1.0, bias=None, accum=None):
        def f(e):
            kw = {}
            if bias is not None:
                kw['bias'] = bias
            if accum is not None:
                kw['accum_out'] = accum
            return e.activation(out=out, in_=in_, func=func, scale=scale, **kw)
        self.P.op('act', f, r, w)

    def cp(self, eng, out, in_, r, w):
        if eng == 'act':
            self.P.op('act', lambda e: e.activation(out=out, in_=in_, func=AF.Copy), r, w)
        else:
            self.P.op(eng, lambda e: e.tensor_copy(out=out, in_=in_), r, w)

    def mm(self, out, lhsT, rhs, start, stop, r, w):
        self.P.op('pe', lambda e: e.matmul(out, lhsT=lhsT, rhs=rhs, start=start, stop=stop), r, w)

    def dma(self, q, out, in_, sem, r, w, slow=False):
        if sem == 'c0':
            sem = 'c_' + w[0]
        if slow:
            self.P.dma(q, lambda e: e.dma_start(out=out, in_=in_, allow_slow_non_contiguous=True), sem, r, w)
        else:
            self.P.dma(q, lambda e: e.dma_start(out=out, in_=in_), sem, r, w)

    def memset(self, eng, out, val, w):
        self.P.op(eng, lambda e: e.memset(out, val), (), w)

    def recip(self, out, in_, r, w):
        self.P.op('dve', lambda e: e.reciprocal(out=out, in_=in_), r, w)

    def reduce(self, out, in_, r, w):
        self.P.op('dve', lambda e: e.tensor_reduce(out=out, in_=in_, axis=AX.X, op=ALU.add), r, w)

    def scan(self, out, d0, d1, r, w):
        self.P.op('dve', lambda e: e.tensor_tensor_scan(out=out, data0=d0, data1=d1, initial=0.0,
                                                       op0=ALU.mult, op1=ALU.add), r, w)


WPIECES = [
    ("w_in", 1024, 1536, "nm0"), ("glu_w", 512, 512, None), ("w_out", 1024, 1024, None),
    ("w_qkv", 1024, 1536, "nm1"), ("w_o", 1024, 1024, None),
    ("wg0", 1024, FF, "nf0"), ("wu0", 1024, FF, "nf0"), ("wg1", 1024, FF, "nf1"), ("wu1", 1024, FF, "nf1"),
]


def build(LS, LP, OWNP, debug=False):
    assert LS % 1024 == 0 and LP % 1024 == 0 and OWNP % 512 == 0
    nc = bass.Bass("TRN2", target_bir_lowering=False)
    stack = contextlib.ExitStack()
    P = Prog(nc, stack)
    k = K(P)

    def din(name, shape):
        return nc.dram_tensor(name, list(shape), F32, kind="ExternalInput").ap()

    def dscr(name, shape, dt=BF16):
        return nc.dram_tensor(name, list(shape), dt).ap()

    I = {}
    for nm, shp in [("xs", [LS, D]), ("xp", [LP, D]), ("norm_mix", [2, D]), ("norm_ffn", [2, D]),
                    ("w_in", [D, 1536]), ("glu_w", [512, 512]), ("glu_b", [512]), ("w_out", [D, D]),
                    ("w_qkv", [D, 1536]), ("w_o", [D, D]),
                    ("wg0", [D, FF]), ("wu0", [D, FF]), ("wd0", [FF, D]),
                    ("wg1", [D, FF]), ("wu1", [D, FF]), ("wd1", [FF, D]),
                    ("lam_re", [2, 32, 64]), ("lam_im", [2, 32, 64]), ("log_dt", [2, 32]),
                    ("b_re", [2, 32, 64, 16]), ("b_im", [2, 32, 64, 16]),
                    ("c_re", [2, 32, 16, 64]), ("c_im", [2, 32, 16, 64]), ("ssm_d", [512]),
                    ("gm_norm", [4, 128]), ("gm_w_s", [4, 128, 128]), ("gm_b_s", [4, 128]),
                    ("q_norm", [64]), ("k_norm", [64]), ("attn_sink", [16]),
                    ("biasg", [128, 16, 3, 128]), ("maskc", [128, 3, 128]),
                    ("ident", [128, 128]), ("maskf", [128, 128]), ("maskb", [128, 128]),
                    ("kexp", [64 * NSLOT]), ("iota32", [32]), ("sgn1", [128]), ("sgn2", [128])]:
        I[nm] = din(nm, shp)
    ys = nc.dram_tensor("ys", [LS, D], F32, kind="ExternalOutput").ap()
    yp = nc.dram_tensor("yp", [OWNP, D], F32, kind="ExternalOutput").ap()
    dbg = {}

    WS = {}
    for nm, Kd, N, g in WPIECES:
        WS[nm] = dscr("s_" + nm, [(N + 511) // 512, 128, Kd // 128, 512])
    WS["wd0"] = dscr("s_wd0", [8, 128, NJ, 128])
    WS["wd1"] = dscr("s_wd1", [8, 128, NJ, 128])
    ssmw = dscr("s_ssmw", [32, 128, 9, 128])
    Lmax = max(LS, LP)
    if debug:
        gTs = nc.dram_tensor("dbg_gT", [4, 128, Lmax], BF16, kind="ExternalOutput").ap()
        dbg_x2 = nc.dram_tensor("dbg_x2", [Lmax // 512, 128, 8, 512], F32, kind="ExternalOutput").ap()
        dbg_x3 = nc.dram_tensor("dbg_x3", [Lmax // 512, 128, 8, 512], F32, kind="ExternalOutput").ap()
        dbg_oT = nc.dram_tensor("dbg_oT", [Lmax // 512, 128, 8, 512], BF16, kind="ExternalOutput").ap()
        dbg_qT = nc.dram_tensor("dbg_qT", [128, 8, 512], BF16, kind="ExternalOutput").ap()
        dbg_kT = nc.dram_tensor("dbg_kT", [128, 2, 12 * 128], BF16, kind="ExternalOutput").ap()
        dbg_V = nc.dram_tensor("dbg_V", [128, 12, 4, 65], BF16, kind="ExternalOutput").ap()
        dbg_bias = nc.dram_tensor("dbg_bias", [128, 16, 3, 128], BF16, kind="ExternalOutput").ap()
        dbg_x1 = nc.dram_tensor("dbg_x1", [Lmax // 512, 128, 8, 512], F32, kind="ExternalOutput").ap()
    else:
        gTs = dscr("s_gT", [4, 128, Lmax])

    base = (nc.sbuf_base + 63) // 64 * 64
    top = nc.sbuf_top // 64 * 64
    A0 = Arena(nc, base, top)
    psum = nc.alloc_psum_tensor("ps", [128, 4096], F32).ap()

    def PS(b, n=512):
        return psum[:, b * 512:b * 512 + n]

    def PSR(b):
        return f"ps{b}"

    ident = A0.alloc("ident", [128, 128], F32)
    identb = A0.alloc("identb", [128, 128], BF16)
    ones_col = A0.alloc("ones_col", [128, 1], BF16)
    ones_row = A0.alloc("ones_row", [1, 128], F32)
    rho8 = A0.alloc("rho8", [128, 64], F32)
    phi = A0.alloc("phi", [128, 64], F32)
    f1 = A0.alloc("f1", [128, 64], F32)
    iota = A0.alloc("iota", [128, 32], F32)
    glub = A0.alloc("glub", [128, 4], F32)
    gmn = A0.alloc("gmn", [128, 512], F32)
    bsT = A0.alloc("bsT", [128, 4, 128], F32)
    wsT = A0.alloc("wsT", [128, 4, 128], BF16)
    esink = A0.alloc("esink", [128, 16], F32)
    gqk = A0.alloc("gqk", [128, 1], F32)
    sgn1 = A0.alloc("sgn1", [128, 1], F32)
    sgn2 = A0.alloc("sgn2", [128, 1], F32)
    epsc = A0.alloc("epsc", [128, 1], F32)
    ccol = A0.alloc("ccol", [128, 3], F32)

    k.dma('sp', ident, I["ident"], 'c0', [], ['ident'])
    k.cp('dve', identb, ident, ['ident'], ['identb'])
    k.memset('dve', ones_col, 1.0, ['ones_col'])
    k.memset('dve', ones_row, 1.0, ['ones_row'])
    k.memset('dve', epsc, EPS, ['epsc'])
    k.memset('dve', ccol[:, 0:1], MAGIC, ['ccol'])
    k.memset('dve', ccol[:, 1:2], -MAGIC, ['ccol'])
    k.memset('dve', ccol[:, 2:3], 0.25, ['ccol'])
    k.dma('sp', iota, I["iota32"].partition_broadcast(128), 'c0', [], ['iota'])
    k.dma('sp', sgn1, I["sgn1"].rearrange("(p o) -> p o", o=1), 'c0', [], ['sgn1'])
    k.dma('sp', sgn2, I["sgn2"].rearrange("(p o) -> p o", o=1), 'c0', [], ['sgn2'])
    k.dma('sp', glub, I["glu_b"].rearrange("(c p) -> p c", p=128), 'c0', [], ['glub'], slow=True)
    k.dma('sp', gmn, I["gm_norm"].rearrange("h d -> (h d)").partition_broadcast(128), 'c0', [], ['gmn'])
    k.dma('sp', bsT, I["gm_b_s"].rearrange("h i -> (h i)").partition_broadcast(128), 'c0', [], ['bsT'])
    k.dma('sp', esink, I["attn_sink"].partition_broadcast(128), 'c0', [], ['esink'])
    k.act(esink, esink, AF.Exp, ['esink'], ['esink'])

    gains = A0.alloc("gains", [128, 4, 8], F32)
    S1 = A0.sub()
    tq = S1.alloc("tq", [128, 2], F32)
    for half in range(2):
        k.dma('sp', tq[half * 64:(half + 1) * 64, 0:1], I["q_norm"].rearrange("(p o) -> p o", o=1), 'c0', [], ['tq'])
        k.dma('sp', tq[half * 64:(half + 1) * 64, 1:2], I["k_norm"].rearrange("(p o) -> p o", o=1), 'c0', [], ['tq'])
    k.stt('dve', gqk, tq[:, 0:1], 0.125, tq[:, 1:2], ALU.mult, ALU.mult, ['tq'], ['gqk'])
    wst = S1.alloc("wst", [128, 4, 128], F32)
    k.dma('sp', wst, I["gm_w_s"].rearrange("h i j -> i h j"), 'c0', [], ['wst'])
    for h in range(4):
        k.mm(PS(h, 128), wst[:, h, :], ident, True, True, ['wst', 'ident'], [PSR(h)])
        k.cp('dve', wsT[:, h, :], PS(h, 128), [PSR(h)], ['wsT'])

    for gi, (src, row) in enumerate([("norm_mix", 0), ("norm_mix", 1), ("norm_ffn", 0), ("norm_ffn", 1)]):
        k.dma('sp', gains[:, gi, :], I[src][row].rearrange("(c p) -> p c", p=128), 'c0', [], ['gains'], slow=True)
    gidx = {"nm0": 0, "nm1": 1, "nf0": 2, "nf1": 3}
    top_hi = top
    DEF_BYTES = 2 * (8 * 512 * 4) + 2 * (8 * 512 * 2) + 256
    DA = Arena(nc, (top_hi - DEF_BYTES) // 64 * 64, top_hi)
    stg = [DA.alloc(f"stg{i}", [128, 8, 512], F32) for i in range(2)]
    stb = [DA.alloc(f"stb{i}", [128, 8, 512], BF16) for i in range(2)]
    cnt = [0]
    cast_jobs = []

    def mk_piece_job(nm, Kd, N, g, pc, engs):
        def job():
            KC = Kd // 128
            src = I[nm].rearrange("(c p) n -> p c n", p=128)
            n0 = pc * 512
            nn = min(512, N - n0)
            s_ = cnt[0] % 2
            cnt[0] += 1
            k.dma('sp', stg[s_][:, 0:KC, 0:nn], src[:, :, n0:n0 + nn], f'stg{s_}', [], [f'stg{s_}'])
            for kc in range(KC):
                eng = engs[(cnt[0] + kc) % len(engs)]
                if g is None:
                    k.cp(eng, stb[s_][:, kc, 0:nn], stg[s_][:, kc, 0:nn], [f'stg{s_}'], [f'stb{s_}'])
                elif eng == 'act':
                    k.act(stb[s_][:, kc, 0:nn], stg[s_][:, kc, 0:nn], AF.Copy, [f'stg{s_}', 'gains'], [f'stb{s_}'],
                          scale=gains[:, gidx[g], kc:kc + 1])
                else:
                    k.ts(eng, stb[s_][:, kc, 0:nn], stg[s_][:, kc, 0:nn], gains[:, gidx[g], kc:kc + 1], ALU.mult,
                         [f'stg{s_}', 'gains'], [f'stb{s_}'])
            k.dma('pool', WS[nm][pc][:, :, 0:nn], stb[s_][:, 0:KC, 0:nn], f'stb{s_}', [f'stb{s_}'], ['WS_' + nm])
        return job

    def mk_wd_job(l, fc, jh, engs):
        def job():
            src = I[f"wd{l}"].rearrange("(j p) n -> p j n", p=128)
            s_ = cnt[0] % 2
            cnt[0] += 1
            stgv = stg[s_].rearrange("p a b -> p (a b)")[:, 0:11 * 128].rearrange("p (j n) -> p j n", n=128)
            stbv = stb[s_].rearrange("p a b -> p (a b)")[:, 0:11 * 128].rearrange("p (j n) -> p j n", n=128)
            k.dma('sp', stgv, src[:, jh * 11:(jh + 1) * 11, fc * 128:(fc + 1) * 128], f'stg{s_}', [], [f'stg{s_}'])
            k.cp(engs[cnt[0] % len(engs)], stbv, stgv, [f'stg{s_}'], [f'stb{s_}'])
            k.dma('pool', WS[f"wd{l}"][fc][:, jh * 11:(jh + 1) * 11, :], stbv, f'stb{s_}', [f'stb{s_}'], [f'WS_wd{l}'])
        return job

    for nm, Kd, N, g in WPIECES:
        for pc in range((N + 511) // 512):
            if nm == "w_in":
                mk_piece_job(nm, Kd, N, g, pc, ['dve', 'act'])()
            else:
                cast_jobs.append(mk_piece_job(nm, Kd, N, g, pc, ['act']))
    for l in range(2):
        for fc in range(8):
            for jh in range(2):
                cast_jobs.append(mk_wd_job(l, fc, jh, ['act']))

    def bc(ap, shape):
        return ap.to_broadcast(shape)

    kx = S1.alloc("kx", [128, 64, NSLOT], F32)
    k.dma('sp', kx, I["kexp"].partition_broadcast(128), 'c0', [], ['kx'])
    lr = S1.alloc("lr", [128, 64], F32)
    li = S1.alloc("li", [128, 64], F32)
    dtb = S1.alloc("dtb", [128, 64], F32)
    for half in range(2):
        k.dma('sp', lr[half * 64:(half + 1) * 64, :], I["lam_re"].rearrange("d g p -> p (d g)"), 'c0', [], ['lr'], slow=True)
        k.dma('sp', li[half * 64:(half + 1) * 64, :], I["lam_im"].rearrange("d g p -> p (d g)"), 'c0', [], ['li'], slow=True)
    k.dma('sp', dtb, I["log_dt"].rearrange("d g -> (d g)").partition_broadcast(128), 'c0', [], ['dtb'])
    k.act(dtb, dtb, AF.Exp, ['dtb'], ['dtb'])
    lrdt = S1.alloc("lrdt", [128, 64], F32)
    lidt = S1.alloc("lidt", [128, 64], F32)
    k.tt('dve', lrdt, lr, dtb, ALU.mult, ['lr', 'dtb'], ['lrdt'])
    k.stt('dve', lidt, li, 1.0 / TWO_PI, dtb, ALU.mult, ALU.mult, ['li', 'dtb'], ['lidt'])
    MAG = S1.alloc("MAG", [128, 64, NSLOT], F32)
    Wt = S1.alloc("Wt", [128, 64, NSLOT], F32)
    Rt = S1.alloc("Rt", [128, 64, NSLOT], F32)
    SINT = S1.alloc("SINT", [128, 64, NSLOT], F32)
    COST = S1.alloc("COST", [128, 64, NSLOT], F32)
    sh3 = [128, 64, NSLOT]
    k.tt('dve', MAG, kx, bc(lrdt.unsqueeze(2), sh3), ALU.mult, ['kx', 'lrdt'], ['MAG'])
    k.act(MAG, MAG, AF.Exp, ['MAG'], ['MAG'])
    k.tt('dve', Wt, kx, bc(lidt.unsqueeze(2), sh3), ALU.mult, ['kx', 'lidt'], ['Wt'])
    k.ts('dve', Rt, Wt, MAGIC, ALU.add, ['Wt'], ['Rt'], s2=MAGIC, op1=ALU.subtract)
    k.tt('dve', SINT, Wt, Rt, ALU.subtract, ['Wt', 'Rt'], ['SINT'])
    k.cp('dve', phi, SINT[:, :, 25], ['SINT'], ['phi'])
    k.ts('dve', f1, phi, 32.0, ALU.mult, ['phi'], ['f1'])
    f1r = S1.alloc("f1r", [128, 64], F32)
    k.ts('dve', f1r, f1, MAGIC, ALU.add, ['f1'], ['f1r'], s2=MAGIC, op1=ALU.subtract)
    k.tt('dve', f1, f1, f1r, ALU.subtract, ['f1', 'f1r'], ['f1'])
    k.act(SINT, SINT, AF.Sin, ['SINT'], ['SINT'], scale=TWO_PI)
    k.ts('dve', Wt, Wt, 0.25, ALU.add, ['Wt'], ['Wt'])
    k.ts('dve', Rt, Wt, MAGIC, ALU.add, ['Wt'], ['Rt'], s2=MAGIC, op1=ALU.subtract)
    k.tt('dve', COST, Wt, Rt, ALU.subtract, ['Wt', 'Rt'], ['COST'])
    k.act(COST, COST, AF.Sin, ['COST'], ['COST'], scale=TWO_PI)
    AR, AI = COST, SINT
    k.tt('dve', AR, MAG, COST, ALU.mult, ['MAG', 'COST'], ['COST'])
    k.tt('dve', AI, MAG, SINT, ALU.mult, ['MAG', 'SINT'], ['SINT'])
    k.cp('dve', rho8, MAG[:, :, 25], ['MAG'], ['rho8'])
    den = S1.alloc("den", [128, 64], F32)
    t1 = S1.alloc("t1", [128, 64], F32)
    t2 = S1.alloc("t2", [128, 64], F32)
    arm1 = S1.alloc("arm1", [128, 64], F32)
    zr = S1.alloc("zr", [128, 64], F32)
    zi = S1.alloc("zi", [128, 64], F32)
    k.tt('dve', den, lr, lr, ALU.mult, ['lr'], ['den'])
    k.tt('dve', t1, li, li, ALU.mult, ['li'], ['t1'])
    k.tt('dve', den, den, t1, ALU.add, ['den', 't1'], ['den'])
    k.recip(den, den, ['den'], ['den'])
    k.ts('dve', arm1, AR[:, :, 24], -1.0, ALU.add, ['COST'], ['arm1'])
    k.tt('dve', t1, arm1, lr, ALU.mult, ['arm1', 'lr'], ['t1'])
    k.tt('dve', t2, AI[:, :, 24], li, ALU.mult, ['SINT', 'li'], ['t2'])
    k.tt('dve', t1, t1, t2, ALU.add, ['t1', 't2'], ['t1'])
    k.tt('dve', zr, t1, den, ALU.mult, ['t1', 'den'], ['zr'])
    k.tt('dve', t1, AI[:, :, 24], lr, ALU.mult, ['SINT', 'lr'], ['t1'])
    k.tt('dve', t2, arm1, li, ALU.mult, ['arm1', 'li'], ['t2'])
    k.tt('dve', t1, t1, t2, ALU.subtract, ['t1', 't2'], ['t1'])
    k.tt('dve', zi, t1, den, ALU.mult, ['t1', 'den'], ['zi'])
    P0 = S1.alloc("P0", [128, 64, 16], F32)
    Q0 = S1.alloc("Q0", [128, 64, 16], F32)
    Pm = S1.alloc("Pm", [128, 64, 16], F32)
    Qm = S1.alloc("Qm", [128, 64, 16], F32)
    tb = S1.alloc("tb", [128, 64, 16], F32)
    bre = I["b_re"].rearrange("d g p h -> p (d g) h")
    bim = I["b_im"].rearrange("d g p h -> p (d g) h")
    k.dma('sp', P0[0:64], bre, 'c0', [], ['P0'])
    k.dma('sp', P0[64:128], bim, 'c0', [], ['P0'])
    k.dma('sp', Q0[0:64], bim, 'c0', [], ['Q0'])
    k.dma('sp', Q0[64:128], bre, 'c0', [], ['Q0'])
    sh = [128, 64, 16]
    zrb, zib = bc(zr.unsqueeze(2), sh), bc(zi.unsqueeze(2), sh)
    k.tt('dve', tb, Q0, zib, ALU.mult, ['Q0', 'zi'], ['tb'])
    k.tt('dve', Pm, P0, zrb, ALU.mult, ['P0', 'zr'], ['Pm'])
    k.stt('dve', Pm, tb, sgn1[:, 0:1], Pm, ALU.mult, ALU.add, ['tb', 'Pm', 'sgn1'], ['Pm'])
    k.tt('dve', tb, P0, zib, ALU.mult, ['P0', 'zi'], ['tb'])
    k.tt('dve', Qm, Q0, zrb, ALU.mult, ['Q0', 'zr'], ['Qm'])
    k.stt('dve', Qm, tb, sgn2[:, 0:1], Qm, ALU.mult, ALU.add, ['tb', 'Qm', 'sgn2'], ['Qm'])
    CT1 = S1.alloc("CT1", [128, 64, 16], F32)
    CT2 = S1.alloc("CT2", [128, 64, 16], F32)
    Cst = S1.alloc("Cst", [128, 8, 2, 64], F32)
    cre = I["c_re"].rearrange("d (gq g8) h p -> (g8 h) (d gq) p", g8=8)
    cim = I["c_im"].rearrange("d (gq g8) h p -> (g8 h) (d gq) p", g8=8)
    for which, CT in enumerate([CT1, CT2]):
        a_, b_ = (cre, cim) if which == 0 else (cim, cre)
        k.dma('sp', Cst[:, :, 0, :], a_, 'c0', [], ['Cst'])
        k.dma('sp', Cst[:, :, 1, :], b_, 'c0', [], ['Cst'])
        for blk in range(8):
            k.mm(PS(blk, 128), Cst[:, blk, :, :].rearrange("p a b -> p (a b)"), ident, True, True,
                 ['Cst', 'ident'], [PSR(blk)])
            k.cp('dve', CT[:, blk * 8:(blk + 1) * 8, :].rearrange("p a b -> p (a b)"), PS(blk, 128),
                 [PSR(blk)], ['CT%d' % which])
    dcol = S1.alloc("dcol", [128, 32], F32)
    for tau in range(8):
        k.dma('sp', dcol[tau * 16:(tau + 1) * 16, :], I["ssm_d"].rearrange("(g h) -> h g", h=16), 'c0', [], ['dcol'], slow=True)
    mkf = S1.alloc("mkf", [128, 128], F32)
    mkb = S1.alloc("mkb", [128, 128], F32)
    k.dma('sp', mkf, I["maskf"], 'c0', [], ['mkf'])
    k.dma('sp', mkb, I["maskb"], 'c0', [], ['mkb'])

    GB = 8
    sh4 = [128, GB, 8, 16]
    TW = {}
    for nm in ["WBt", "WBs", "Vt", "WC", "WC2"]:
        for d in range(2):
            TW[nm, d] = S1.alloc(f"{nm}{d}", sh4, F32)
    ta = S1.alloc("ta", sh4, F32)
    tb4 = S1.alloc("tb4", sh4, F32)
    wtile = [S1.alloc(f"wtile{i}", [128, 9, 128], BF16) for i in range(2)]
    kt1 = S1.alloc("kt1", [128, 128], F32)
    kt2 = S1.alloc("kt2", [128, 128], F32)
    for gb in range(32 // GB):
        for d in range(2):
            gs = slice(d * 32 + gb * GB, d * 32 + (gb + 1) * GB)

            def ER(s0):
                return bc(AR[:, gs, s0:s0 + 8].unsqueeze(3), sh4), bc(AI[:, gs, s0:s0 + 8].unsqueeze(3), sh4)

            def HB(t):
                return bc(t[:, gs, :].unsqueeze(2), sh4)
            rr = ['COST', 'SINT', 'Pm', 'Qm', 'CT0', 'CT1', 'sgn1', 'sgn2']
            er, ei = ER(0)
            o = TW["WBt", d]
            k.tt('dve', ta, er, HB(Pm), ALU.mult, rr, ['ta'])
            k.tt('dve', tb4, ei, HB(Qm), ALU.mult, rr, ['tb4'])
            k.stt('dve', o, tb4, sgn1[:, 0:1], ta, ALU.mult, ALU.add, ['ta', 'tb4'] + rr, ['WBt%d' % d])
            o = TW["WBs", d]
            k.tt('dve', ta, er, HB(Qm), ALU.mult, rr, ['ta'])
            k.tt('dve', tb4, ei, HB(Pm), ALU.mult, rr, ['tb4'])
            if d == 0:
                k.stt('dve', o, ta, sgn2[:, 0:1], tb4, ALU.mult, ALU.add, ['ta', 'tb4'] + rr, ['WBs%d' % d])
            else:
                k.stt('dve', o, ta, sgn1[:, 0:1], tb4, ALU.mult, ALU.subtract, ['ta', 'tb4'] + rr, ['WBs%d' % d])
            er, ei = ER(8)
            o = TW["Vt", d]
            k.tt('dve', ta, er, HB(Pm), ALU.mult, rr, ['ta'])
            k.tt('dve', tb4, ei, HB(Qm), ALU.mult, rr, ['tb4'])
            k.stt('dve', o, tb4, sgn1[:, 0:1], ta, ALU.mult, ALU.add, ['ta', 'tb4'] + rr, ['Vt%d' % d])
            er, ei = ER(16)
            o = TW["WC", d]
            k.tt('dve', ta, er, HB(CT1), ALU.mult, rr, ['ta'])
            k.tt('dve', tb4, ei, HB(CT2), ALU.mult, rr, ['tb4'])
            k.stt('dve', o, ta, sgn2[:, 0:1], tb4, ALU.mult, ALU.subtract, ['ta', 'tb4'] + rr, ['WC%d' % d])
            o = TW["WC2", d]
            k.tt('dve', ta, er, HB(CT2), ALU.mult, rr, ['ta'])
            k.tt('dve', tb4, ei, HB(CT1), ALU.mult, rr, ['tb4'])
            if d == 0:
                k.stt('dve', o, tb4, sgn1[:, 0:1], ta, ALU.mult, ALU.subtract, ['ta', 'tb4'] + rr, ['WC2%d' % d])
            else:
                k.stt('dve', o, tb4, sgn2[:, 0:1], ta, ALU.mult, ALU.add, ['ta', 'tb4'] + rr, ['WC2%d' % d])
        for gi in range(GB):
            g = gb * GB + gi
            wt = wtile[g % 2]
            wr = f'wtile{g % 2}'

            def V2(t):
                return t[:, gi, :, :].rearrange("p a b -> p (a b)")
            for d in range(2):
                k.mm(PS(0 + d, 128), V2(TW["WBt", d]), ident, True, True, ['WBt%d' % d, 'ident'], [PSR(0 + d)])
                k.cp('act', wt[:, d * 4 + 0, :], PS(0 + d, 128), [PSR(0 + d)], [wr])
                k.mm(PS(2 + d, 128), V2(TW["WBs", d]), ident, True, True, ['WBs%d' % d, 'ident'], [PSR(2 + d)])
                k.cp('act', wt[:, d * 4 + 1, :], PS(2 + d, 128), [PSR(2 + d)], [wr])
                k.cp('dve', wt[:, d * 4 + 2, :], V2(TW["WC", d]), ['WC%d' % d], [wr])
                k.cp('dve', wt[:, d * 4 + 3, :], V2(TW["WC2", d]), ['WC2%d' % d], [wr])
                k.mm(PS(4 + d, 128), V2(TW["Vt", d]), V2(TW["WC", d]), True, True, ['Vt%d' % d, 'WC%d' % d], [PSR(4 + d)])
            k.tt('dve', kt1, PS(4, 128), mkf, ALU.mult, [PSR(4), 'mkf'], ['kt1'])
            k.tt('dve', kt2, PS(5, 128), mkb, ALU.mult, [PSR(5), 'mkb'], ['kt2'])
            k.tt('dve', kt1, kt1, kt2, ALU.add, ['kt1', 'kt2'], ['kt1'])
            k.stt('dve', wt[:, 8, :], ident, dcol[:, g:g + 1], kt1, ALU.mult, ALU.add, ['kt1', 'ident', 'dcol'], [wr])
            k.dma('pool', ssmw[g], wt, wr, [wr], ['ssmw'])
    assert S1.cur <= DA.lo, (S1.cur, DA.lo)
    P.barrier_all()

    uid = [0]

    def XR(xr):
        return [f'{xr}.{c}' for c in range(8)]

    def ssq_step(xT, xr, sq, kc, n=512, bank_a=0):
        s_ = kc % 2
        if kc % 4 == 3:
            k.tt('dve', sq[s_][:, 0:n], xT[:, kc, 0:n], xT[:, kc, 0:n], ALU.mult, [f'{xr}.{kc}'], [f'sq{s_}'])
        else:
            k.act(sq[s_][:, 0:n], xT[:, kc, 0:n], AF.Square, [f'{xr}.{kc}'], [f'sq{s_}'])
        k.mm(PS(bank_a, n)[0:1, :], ones_col, sq[s_][:, 0:n], kc == 0, kc == 7, [f'sq{s_}', 'ones_col'], [PSR(bank_a)])

    def norm_to_hT(xT, xr, hT, hr, sq, rbc, rrow, n=512, bank_a=0, bank_b=1, fused=False):
        if not fused:
            for kc in range(8):
                ssq_step(xT, xr, sq, kc, n, bank_a)
        k.act(rrow[:, 0:n], PS(bank_a, n)[0:1, :], AF.Ln, [PSR(bank_a), 'epsc'], ['rrow'], scale=1.0 / D, bias=epsc[0:1, 0:1])
        k.act(rrow[:, 0:n], rrow[:, 0:n], AF.Exp, ['rrow'], ['rrow'], scale=-0.5)
        k.mm(PS(bank_b, n), ones_row, rrow[:, 0:n], True, True, ['rrow', 'ones_row'], [PSR(bank_b)])
        for kc in range(8):
            k.tt('dve', hT[:, kc, 0:n], xT[:, kc, 0:n], PS(bank_b, n), ALU.mult, [f'{xr}.{kc}', PSR(bank_b)], [f'{hr}{kc}'])

    def load_xT(xsrc, t0, xin, xT, xr, nblk=4):
        for b in range(nblk):
            s = b % len(xin)
            k.dma('sp', xin[s], xsrc[t0 + b * 128:t0 + (b + 1) * 128, :], f'xin{s}', [], [f'xin{s}'])
            for kc in range(8):
                k.mm(PS(kc)[:, b * 128:(b + 1) * 128], xin[s][:, kc * 128:(kc + 1) * 128], ident, True, True,
                     [f'xin{s}', 'ident'], [PSR(kc)])
        for kc in range(8):
            k.cp('act' if kc % 2 == 0 else 'dve', xT[:, kc, 0:nblk * 128], PS(kc, nblk * 128), [PSR(kc)], [f'{xr}.{kc}'])

    def ssm_phases(xsrc, L, LC, jobs=None, hi=None):
        nch = L // 8
        PA = Arena(nc, A0.cur, hi if hi is not None else A0.hi)
        U = PA.alloc("U", [128, 32, nch], BF16)
        AA = PA.sub()
        xin = [AA.alloc(f"xin{i}", [128, D], F32) for i in range(2)]
        xT = AA.alloc("xT", [128, 8, 512], F32)
        sq = [AA.alloc(f"sq{i}", [128, 512], BF16) for i in range(2)]
        rbc = AA.alloc("rbc", [128, 512], F32)
        rrow = AA.alloc("rrow", [1, 512], F32)
        hT1k = AA.alloc("hT1k", [128, 8, 1024], BF16)
        wa = AA.alloc("wa", [128, 8, 512], BF16)
        Zc = AA.alloc("Zc", [128, 32, 8, 16], BF16)
        k.dma('sp', wa, WS["w_in"][0], 'wa', ['WS_w_in'], ['wa'])
        for t1k in range(L // 1024):
            for half in range(2):
                t0 = t1k * 1024 + half * 512
                load_xT(xsrc, t0, xin, xT, 'xTa')
                norm_to_hT(xT, 'xTa', hT1k[:, :, half * 512:(half + 1) * 512], 'hT1k', sq, rbc, rrow)
            for tau in range(8):
                for kc in range(8):
                    k.mm(PS(tau), hT1k[:, kc, tau::8], wa[:, kc, :], kc == 0, kc == 7, [f'hT1k{kc}', 'wa'], [PSR(tau)])
                k.cp('act' if tau % 2 == 0 else 'dve', Zc[:, :, tau, :], PS(tau).rearrange("p (g h) -> p g h", h=16), [PSR(tau)], ['Zc'])
            for g in range(32):
                b = g // 4
                k.mm(PS(b)[:, (g % 4) * 128:(g % 4 + 1) * 128], Zc[:, g, :, :].rearrange("p a b -> p (a b)"), identb, True, True,
                     ['Zc', 'identb'], [PSR(b)])
                if g % 4 == 3:
                    k.cp('act' if b % 2 == 0 else 'dve', U[:, b * 4:(b + 1) * 4, t1k * 128:(t1k + 1) * 128],
                         PS(b).rearrange("p (g c) -> p g c", c=128), [PSR(b)], [f'U{b * 4 + q_}' for q_ in range(4)])
        P.barrier_all()
        BA = PA.sub()
        wset = [BA.alloc(f"wset{i}", [128, 9, 128], BF16) for i in range(2)]
        n1 = nch // 32
        tab = [BA.alloc(f"tab{i}", [128, 2, nch], F32) for i in range(2)]
        tw = BA.alloc("tw", [128, 2, nch], F32)
        tr = BA.alloc("tr", [128, 2, nch], F32)
        xt = [BA.alloc(f"xt{i}", [128, nch], F32) for i in range(2)]
        xt2 = BA.alloc("xt2", [128, nch], F32)
        St = BA.alloc("St", [128, nch], F32)
        P12 = [BA.alloc(f"P12{i}", [128, 2, nch], BF16) for i in range(2)]
        nb = (nch + 511) // 512
        units = [(g, d) for g in range(32) for d in range(2)]

        def wsel(g):
            return wset[g % 2], f'wset{g % 2}'

        def tabgen(g, d):
            gd = d * 32 + g
            w0 = tw[:, 0, :].rearrange("p (a b) -> p a b", b=32)
            k.ts('dve', w0, iota[:, :].unsqueeze(1).to_broadcast([128, n1, 32]), phi[:, gd:gd + 1], ALU.mult,
                 ['iota', 'phi'], ['tw'])
            k.stt('dve', w0, iota[:, 0:n1].unsqueeze(2).to_broadcast([128, n1, 32]), f1[:, gd:gd + 1], w0,
                  ALU.mult, ALU.add, ['iota', 'f1', 'tw'], ['tw'])
            k.ts('dve', tw[:, 1, :], tw[:, 0, :], 0.25, ALU.add, ['tw'], ['tw'])
            k.ts('dve', tr, tw, MAGIC, ALU.add, ['tw'], ['tr'], s2=MAGIC, op1=ALU.subtract)
            k.tt('dve', tr, tw, tr, ALU.subtract, ['tw', 'tr'], ['tr'])
            k.act(tab[d], tr, AF.Sin, ['tr'], [f'tab{d}'], scale=TWO_PI)

        def xmm(g, d):
            ws_, wsr = wsel(g)
            if d == 0:
                k.dma('sp', ws_, ssmw[g], wsr, ['ssmw'], [wsr])
            for bi in range(nb):
                cs = slice(bi * 512, min(nch, (bi + 1) * 512))
                n = cs.stop - cs.start
                k.mm(PS(bi, n), ws_[:, d * 4 + 0, :], U[:, g, cs], True, True, [wsr, f'U{g}'], [PSR(bi)])
                k.mm(PS(nb + bi, n), ws_[:, d * 4 + 1, :], U[:, g, cs], True, True, [wsr, f'U{g}'], [PSR(nb + bi)])

        def rot(g, d):
            tb_, tbr, x_, xr_ = tab[d], f'tab{d}', xt[d], f'xt{d}'
            for bi in range(nb):
                cs = slice(bi * 512, min(nch, (bi + 1) * 512))
                n = cs.stop - cs.start
                k.tt('dve', x_[:, cs], PS(bi, n), tb_[:, 1, cs], ALU.mult, [PSR(bi), tbr], [xr_])
                k.tt('dve', xt2[:, cs], PS(nb + bi, n), tb_[:, 0, cs], ALU.mult, [PSR(nb + bi), tbr], ['xt2'])
            k.tt('dve', x_, x_, xt2, ALU.add, [xr_, 'xt2'], [xr_])

        def scn(g, d):
            gd = d * 32 + g
            tb_, tbr, x_, xr_ = tab[d], f'tab{d}', xt[d], f'xt{d}'
            rb = rho8[:, gd:gd + 1].to_broadcast([128, nch])
            if d == 0:
                k.scan(St, rb, x_, [xr_, 'rho8'], ['St'])
            else:
                k.scan(St[:, ::-1], rb, x_[:, ::-1], [xr_, 'rho8'], ['St'])
            pp = P12[d]
            k.tt('dve', pp[:, 0, :], St, tb_[:, 1, :], ALU.mult, ['St', tbr], [f'P12{d}'])
            k.tt('dve', pp[:, 1, :], St, tb_[:, 0, :], ALU.mult, ['St', tbr], [f'P12{d}'])

        def ymm(g):
            ws_, wsr = wsel(g)
            for bi in range(nb):
                c0 = bi * 512
                c1 = min(nch, c0 + 512)
                yb = 4 + bi
                rdeps = [wsr, f'U{g}', 'P120', 'P121']
                k.mm(PS(yb, c1 - c0), ws_[:, 8, :], U[:, g, c0:c1], True, False, rdeps, [PSR(yb)])
                lo = max(c0, 1)
                for wi in (2, 3):
                    k.mm(psum[:, yb * 512 + (lo - c0):yb * 512 + (c1 - c0)], ws_[:, wi, :],
                         P12[0][:, wi - 2, lo - 1:c1 - 1], False, False, rdeps, [PSR(yb)])
                hi = min(c1, nch - 1)
                for wi in (6, 7):
                    k.mm(psum[:, yb * 512:yb * 512 + (hi - c0)], ws_[:, wi, :],
                         P12[1][:, wi - 6, c0 + 1:hi + 1], False, wi == 7, rdeps, [PSR(yb)])
                k.act(U[:, g, c0:c1], PS(yb, c1 - c0), AF.Gelu_apprx_tanh, [PSR(yb)], [f'U{g}'])

        tabgen(*units[0])
        for ui, (g, d) in enumerate(units):
            xmm(g, d)
            if ui + 1 < len(units):
                tabgen(*units[ui + 1])
            rot(g, d)
            scn(g, d)
            if d == 1:
                ymm(g)
            if jobs:
                jobs.pop(0)()
        while jobs:
            jobs.pop(0)()
        P.barrier_all()
        CA = PA.sub()
        Gc = CA.alloc("Gc", [128, 8, 512], BF16)
        gTt = [CA.alloc(f"gTt{i}", [128, 4, 1024], BF16) for i in range(2)]
        for t1k in range(LC // 1024):
            gt = gTt[t1k % 2]
            gtr = f'gTt{t1k % 2}'
            for g in range(32):
                b = g // 4
                k.mm(PS(b)[:, (g % 4) * 128:(g % 4 + 1) * 128], U[:, g, t1k * 128:(t1k + 1) * 128], identb, True, True,
                     [f'U{g}', 'identb'], [PSR(b)])
                if g % 4 == 3:
                    src = PS(b).rearrange("p (g t h) -> p g t h", g=4, t=8)
                    dst = Gc[:, :, b * 64:(b + 1) * 64].rearrange("p t (g h) -> p g t h", g=4)
                    k.cp('act' if b % 2 == 0 else 'dve', dst, src, [PSR(b)], ['Gc'])
            for j in range(4):
                for th in range(2):
                    b = j * 2 + th
                    for ti in range(4):
                        tau = th * 4 + ti
                        k.mm(PS(b)[:, ti * 128:(ti + 1) * 128], Gc[:, tau, j * 128:(j + 1) * 128], identb, True, True,
                             ['Gc', 'identb'], [PSR(b)])
                    src = PS(b).rearrange("p (t c) -> p t c", t=4)
                    dst = gt[:, j, :].rearrange("p (c t) -> p t c", t=8)[:, th * 4:(th + 1) * 4, :]
                    k.cp('act' if b % 2 == 0 else 'dve', dst, src, [PSR(b)], [gtr])
            k.dma('pool', gTs[:, :, t1k * 1024:(t1k + 1) * 1024].rearrange("j p t -> p j t"), gt, gtr, [gtr], ['gTs'])
        P.barrier_all()

    def main_seq(xsrc, ydst, L, OWN):
        MA = A0.sub()
        R = 5
        ring = [MA.alloc(f"ring{i}", [128, 4096], BF16) for i in range(R)]
        biasT = MA.alloc("biasT", [128, 16, 3, 128], BF16)
        xin = [MA.alloc(f"xin{i}", [128, D], F32) for i in range(4)]
        xouts = [xin[2], xin[3]]
        xb = [MA.alloc(f"xb{i}", [128, 8, 512], F32) for i in range(2)]
        hT = MA.alloc("hT", [128, 8, 512], BF16)
        sq = [MA.alloc(f"sq{i}", [128, 512], BF16) for i in range(2)]
        mst = MA.alloc("mst", [128, 3, 128], F32)
        rbc = None
        rrow = MA.alloc("rrow", [1, 512], F32)
        gT = MA.alloc("gT", [128, 4, 512], BF16)
        sgate = [MA.alloc(f"sgate{i}", [128, 512], F32) for i in range(2)]
        qT = [MA.alloc(f"qT{i}", [128, 8, 512], BF16) for i in range(2)]
        NR = 12
        kT = MA.alloc("kT", [128, 2, NR * 128], BF16)
        Vr = MA.alloc("Vr", [128, NR, 4, 65], BF16)
        stat = MA.alloc("stat", [128, 64], F32)
        GX = MA.sub()
        ubT = GX.alloc("ubT", [128, 4, 512], BF16)
        vgs = [GX.alloc(f"vg{i}", [128, 512], F32) for i in range(2)]
        vsqs = [GX.alloc(f"vsq{i}", [128, 512], F32) for i in range(2)]
        vns = [GX.alloc(f"vn{i}", [128, 512], BF16) for i in range(2)]
        ybT = GX.alloc("ybT", [128, 4, 512], BF16)
        sg = GX.alloc("sg", [128, 512], F32)
        yaT = GX.alloc("yaT", [128, 4, 512], BF16)
        GXq = MA.sub()
        qk = GXq.alloc("qk", [128, 1280], F32)
        qsq = GXq.alloc("qsq", [128, 1280], F32)
        qn = GXq.alloc("qn", [128, 1280], BF16)
        MA.cur = max(GX.cur, GXq.cur)
        GY = MA.sub()
        actT = GY.alloc("actT", [128, NJ, 512], BF16)
        GYa = MA.sub()
        PT = [GYa.alloc(f"PT{i}", [128, 3, 512], BF16) for i in range(2)]
        on = GYa.alloc("on", [128, 1024], BF16)
        oT = GYa.alloc("oT", [128, 8, 512], BF16)
        MA.cur = max(GY.cur, GYa.cur)
        L0N = ['ubT', 'vg0', 'vg1', 'vsq0', 'vsq1', 'vn0', 'vn1', 'ybT', 'sg', 'yaT']
        QKN = ['qk', 'qsq', 'qn']
        ATN = ['PT0', 'PT1', 'on', 'oT']
        NBseq = L // 128

        mstage = mst
        k.dma('sp', mstage, I["maskc"], 'mstd', [], ['mst'])
        bst = xb[0].rearrange("p a b -> p (a b)")[:, 0:2048].rearrange("p (h q) -> p h q", h=16)
        for r in range(3):
            k.dma('sp', bst, I["biasg"][:, :, r, :], 'bst', [], XR('xb0'))
            k.tt('dve', biasT[:, :, r, :], bst, mstage[:, r, :].unsqueeze(1).to_broadcast([128, 16, 128]), ALU.add,
                 XR('xb0') + ['mst'], ['biasT'])
        k.memset('dve', Vr[:, :, :, 64:65], 1.0, ['Vr'])

        ridx = [0]

        def wload(src, dreg, view):
            s = ridx[0] % R
            ridx[0] += 1
            tot = int(np.prod(src.shape[1:]))
            dst = ring[s][:, 0:tot]
            if view is not None:
                dst = dst.rearrange(view[0], **view[1])
            k.dma('sp', dst, src, f'ring{s}', [dreg], [f'ring{s}'])
            return dst, f'ring{s}'

        def wpiece(nm, pc, KC=8, ncol=512):
            src = WS[nm][pc]
            if ncol != 512:
                src = src[:, :, 0:ncol]
            return wload(src, 'WS_' + nm, ("p (c n) -> p c n", dict(n=ncol)))

        def ffn(l, xT, xr, n):
            P.alias(['actT'], ATN)
            norm_to_hT(xT, xr, hT, 'hT', sq, rbc, rrow, n=n, fused=True)
            jcount = 0
            for pc in range(6):
                ncol = 512 if pc < 5 else 256
                wg_, wgr = wpiece(f"wg{l}", pc, ncol=ncol)
                wu_, wur = wpiece(f"wu{l}", pc, ncol=ncol)
                for jj in range(ncol // 128):
                    j = pc * 4 + jj
                    ba, bb = (2, 3) if jcount % 2 == 0 else (4, 5)
                    s = jcount % 2
                    jcount += 1
                    for kc in range(8):
                        k.mm(PS(ba, n), wg_[:, kc, jj * 128:(jj + 1) * 128], hT[:, kc, 0:n], kc == 0, kc == 7, [wgr, f'hT{kc}'], [PSR(ba)])
                    for kc in range(8):
                        k.mm(PS(bb, n), wu_[:, kc, jj * 128:(jj + 1) * 128], hT[:, kc, 0:n], kc == 0, kc == 7, [wur, f'hT{kc}'], [PSR(bb)])
                    k.act(sgate[s][:, 0:n], PS(ba, n), AF.Silu, [PSR(ba)], [f'sgate{s}'])
                    k.tt('dve', actT[:, j, 0:n], sgate[s][:, 0:n], PS(bb, n), ALU.mult, [f'sgate{s}', PSR(bb)], ['actT'])
            for fc in range(8):
                wd_, wdr = wload(WS[f"wd{l}"][fc], f'WS_wd{l}', ("p (j n) -> p j n", dict(n=128)))
                bo = 6 + fc % 2
                for j in range(NJ):
                    k.mm(PS(bo, n), wd_[:, j, :], actT[:, j, 0:n], j == 0, j == NJ - 1, [wdr, 'actT'], [PSR(bo)])
                k.tt('dve', xT[:, fc, 0:n], xT[:, fc, 0:n], PS(bo, n), ALU.add, [f'{xr}.{fc}', PSR(bo)], [f'{xr}.{fc}'])
                if l == 0 and fc >= 1:
                    ssq_step(xT, xr, sq, fc - 1, n)
            if l == 0:
                ssq_step(xT, xr, sq, 7, n)

        def L0(i, xT, xr, n):
            t0 = i * 512
            nblk = n // 128
            P.alias(L0N, QKN)
            load_xT(xsrc, t0, xin, xT, xr, nblk=nblk)
            k.dma('sp', gT[:, :, 0:n], gTs[:, :, t0:t0 + n].rearrange("j p t -> p j t"), 'gT', ['gTs'], ['gT'])
            norm_to_hT(xT, xr, hT, 'hT', sq, rbc, rrow, n=n)
            w1, w1r = wpiece("w_in", 1)
            for jj in range(4):
                bk = 2 + jj % 2
                for kc in range(8):
                    k.mm(PS(bk, n), w1[:, kc, jj * 128:(jj + 1) * 128], hT[:, kc, 0:n], kc == 0, kc == 7, [w1r, f'hT{kc}'], [PSR(bk)])
                k.act(ubT[:, jj, 0:n], PS(bk, n), AF.Gelu_apprx_tanh, [PSR(bk)], ['ubT'])
            w2, w2r = wpiece("w_in", 2)
            for b in range(nblk):
                bk = 4 + b
                ts_ = slice(b * 128, (b + 1) * 128)
                for kc in range(8):
                    k.mm(PS(bk), hT[:, kc, ts_], w2[:, kc, :], kc == 0, kc == 7, [w2r, f'hT{kc}'], [PSR(bk)])

            def chainA(b):
                bk = 4 + b
                q_ = b % 2
                st_ = stat[:, 48 + 4 * b:52 + 4 * b]
                k.act(vgs[q_], PS(bk), AF.Gelu_apprx_tanh, [PSR(bk)], [f'vg{q_}'])
                k.act(vsqs[q_], vgs[q_], AF.Square, [f'vg{q_}'], [f'vsq{q_}'])
                k.reduce(st_, vsqs[q_].rearrange("p (h d) -> p h d", h=4), [f'vsq{q_}'], [f'vst{b}'])

            def chainB(b):
                q_ = b % 2
                st_ = stat[:, 48 + 4 * b:52 + 4 * b]
                k.act(st_, st_, AF.Ln, [f'vst{b}', 'epsc'], [f'vst{b}'], scale=1.0 / 128, bias=epsc[:, 0:1])
                k.act(st_, st_, AF.Exp, [f'vst{b}'], [f'vst{b}'], scale=-0.5)
                for h in range(4):
                    hs = slice(h * 128, (h + 1) * 128)
                    k.stt('dve', vns[q_][:, hs], vgs[q_][:, hs], st_[:, h:h + 1], gmn[:, hs], ALU.mult, ALU.mult,
                          [f'vg{q_}', f'vst{b}', 'gmn'], [f'vn{q_}'])

            def mixed(b):
                bm = 4 + b
                q_ = b % 2
                ts_ = slice(b * 128, (b + 1) * 128)
                for h in range(4):
                    hs = slice(h * 128, (h + 1) * 128)
                    k.mm(PS(bm)[:, hs], vns[q_][:, hs], wsT[:, h, :], True, True, [f'vn{q_}', 'wsT'], [PSR(bm)])
                v3 = vsqs[q_].rearrange("p (h d) -> p h d", h=4)
                k.tt('dve', v3, PS(bm).rearrange("p (h d) -> p h d", h=4), bsT, ALU.add, [PSR(bm), 'bsT'], [f'vsq{q_}'])
                k.tt('dve', ybT[:, :, ts_], v3, ubT[:, :, ts_], ALU.mult, [f'vsq{q_}', 'ubT'], ['ybT'])

            for b0 in range(0, nblk, 2):
                bs = [b for b in (b0, b0 + 1) if b < nblk]
                for b in bs:
                    chainA(b)
                for b in bs:
                    chainB(b)
                for b in bs:
                    mixed(b)
            wl, wlr = wload(WS["glu_w"][0], 'WS_glu_w', ("p (c n) -> p c n", dict(n=512)))
            for fo in range(4):
                bk = 2 + fo % 2
                for kc in range(4):
                    k.mm(PS(bk, n), wl[:, kc, fo * 128:(fo + 1) * 128], gT[:, kc, 0:n], kc == 0, kc == 3, [wlr, 'gT'], [PSR(bk)])
                k.act(sg[:, 0:n], PS(bk, n), AF.Sigmoid, [PSR(bk), 'glub'], ['sg'], bias=glub[:, fo:fo + 1])
                k.tt('dve', yaT[:, fo, 0:n], gT[:, fo, 0:n], sg[:, 0:n], ALU.mult, ['gT', 'sg'], ['yaT'])
            for pc in range(2):
                wo_, wor = wpiece("w_out", pc)
                for fci in range(4):
                    fc = pc * 4 + fci
                    bo = 6 + fc % 2
                    for kc in range(8):
                        rhs = yaT[:, kc, 0:n] if kc < 4 else ybT[:, kc - 4, 0:n]
                        k.mm(PS(bo, n), wo_[:, kc, fci * 128:(fci + 1) * 128], rhs, kc == 0, kc == 7, [wor, 'yaT', 'ybT'], [PSR(bo)])
                    k.tt('dve', xT[:, fc, 0:n], xT[:, fc, 0:n], PS(bo, n), ALU.add, [f'{xr}.{fc}', PSR(bo)], [f'{xr}.{fc}'])
                    if fc >= 1:
                        ssq_step(xT, xr, sq, fc - 1, n)
            ssq_step(xT, xr, sq, 7, n)
            if debug and n == 512:
                k.dma('pool', dbg_x1[i], xT, 'dbg', XR(xr), [])
            ffn(0, xT, xr, n)
            if debug and n == 512:
                k.dma('pool', dbg_x2[i], xT, 'dbg', XR(xr), [])

        def QKV(i, xT, xr, n, qTs, qTr):
            nblk = n // 128
            P.alias(QKN, L0N)
            norm_to_hT(xT, xr, hT, 'hT', sq, rbc, rrow, n=n, fused=True)
            wq = [wpiece("w_qkv", pc) for pc in range(3)]

            def qmm(b):
                ts_ = slice(b * 128, (b + 1) * 128)
                for pc in range(3):
                    for kc in range(8):
                        k.mm(PS(1 + pc), hT[:, kc, ts_], wq[pc][0][:, kc, :], kc == 0, kc == 7, [wq[pc][1], f'hT{kc}'], [PSR(1 + pc)])

            def qchain(b):
                B = i * 4 + b
                slot = B % NR
                k.cp('act', qk[:, 0:512], PS(1), [PSR(1)], ['qk'])
                k.cp('act', qk[:, 512:1024], PS(2), [PSR(2)], ['qk'])
                k.cp('act', qk[:, 1024:1280], PS(3)[:, 0:256], [PSR(3)], ['qk'])
                k.cp('act', Vr[:, slot, :, 0:64], PS(3)[:, 256:512].rearrange("p (v d) -> p v d", v=4), [PSR(3)], ['Vr'])
                k.act(qsq, qk, AF.Square, ['qk'], ['qsq'])
                k.reduce(stat[:, 8:28], qsq.rearrange("p (h d) -> p h d", d=64), ['qsq'], ['stat'])
                k.act(stat[:, 8:28], stat[:, 8:28], AF.Ln, ['stat', 'epsc'], ['stat'], scale=1.0 / 64, bias=epsc[:, 0:1])
                k.act(stat[:, 8:28], stat[:, 8:28], AF.Exp, ['stat'], ['stat'], scale=-0.5)
                for A in range(2):
                    i0_ = qk[:, A * 512:(A + 1) * 512].rearrange("p (b c d) -> p b c d", b=2, c=4)
                    i1_ = stat[:, 8 + A * 8:16 + A * 8].rearrange("p (b c) -> p b c", b=2).unsqueeze(3).to_broadcast([128, 2, 4, 64])
                    o_ = qn[:, A * 512:(A + 1) * 512].rearrange("p (c b d) -> p b c d", c=4, b=2)
                    k.tt('dve', o_, i0_, i1_, ALU.mult, ['qk', 'stat'], ['qn'])
                k.tt('dve', qn[:, 1024:1280].rearrange("p (h d) -> p h d", d=64), qk[:, 1024:1280].rearrange("p (h d) -> p h d", d=64),
                     stat[:, 24:28].unsqueeze(2).to_broadcast([128, 4, 64]), ALU.mult, ['qk', 'stat'], ['qn'])

            def qtr(b):
                B = i * 4 + b
                slot = B % NR
                ts_ = slice(b * 128, (b + 1) * 128)
                for half in range(2):
                    bk = 4 + half
                    for pi in range(4):
                        pr = half * 4 + pi
                        k.mm(PS(bk)[:, pi * 128:(pi + 1) * 128], qn[:, pr * 128:(pr + 1) * 128], identb, True, True, ['qn', 'identb'], [PSR(bk)])
                    k.cp('dve' if half == 0 else 'act', qTs[:, half * 4:(half + 1) * 4, ts_], PS(bk).rearrange("p (a t) -> p a t", a=4),
                         [PSR(bk)], [qTr])
                for kp in range(2):
                    k.mm(PS(6)[:, kp * 128:(kp + 1) * 128], qn[:, 1024 + kp * 128:1024 + (kp + 1) * 128], identb, True, True,
                         ['qn', 'identb'], [PSR(6)])
                k.ts('dve', kT[:, :, slot * 128:(slot + 1) * 128], PS(6)[:, 0:256].rearrange("p (a t) -> p a t", a=2), gqk[:, 0:1], ALU.mult,
                     [PSR(6), 'gqk'], ['kT'])

            qmm(0)
            for b in range(nblk):
                qchain(b)
                if b + 1 < nblk:
                    qmm(b + 1)
                qtr(b)

        def L1(i, xT, xr, qTs, qTr):
            P.alias(ATN, ['actT'])
            n = 512
            sbc = [0]

            def rs_of(qb):
                QB = i * 4 + qb
                return [r for r in range(3) if 0 <= QB - 1 + r < NBseq]

            def scores(qb, kv):
                QB = i * 4 + qb
                qs_ = slice(qb * 128, (qb + 1) * 128)
                hs = slice((kv % 2) * 64, (kv % 2) * 64 + 64)
                kp = kv // 2
                pt = PT[kv % 2]
                ptr = f'PT{kv % 2}'
                for r in rs_of(qb):
                    slot = (QB - 1 + r) % NR
                    sb = 1 + (sbc[0] % 3)
                    sbc[0] += 1
                    k.mm(PS(sb), kT[hs, kp, slot * 128:(slot + 1) * 128], qTs[hs, kp * 4:kp * 4 + 4, qs_], True, False,
                         ['kT', qTr], [PSR(sb)])
                    k.mm(PS(sb), identb, biasT[:, kv * 4:(kv + 1) * 4, r, :], False, True, ['identb', 'biasT'], [PSR(sb)])
                    k.act(pt[:, r, :], PS(sb), AF.Exp, [PSR(sb)], [ptr])

            def pv(qb, kv):
                QB = i * 4 + qb
                rs = rs_of(qb)
                pt = PT[kv % 2]
                ptr = f'PT{kv % 2}'
                for hh in range(4):
                    h = kv * 4 + hh
                    col = (h // 7) * 512 + (h % 7) * 65
                    for r in rs:
                        slot = (QB - 1 + r) % NR
                        k.mm(psum[:, 5 * 512 + col:5 * 512 + col + 65], pt[:, r, hh * 128:(hh + 1) * 128], Vr[:, slot, kv, :],
                             r == rs[0], r == rs[-1], [ptr, 'Vr'], [PSR(5 + h // 7)])

            def fin(qb):
                qs_ = slice(qb * 128, (qb + 1) * 128)
                for bg, (h0, nh) in enumerate([(0, 7), (7, 7), (14, 2)]):
                    ov = PS(5 + bg)[:, 0:nh * 65].rearrange("p (h e) -> p h e", e=65)
                    k.tt('dve', stat[:, 32 + h0:32 + h0 + nh].unsqueeze(2), ov[:, :, 64:65], esink[:, h0:h0 + nh].unsqueeze(2), ALU.add,
                         [PSR(5 + bg), 'esink'], ['stat2'])
                k.recip(stat[:, 32:48], stat[:, 32:48], ['stat2'], ['stat2'])
                for bg, (h0, nh) in enumerate([(0, 7), (7, 7), (14, 2)]):
                    ov = PS(5 + bg)[:, 0:nh * 65].rearrange("p (h e) -> p h e", e=65)
                    k.tt('dve', on[:, h0 * 64:(h0 + nh) * 64].rearrange("p (h d) -> p h d", d=64), ov[:, :, 0:64],
                         stat[:, 32 + h0:32 + h0 + nh].unsqueeze(2).to_broadcast([128, nh, 64]), ALU.mult, [PSR(5 + bg), 'stat2'], ['on'])
                for half in range(2):
                    bk = 0 if half == 0 else 4
                    for ci in range(4):
                        kc = half * 4 + ci
                        k.mm(PS(bk)[:, ci * 128:(ci + 1) * 128], on[:, kc * 128:(kc + 1) * 128], identb, True, True, ['on', 'identb'], [PSR(bk)])
                    k.cp('act' if half == 0 else 'dve', oT[:, half * 4:(half + 1) * 4, qs_], PS(bk).rearrange("p (a t) -> p a t", a=4),
                         [PSR(bk)], ['oT'])

            aunits = [(qb, kv) for qb in range(4) for kv in range(4)]
            scores(*aunits[0])
            for ui, (qb, kv) in enumerate(aunits):
                if ui + 1 < len(aunits):
                    scores(*aunits[ui + 1])
                pv(qb, kv)
                if kv == 3:
                    fin(qb)
            if debug:
                k.dma('pool', dbg_oT[i], oT, 'dbg', ['oT'], [])
            for pc in range(2):
                wo_, wor = wpiece("w_o", pc)
                for fci in range(4):
                    fc = pc * 4 + fci
                    bo = 6 + fc % 2
                    for kc in range(8):
                        k.mm(PS(bo, n), wo_[:, kc, fci * 128:(fci + 1) * 128], oT[:, kc, :], kc == 0, kc == 7, [wor, 'oT'], [PSR(bo)])
                    k.tt('dve', xT[:, fc, :], xT[:, fc, :], PS(bo, n), ALU.add, [f'{xr}.{fc}', PSR(bo)], [f'{xr}.{fc}'])
                    if fc >= 1:
                        ssq_step(xT, xr, sq, fc - 1, n)
            ssq_step(xT, xr, sq, 7, n)
            if debug:
                k.dma('pool', dbg_x3[i], xT, 'dbg', XR(xr), [])
            ffn(1, xT, xr, n)
            for b in range(4):
                ts_ = slice(b * 128, (b + 1) * 128)
                for kc in range(8):
                    bk = 1 + (b % 2) * 2 + kc // 4
                    k.mm(PS(bk)[:, (kc % 4) * 128:(kc % 4 + 1) * 128], xT[:, kc, ts_], ident, True, True, [f'{xr}.{kc}', 'ident'], [PSR(bk)])
                b0 = 1 + (b % 2) * 2
                xo, xor_ = xouts[b % 2], f'xin{2 + b % 2}'
                k.cp('act', xo[:, 0:512], PS(b0), [PSR(b0)], [xor_])
                k.cp('dve', xo[:, 512:1024], PS(b0 + 1), [PSR(b0 + 1)], [xor_])
                k.dma('pool', ydst[i * 512 + b * 128:i * 512 + (b + 1) * 128, :], xo, xor_, [xor_], [])

        nt = OWN // 512
        for i in range(nt):
            s = i % 2
            L0(i, xb[s], f'xb{s}', 512)
            QKV(i, xb[s], f'xb{s}', 512, qT[s], f'qT{s}')
            if i >= 1:
                L1(i - 1, xb[1 - s], f'xb{1 - s}', qT[1 - s], f'qT{1 - s}')
        if OWN < L:
            s = nt % 2
            L0(nt, xb[s], f'xb{s}', 128)
            QKV(nt, xb[s], f'xb{s}', 128, qT[s], f'qT{s}')
        s = (nt - 1) % 2
        L1(nt - 1, xb[s], f'xb{s}', qT[s], f'qT{s}')
        if debug:
            k.dma('pool', dbg_qT, qT[s], 'dbg', [f'qT{s}'], [])
            k.dma('pool', dbg_kT, kT, 'dbg', ['kT'], [])
            k.dma('pool', dbg_V, Vr, 'dbg', ['Vr'], [])
            k.dma('pool', dbg_bias, biasT, 'dbg', ['biasT'], [])
        P.barrier_all()

    ssm_phases(I["xs"], LS, LS, jobs=cast_jobs, hi=DA.lo)
    main_seq(I["xs"], ys, LS, LS)
    lc = ((OWNP + 128 + 1023) // 1024) * 1024 if OWNP < LP else LP
    ssm_phases(I["xp"], LP, min(lc, LP))
    main_seq(I["xp"], yp, LP, OWNP)
    block = stack.enter_context(nc.Block())
    P.emit(block)
    stack.close()
    return nc


def _rel_bucket_np(rel):
    half, max_exact = 16, 8
    ret = (rel > 0).astype(np.int32) * half
    n = np.abs(rel)
    nf = np.maximum(n, 1).astype(np.float32)
    large = max_exact + (np.log(nf / np.float32(max_exact)) / np.float32(np.log(128 / max_exact))
                         * np.float32(half - max_exact)).astype(np.int32)
    large = np.minimum(large, half - 1)
    return ret + np.where(n < max_exact, n, large)


def _consts():
    c = {}
    c["ident"] = np.eye(128, dtype=np.float32)
    tt = np.arange(128) // 16
    c["maskf"] = (tt[:, None] <= tt[None, :]).astype(np.float32)
    c["maskb"] = (tt[:, None] >= tt[None, :]).astype(np.float32)
    kexp = np.zeros((64, NSLOT), np.float32)
    for d in range(2):
        for tau in range(8):
            e = 7 - tau if d == 0 else tau
            kexp[d * 32:(d + 1) * 32, tau] = e
            kexp[d * 32:(d + 1) * 32, 8 + tau] = e - 8
            kexp[d * 32:(d + 1) * 32, 16 + tau] = tau + 1 if d == 0 else 8 - tau
    kexp[:, 24] = 1
    kexp[:, 25] = 8
    c["kexp"] = kexp.reshape(-1)
    c["iota32"] = np.arange(32, dtype=np.float32)
    c["sgn1"] = np.concatenate([-np.ones(64), np.ones(64)]).astype(np.float32)
    c["sgn2"] = -c["sgn1"]
    kk = np.arange(128)[:, None, None]
    r = np.arange(3)[None, :, None]
    qq = np.arange(128)[None, None, :]
    rel = (r - 1) * 128 + kk - qq
    c["maskc"] = np.where(np.abs(rel) <= 128, 0.0, -80.0).astype(np.float32)
    c["_rel"] = rel
    return c


def _core_inputs(inp, core, xs, xp, consts):
    rev = (core % 2 == 1)
    f = (lambda a: np.ascontiguousarray(a[::-1])) if rev else (lambda a: np.ascontiguousarray(a))
    m = {}
    m["xs"] = f(xs)
    m["xp"] = f(xp)
    m["norm_mix"] = inp["norm_mix"]
    m["norm_ffn"] = inp["norm_ffn"]
    m["w_in"] = inp["w_in_even"][0]
    m["glu_w"] = inp["glu_w"][0]
    m["glu_b"] = inp["glu_b"][0]
    m["w_out"] = inp["w_out_even"][0]
    m["w_qkv"] = inp["w_qkv"][0]
    m["w_o"] = inp["w_o"][0]
    for l in range(2):
        m[f"wg{l}"] = inp["ffn_w_gate"][l]
        m[f"wu{l}"] = inp["ffn_w_up"][l]
        m[f"wd{l}"] = inp["ffn_w_down"][l]
    dsw = (lambda a: np.ascontiguousarray(a[::-1])) if rev else (lambda a: a)
    m["lam_re"] = dsw(inp["ssm_lam_re"][0])
    m["lam_im"] = dsw(inp["ssm_lam_im"][0])
    m["log_dt"] = dsw(inp["ssm_log_dt"][0])
    m["b_re"] = dsw(inp["ssm_b_re"][0])
    m["b_im"] = dsw(inp["ssm_b_im"][0])
    m["c_re"] = dsw(inp["ssm_c_re"][0])
    m["c_im"] = dsw(inp["ssm_c_im"][0])
    m["ssm_d"] = inp["ssm_d"][0]
    m["gm_norm"] = inp["gm_norm"][0]
    ws, bs = inp["gm_w_s"][0], inp["gm_b_s"][0]
    if rev:
        ws, bs = ws[:, ::-1, ::-1], bs[:, ::-1]
    m["gm_w_s"] = np.ascontiguousarray(ws)
    m["gm_b_s"] = np.ascontiguousarray(bs)
    m["q_norm"] = inp["q_norm"][0]
    m["k_norm"] = inp["k_norm"][0]
    m["attn_sink"] = inp["attn_sink"][0]
    rel = consts["_rel"]
    bucket = _rel_bucket_np(-rel if rev else rel)
    bg = inp["rel_table"][bucket]
    m["biasg"] = np.ascontiguousarray(bg.transpose(0, 3, 1, 2))
    for kname in ["ident", "maskf", "maskb", "kexp", "iota32", "sgn1", "sgn2", "maskc"]:
        m[kname] = consts[kname]
    return {k_: np.ascontiguousarray(v, dtype=np.float32) for k_, v in m.items()}


_NC_CACHE = {}


def run_cores(inp, xs_list, xp_list, LS, LP, OWNP, ncores, debug=False):
    key = (LS, LP, OWNP, debug)
    if key not in _NC_CACHE:
        _NC_CACHE[key] = build(LS, LP, OWNP, debug=debug)
    nc = _NC_CACHE[key]
    consts = _consts()
    in_maps = [_core_inputs(inp, c, xs_list[c], xp_list[c], consts) for c in range(ncores)]
    res = run_bass_kernel_spmd(nc, in_maps, core_ids=list(range(ncores)))
    outs = []
    for c in range(ncores):
        r = res.results[c]
        ys_, yp_ = np.asarray(r["ys"]), np.asarray(r["yp"])
        if c % 2 == 1:
            ys_, yp_ = ys_[::-1], yp_[::-1]
        outs.append((ys_, yp_, r) if debug else (ys_, yp_))
    return outs


def kernel(**inputs):
    inp = {k_: np.asarray(v) for k_, v in inputs.items()}
    x_prompt, x_sample = inp["x_prompt"], inp["x_sample"]
    B, LP, _ = x_prompt.shape
    BS, LS, _ = x_sample.shape
    ncores = 8
    xs_list = [x_sample[c] for c in range(ncores)]
    xp_list = [x_prompt[c // 2] for c in range(ncores)]
    outs = run_cores(inp, xs_list, xp_list, LS, LP, LP // 2, ncores)
    y_sample = np.stack([outs[c][0] for c in range(ncores)], axis=0).astype(np.float32)
    y_prompt = np.zeros_like(x_prompt, dtype=np.float32)
    H = LP // 2
    for c in range(ncores):
        b = c // 2
        if c % 2 == 0:
            y_prompt[b, 0:H] = outs[c][1]
        else:
            y_prompt[b, H:LP] = outs[c][1]
    return (y_prompt, y_sample)
```

```python
import contextlib
import numpy as np
import ml_dtypes
import concourse.bass as bass
import concourse.mybir as mybir
from concourse.bass_utils import run_bass_kernel_spmd

F32 = mybir.dt.float32
BF16 = mybir.dt.bfloat16
AF = mybir.ActivationFunctionType
ALU = mybir.AluOpType
AX = mybir.AxisListType

D = 1024
FF = 2816
NJ = FF // 128
EPS = 1e-6
MAGIC = 12582912.0
TWO_PI = float(2 * np.pi)
ENGS = ['pe', 'act', 'dve', 'pool', 'sp']
NSLOT = 27


class Prog:
    def __init__(self, nc, stack):
        self.nc = nc
        self.stack = stack
        self.ops = {e: [] for e in ENGS}
        self.cnt = {e: 0 for e in ENGS}
        self.seen = {e: {} for e in ENGS}
        self.reg = {}
        self.dmasem = {}
        self.sems = {}

    def sem(self, key):
        if key not in self.sems:
            self.sems[key] = self.stack.enter_context(self.nc.semaphore("s_" + str(key)))
        return self.sems[key]

    def _deps(self, eng, reads, writes):
        need = {}

        def add(k, v):
            if v > need.get(k, 0):
                need[k] = v
        for r in reads:
            st = self.reg.get(r)
            if st and st['w']:
                add(*st['w'])
        for r in writes:
            st = self.reg.get(r)
            if st:
                if st['w']:
                    add(*st['w'])
                for k, v in st['r'].items():
                    add(k, v)
        seen = self.seen[eng]
        waits = []
        for k, v in need.items():
            if eng == 'pe' and k == 'pe':
                continue
            if k.startswith('d_'):
                v = self.dmasem[k]
            if seen.get(k, 0) < v:
                seen[k] = v
                waits.append((k, v))
        return waits

    def _commit(self, reads, writes, tok):
        k, v = tok
        for r in reads:
            st = self.reg.setdefault(r, {'w': None, 'r': {}})
            if st['r'].get(k, 0) < v:
                st['r'][k] = v
        for r in writes:
            self.reg[r] = {'w': tok, 'r': {}}

    def op(self, eng, fn, reads=(), writes=()):
        waits = self._deps(eng, reads, writes)
        self.cnt[eng] += 1
        tok = (eng, self.cnt[eng])
        self.ops[eng].append((waits, fn, (eng, 1)))
        self._commit(reads, writes, tok)

    def dma(self, eng, fn, semname, reads=(), writes=()):
        waits = self._deps(eng, reads, writes)
        key = 'd_' + semname
        c = self.dmasem.get(key, 0) + 16
        self.dmasem[key] = c
        self.ops[eng].append((waits, fn, (key, 16)))
        self._commit(reads, writes, (key, c))

    def alias(self, new, olds):
        merged = {}
        for o in olds:
            st = self.reg.get(o)
            if not st:
                continue
            if st['w']:
                k, v = st['w']
                merged[k] = max(merged.get(k, 0), v)
            for k, v in st['r'].items():
                merged[k] = max(merged.get(k, 0), v)
        for n in new:
            st = self.reg.get(n)
            m2 = dict(merged)
            if st:
                if st['w']:
                    k, v = st['w']
                    m2[k] = max(m2.get(k, 0), v)
                for k, v in st['r'].items():
                    m2[k] = max(m2.get(k, 0), v)
            self.reg[n] = {'w': None, 'r': m2}

    def barrier_all(self):
        for e in ENGS:
            waits = []
            for k, v in list(self.cnt.items()):
                if v > 0 and self.seen[e].get(k, 0) < v and not (e == 'pe' and k == 'pe'):
                    self.seen[e][k] = v
                    waits.append((k, v))
            for k, v in self.dmasem.items():
                if self.seen[e].get(k, 0) < v:
                    self.seen[e][k] = v
                    waits.append((k, v))
            if waits:
                self.ops[e].append((waits, None, None))
        self.reg = {}

    def emit(self, block):
        engobj = {'pe': 'tensor', 'act': 'scalar', 'dve': 'vector', 'pool': 'gpsimd', 'sp': 'sync'}
        for e in ENGS:
            self.sem(e)
        for k in self.dmasem:
            self.sem(k)
        for e in ENGS:
            ops = self.ops[e]

            def body(eng, ops=ops):
                for waits, fn, inc in ops:
                    for k, v in waits:
                        eng.wait_ge(self.sem(k), v)
                    if fn is not None:
                        ins = fn(eng)
                        ins.then_inc(self.sem(inc[0]), inc[1])
            getattr(block, engobj[e])(body)


class Arena:
    def __init__(self, nc, lo, hi):
        self.nc, self.lo, self.hi, self.cur = nc, lo, hi, lo
        self.n = 0

    def alloc(self, name, shape, dt):
        size = int(np.prod(shape[1:])) * (4 if dt == F32 else 2)
        off = (self.cur + 63) // 64 * 64
        assert off + size <= self.hi, f"SBUF arena overflow at {name}: need {off + size - self.hi} more bytes"
        self.cur = off + size
        self.n += 1
        return self.nc.alloc_sbuf_tensor_at(f"{name}_{self.n}_{off}", list(shape), dt, offset=off).ap()

    def sub(self):
        return Arena(self.nc, self.cur, self.hi)


class K:
    def __init__(self, P):
        self.P = P

    def tt(self, eng, out, a, b, op, r, w):
        self.P.op(eng, lambda e: e.tensor_tensor(out=out, in0=a, in1=b, op=op), r, w)

    def ts(self, eng, out, a, s1, op0, r, w, s2=None, op1=None):
        if op1 is None:
            self.P.op(eng, lambda e: e.tensor_scalar(out=out, in0=a, scalar1=s1, scalar2=None, op0=op0), r, w)
        else:
            self.P.op(eng, lambda e: e.tensor_scalar(out=out, in0=a, scalar1=s1, scalar2=s2, op0=op0, op1=op1), r, w)

    def stt(self, eng, out, a, scalar, b, op0, op1, r, w):
        self.P.op(eng, lambda e: e.scalar_tensor_tensor(out=out, in0=a, scalar=scalar, in1=b, op0=op0, op1=op1), r, w)

    def act(self, out, in_, func, r, w, scale=1.0, bias=None, accum=None):
        def f(e):
            kw = {}
            if bias is not None:
                kw['bias'] = bias
            if accum is not None:
                kw['accum_out'] = accum
            return e.activation(out=out, in_=in_, func=func, scale=scale, **kw)
        self.P.op('act', f, r, w)

    def cp(self, eng, out, in_, r, w):
        if eng == 'act':
            self.P.op('act', lambda e: e.activation(out=out, in_=in_, func=AF.Copy), r, w)
        else:
            self.P.op(eng, lambda e: e.tensor_copy(out=out, in_=in_), r, w)

    def mm(self, out, lhsT, rhs, start, stop, r, w):
        self.P.op('pe', lambda e: e.matmul(out, lhsT=lhsT, rhs=rhs, start=start, stop=stop), r, w)

    def dma(self, q, out, in_, sem, r, w, slow=False):
        if sem == 'c0':
            sem = 'c_' + w[0]
        if slow:
            self.P.dma(q, lambda e: e.dma_start(out=out, in_=in_, allow_slow_non_contiguous=True), sem, r, w)
        else:
            self.P.dma(q, lambda e: e.dma_start(out=out, in_=in_), sem, r, w)

    def memset(self, eng, out, val, w):
        self.P.op(eng, lambda e: e.memset(out, val), (), w)

    def recip(self, out, in_, r, w):
        self.P.op('dve', lambda e: e.reciprocal(out=out, in_=in_), r, w)

    def reduce(self, out, in_, r, w):
        self.P.op('dve', lambda e: e.tensor_reduce(out=out, in_=in_, axis=AX.X, op=ALU.add), r, w)

    def scan(self, out, d0, d1, r, w):
        self.P.op('dve', lambda e: e.tensor_tensor_scan(out=out, data0=d0, data1=d1, initial=0.0,
                                                       op0=ALU.mult, op1=ALU.add), r, w)


WPIECES = [
    ("w_in", 1024, 1536, "nm0"), ("glu_w", 512, 512, None), ("w_out", 1024, 1024, None),
    ("w_qkv", 1024, 1536, "nm1"), ("w_o", 1024, 1024, None),
    ("wg0", 1024, FF, "nf0"), ("wu0", 1024, FF, "nf0"), ("wg1", 1024, FF, "nf1"), ("wu1", 1024, FF, "nf1"),
]


def build(LS, LP, OWNP, debug=False):
    assert LS % 1024 == 0 and LP % 1024 == 0 and OWNP % 512 == 0
    nc = bass.Bass("TRN2", target_bir_lowering=False)
    stack = contextlib.ExitStack()
    P = Prog(nc, stack)
    k = K(P)

    def din(name, shape):
        return nc.dram_tensor(name, list(shape), F32, kind="ExternalInput").ap()

    def dscr(name, shape, dt=BF16):
        return nc.dram_tensor(name, list(shape), dt).ap()

    I = {}
    for nm, shp in [("xs", [LS, D]), ("xp", [LP, D]), ("norm_mix", [2, D]), ("norm_ffn", [2, D]),
                    ("w_in", [D, 1536]), ("glu_w", [512, 512]), ("glu_b", [512]), ("w_out", [D, D]),
                    ("w_qkv", [D, 1536]), ("w_o", [D, D]),
                    ("wg0", [D, FF]), ("wu0", [D, FF]), ("wd0", [FF, D]),
                    ("wg1", [D, FF]), ("wu1", [D, FF]), ("wd1", [FF, D]),
                    ("lam_re", [2, 32, 64]), ("lam_im", [2, 32, 64]), ("log_dt", [2, 32]),
                    ("b_re", [2, 32, 64, 16]), ("b_im", [2, 32, 64, 16]),
                    ("c_re", [2, 32, 16, 64]), ("c_im", [2, 32, 16, 64]), ("ssm_d", [512]),
                    ("gm_norm", [4, 128]), ("gm_w_s", [4, 128, 128]), ("gm_b_s", [4, 128]),
                    ("q_norm", [64]), ("k_norm", [64]), ("attn_sink", [16]),
                    ("biasg", [128, 16, 3, 128]), ("maskc", [128, 3, 128]),
                    ("ident", [128, 128]), ("maskf", [128, 128]), ("maskb", [128, 128]),
                    ("kexp", [64 * NSLOT]), ("iota32", [32]), ("sgn1", [128]), ("sgn2", [128])]:
        I[nm] = din(nm, shp)
    ys = nc.dram_tensor("ys", [LS, D], F32, kind="ExternalOutput").ap()
    yp = nc.dram_tensor("yp", [OWNP, D], F32, kind="ExternalOutput").ap()
    dbg = {}

    WS = {}
    for nm, Kd, N, g in WPIECES:
        WS[nm] = dscr("s_" + nm, [(N + 511) // 512, 128, Kd // 128, 512])
    WS["wd0"] = dscr("s_wd0", [8, 128, NJ, 128])
    WS["wd1"] = dscr("s_wd1", [8, 128, NJ, 128])
    ssmw = dscr("s_ssmw", [32, 128, 9, 128])
    Lmax = max(LS, LP)
    if debug:
        gTs = nc.dram_tensor("dbg_gT", [4, 128, Lmax], BF16, kind="ExternalOutput").ap()
        dbg_x2 = nc.dram_tensor("dbg_x2", [Lmax // 512, 128, 8, 512], F32, kind="ExternalOutput").ap()
        dbg_x3 = nc.dram_tensor("dbg_x3", [Lmax // 512, 128, 8, 512], F32, kind="ExternalOutput").ap()
        dbg_oT = nc.dram_tensor("dbg_oT", [Lmax // 512, 128, 8, 512], BF16, kind="ExternalOutput").ap()
        dbg_qT = nc.dram_tensor("dbg_qT", [128, 8, 512], BF16, kind="ExternalOutput").ap()
        dbg_kT = nc.dram_tensor("dbg_kT", [128, 2, 12 * 128], BF16, kind="ExternalOutput").ap()
        dbg_V = nc.dram_tensor("dbg_V", [128, 12, 4, 65], BF16, kind="ExternalOutput").ap()
        dbg_bias = nc.dram_tensor("dbg_bias", [128, 16, 3, 128], BF16, kind="ExternalOutput").ap()
        dbg_x1 = nc.dram_tensor("dbg_x1", [Lmax // 512, 128, 8, 512], F32, kind="ExternalOutput").ap()
    else:
        gTs = dscr("s_gT", [4, 128, Lmax])

    base = (nc.sbuf_base + 63) // 64 * 64
    top = nc.sbuf_top // 64 * 64
    A0 = Arena(nc, base, top)
    psum = nc.alloc_psum_tensor("ps", [128, 4096], F32).ap()

    def PS(b, n=512):
        return psum[:, b * 512:b * 512 + n]

    def PSR(b):
        return f"ps{b}"

    ident = A0.alloc("ident", [128, 128], F32)
    identb = A0.alloc("identb", [128, 128], BF16)
    ones_col = A0.alloc("ones_col", [128, 1], BF16)
    ones_row = A0.alloc("ones_row", [1, 128], F32)
    ones_mat = A0.alloc("ones_mat", [128, 128], BF16)
    rho8 = A0.alloc("rho8", [128, 64], F32)
    phi = A0.alloc("phi", [128, 64], F32)
    f1 = A0.alloc("f1", [128, 64], F32)
    iota = A0.alloc("iota", [128, 32], F32)
    glub = A0.alloc("glub", [128, 4], F32)
    gmn = A0.alloc("gmn", [128, 512], F32)
    bsT = A0.alloc("bsT", [128, 4, 128], F32)
    wsT = A0.alloc("wsT", [128, 4, 128], BF16)
    esink = A0.alloc("esink", [128, 16], F32)
    gqk = A0.alloc("gqk", [128, 1], F32)
    sgn1 = A0.alloc("sgn1", [128, 1], F32)
    sgn2 = A0.alloc("sgn2", [128, 1], F32)
    epsc = A0.alloc("epsc", [128, 1], F32)
    ccol = A0.alloc("ccol", [128, 3], F32)

    k.dma('sp', ident, I["ident"], 'c0', [], ['ident'])
    k.cp('dve', identb, ident, ['ident'], ['identb'])
    k.memset('dve', ones_col, 1.0, ['ones_col'])
    k.memset('dve', ones_row, 1.0, ['ones_row'])
    k.memset('dve', ones_mat, 1.0, ['ones_mat'])
    k.memset('dve', epsc, EPS, ['epsc'])
    k.memset('dve', ccol[:, 0:1], MAGIC, ['ccol'])
    k.memset('dve', ccol[:, 1:2], -MAGIC, ['ccol'])
    k.memset('dve', ccol[:, 2:3], 0.25, ['ccol'])
    k.dma('sp', iota, I["iota32"].partition_broadcast(128), 'c0', [], ['iota'])
    k.dma('sp', sgn1, I["sgn1"].rearrange("(p o) -> p o", o=1), 'c0', [], ['sgn1'])
    k.dma('sp', sgn2, I["sgn2"].rearrange("(p o) -> p o", o=1), 'c0', [], ['sgn2'])
    k.dma('sp', glub, I["glu_b"].rearrange("(c p) -> p c", p=128), 'c0', [], ['glub'], slow=True)
    k.dma('sp', gmn, I["gm_norm"].rearrange("h d -> (h d)").partition_broadcast(128), 'c0', [], ['gmn'])
    k.dma('sp', bsT, I["gm_b_s"].rearrange("h i -> (h i)").partition_broadcast(128), 'c0', [], ['bsT'])
    k.dma('sp', esink, I["attn_sink"].partition_broadcast(128), 'c0', [], ['esink'])
    k.act(esink, esink, AF.Exp, ['esink'], ['esink'])

    gains = A0.alloc("gains", [128, 4, 8], F32)
    S1 = A0.sub()
    tq = S1.alloc("tq", [128, 2], F32)
    for half in range(2):
        k.dma('sp', tq[half * 64:(half + 1) * 64, 0:1], I["q_norm"].rearrange("(p o) -> p o", o=1), 'c0', [], ['tq'])
        k.dma('sp', tq[half * 64:(half + 1) * 64, 1:2], I["k_norm"].rearrange("(p o) -> p o", o=1), 'c0', [], ['tq'])
    k.stt('dve', gqk, tq[:, 0:1], 0.125, tq[:, 1:2], ALU.mult, ALU.mult, ['tq'], ['gqk'])
    wst = S1.alloc("wst", [128, 4, 128], F32)
    k.dma('sp', wst, I["gm_w_s"].rearrange("h i j -> i h j"), 'c0', [], ['wst'])
    for h in range(4):
        k.mm(PS(h, 128), wst[:, h, :], ident, True, True, ['wst', 'ident'], [PSR(h)])
        k.cp('dve', wsT[:, h, :], PS(h, 128), [PSR(h)], ['wsT'])

    for gi, (src, row) in enumerate([("norm_mix", 0), ("norm_mix", 1), ("norm_ffn", 0), ("norm_ffn", 1)]):
        k.dma('sp', gains[:, gi, :], I[src][row].rearrange("(c p) -> p c", p=128), 'c0', [], ['gains'], slow=True)
    gidx = {"nm0": 0, "nm1": 1, "nf0": 2, "nf1": 3}
    top_hi = top
    DEF_BYTES = 2 * (8 * 512 * 4) + 2 * (8 * 512 * 2) + 256
    DA = Arena(nc, (top_hi - DEF_BYTES) // 64 * 64, top_hi)
    stg = [DA.alloc(f"stg{i}", [128, 8, 512], F32) for i in range(2)]
    stb = [DA.alloc(f"stb{i}", [128, 8, 512], BF16) for i in range(2)]
    cnt = [0]
    cast_jobs = []

    def mk_piece_job(nm, Kd, N, g, pc, engs):
        def job():
            KC = Kd // 128
            src = I[nm].rearrange("(c p) n -> p c n", p=128)
            n0 = pc * 512
            nn = min(512, N - n0)
            s_ = cnt[0] % 2
            cnt[0] += 1
            k.dma('sp', stg[s_][:, 0:KC, 0:nn], src[:, :, n0:n0 + nn], f'stg{s_}', [], [f'stg{s_}'])
            for kc in range(KC):
                eng = engs[(cnt[0] + kc) % len(engs)]
                if g is None:
                    k.cp(eng, stb[s_][:, kc, 0:nn], stg[s_][:, kc, 0:nn], [f'stg{s_}'], [f'stb{s_}'])
                elif eng == 'act':
                    k.act(stb[s_][:, kc, 0:nn], stg[s_][:, kc, 0:nn], AF.Copy, [f'stg{s_}', 'gains'], [f'stb{s_}'],
                          scale=gains[:, gidx[g], kc:kc + 1])
                else:
                    k.ts(eng, stb[s_][:, kc, 0:nn], stg[s_][:, kc, 0:nn], gains[:, gidx[g], kc:kc + 1], ALU.mult,
                         [f'stg{s_}', 'gains'], [f'stb{s_}'])
            k.dma('pool', WS[nm][pc][:, :, 0:nn], stb[s_][:, 0:KC, 0:nn], f'stb{s_}', [f'stb{s_}'], ['WS_' + nm])
        return job

    def mk_wd_job(l, fc, jh, engs):
        def job():
            src = I[f"wd{l}"].rearrange("(j p) n -> p j n", p=128)
            s_ = cnt[0] % 2
            cnt[0] += 1
            stgv = stg[s_].rearrange("p a b -> p (a b)")[:, 0:11 * 128].rearrange("p (j n) -> p j n", n=128)
            stbv = stb[s_].rearrange("p a b -> p (a b)")[:, 0:11 * 128].rearrange("p (j n) -> p j n", n=128)
            k.dma('sp', stgv, src[:, jh * 11:(jh + 1) * 11, fc * 128:(fc + 1) * 128], f'stg{s_}', [], [f'stg{s_}'])
            k.cp(engs[cnt[0] % len(engs)], stbv, stgv, [f'stg{s_}'], [f'stb{s_}'])
            k.dma('pool', WS[f"wd{l}"][fc][:, jh * 11:(jh + 1) * 11, :], stbv, f'stb{s_}', [f'stb{s_}'], [f'WS_wd{l}'])
        return job

    for nm, Kd, N, g in WPIECES:
        for pc in range((N + 511) // 512):
            if nm == "w_in":
                mk_piece_job(nm, Kd, N, g, pc, ['dve', 'act'])()
            else:
                cast_jobs.append(mk_piece_job(nm, Kd, N, g, pc, ['act']))
    for l in range(2):
        for fc in range(8):
            for jh in range(2):
                cast_jobs.append(mk_wd_job(l, fc, jh, ['act']))

    def bc(ap, shape):
        return ap.to_broadcast(shape)

    kx = S1.alloc("kx", [128, 64, NSLOT], F32)
    k.dma('sp', kx, I["kexp"].partition_broadcast(128), 'c0', [], ['kx'])
    lr = S1.alloc("lr", [128, 64], F32)
    li = S1.alloc("li", [128, 64], F32)
    dtb = S1.alloc("dtb", [128, 64], F32)
    for half in range(2):
        k.dma('sp', lr[half * 64:(half + 1) * 64, :], I["lam_re"].rearrange("d g p -> p (d g)"), 'c0', [], ['lr'], slow=True)
        k.dma('sp', li[half * 64:(half + 1) * 64, :], I["lam_im"].rearrange("d g p -> p (d g)"), 'c0', [], ['li'], slow=True)
    k.dma('sp', dtb, I["log_dt"].rearrange("d g -> (d g)").partition_broadcast(128), 'c0', [], ['dtb'])
    k.act(dtb, dtb, AF.Exp, ['dtb'], ['dtb'])
    lrdt = S1.alloc("lrdt", [128, 64], F32)
    lidt = S1.alloc("lidt", [128, 64], F32)
    k.tt('dve', lrdt, lr, dtb, ALU.mult, ['lr', 'dtb'], ['lrdt'])
    k.stt('dve', lidt, li, 1.0 / TWO_PI, dtb, ALU.mult, ALU.mult, ['li', 'dtb'], ['lidt'])
    MAG = S1.alloc("MAG", [128, 64, NSLOT], F32)
    Wt = S1.alloc("Wt", [128, 64, NSLOT], F32)
    Rt = S1.alloc("Rt", [128, 64, NSLOT], F32)
    SINT = S1.alloc("SINT", [128, 64, NSLOT], F32)
    COST = S1.alloc("COST", [128, 64, NSLOT], F32)
    sh3 = [128, 64, NSLOT]
    k.tt('dve', MAG, kx, bc(lrdt.unsqueeze(2), sh3), ALU.mult, ['kx', 'lrdt'], ['MAG'])
    k.act(MAG, MAG, AF.Exp, ['MAG'], ['MAG'])
    k.tt('dve', Wt, kx, bc(lidt.unsqueeze(2), sh3), ALU.mult, ['kx', 'lidt'], ['Wt'])
    k.ts('dve', Rt, Wt, MAGIC, ALU.add, ['Wt'], ['Rt'], s2=MAGIC, op1=ALU.subtract)
    k.tt('dve', SINT, Wt, Rt, ALU.subtract, ['Wt', 'Rt'], ['SINT'])
    k.cp('dve', phi, SINT[:, :, 25], ['SINT'], ['phi'])
    k.ts('dve', f1, phi, 32.0, ALU.mult, ['phi'], ['f1'])
    f1r = S1.alloc("f1r", [128, 64], F32)
    k.ts('dve', f1r, f1, MAGIC, ALU.add, ['f1'], ['f1r'], s2=MAGIC, op1=ALU.subtract)
    k.tt('dve', f1, f1, f1r, ALU.subtract, ['f1', 'f1r'], ['f1'])
    k.act(SINT, SINT, AF.Sin, ['SINT'], ['SINT'], scale=TWO_PI)
    k.ts('dve', Wt, Wt, 0.25, ALU.add, ['Wt'], ['Wt'])
    k.ts('dve', Rt, Wt, MAGIC, ALU.add, ['Wt'], ['Rt'], s2=MAGIC, op1=ALU.subtract)
    k.tt('dve', COST, Wt, Rt, ALU.subtract, ['Wt', 'Rt'], ['COST'])
    k.act(COST, COST, AF.Sin, ['COST'], ['COST'], scale=TWO_PI)
    AR, AI = COST, SINT
    k.tt('dve', AR, MAG, COST, ALU.mult, ['MAG', 'COST'], ['COST'])
    k.tt('dve', AI, MAG, SINT, ALU.mult, ['MAG', 'SINT'], ['SINT'])
    k.cp('dve', rho8, MAG[:, :, 25], ['MAG'], ['rho8'])
    den = S1.alloc("den", [128, 64], F32)
    t1 = S1.alloc("t1", [128, 64], F32)
    t2 = S1.alloc("t2", [128, 64], F32)
    arm1 = S1.alloc("arm1", [128, 64], F32)
    zr = S1.alloc("zr", [128, 64], F32)
    zi = S1.alloc("zi", [128, 64], F32)
    k.tt('dve', den, lr, lr, ALU.mult, ['lr'], ['den'])
    k.tt('dve', t1, li, li, ALU.mult, ['li'], ['t1'])
    k.tt('dve', den, den, t1, ALU.add, ['den', 't1'], ['den'])
    k.recip(den, den, ['den'], ['den'])
    k.ts('dve', arm1, AR[:, :, 24], -1.0, ALU.add, ['COST'], ['arm1'])
    k.tt('dve', t1, arm1, lr, ALU.mult, ['arm1', 'lr'], ['t1'])
    k.tt('dve', t2, AI[:, :, 24], li, ALU.mult, ['SINT', 'li'], ['t2'])
    k.tt('dve', t1, t1, t2, ALU.add, ['t1', 't2'], ['t1'])
    k.tt('dve', zr, t1, den, ALU.mult, ['t1', 'den'], ['zr'])
    k.tt('dve', t1, AI[:, :, 24], lr, ALU.mult, ['SINT', 'lr'], ['t1'])
    k.tt('dve', t2, arm1, li, ALU.mult, ['arm1', 'li'], ['t2'])
    k.tt('dve', t1, t1, t2, ALU.subtract, ['t1', 't2'], ['t1'])
    k.tt('dve', zi, t1, den, ALU.mult, ['t1', 'den'], ['zi'])
    P0 = S1.alloc("P0", [128, 64, 16], F32)
    Q0 = S1.alloc("Q0", [128, 64, 16], F32)
    Pm = S1.alloc("Pm", [128, 64, 16], F32)
    Qm = S1.alloc("Qm", [128, 64, 16], F32)
    tb = S1.alloc("tb", [128, 64, 16], F32)
    bre = I["b_re"].rearrange("d g p h -> p (d g) h")
    bim = I["b_im"].rearrange("d g p h -> p (d g) h")
    k.dma('sp', P0[0:64], bre, 'c0', [], ['P0'])
    k.dma('sp', P0[64:128], bim, 'c0', [], ['P0'])
    k.dma('sp', Q0[0:64], bim, 'c0', [], ['Q0'])
    k.dma('sp', Q0[64:128], bre, 'c0', [], ['Q0'])
    sh = [128, 64, 16]
    zrb, zib = bc(zr.unsqueeze(2), sh), bc(zi.unsqueeze(2), sh)
    k.tt('dve', tb, Q0, zib, ALU.mult, ['Q0', 'zi'], ['tb'])
    k.tt('dve', Pm, P0, zrb, ALU.mult, ['P0', 'zr'], ['Pm'])
    k.stt('dve', Pm, tb, sgn1[:, 0:1], Pm, ALU.mult, ALU.add, ['tb', 'Pm', 'sgn1'], ['Pm'])
    k.tt('dve', tb, P0, zib, ALU.mult, ['P0', 'zi'], ['tb'])
    k.tt('dve', Qm, Q0, zrb, ALU.mult, ['Q0', 'zr'], ['Qm'])
    k.stt('dve', Qm, tb, sgn2[:, 0:1], Qm, ALU.mult, ALU.add, ['tb', 'Qm', 'sgn2'], ['Qm'])
    CT1 = S1.alloc("CT1", [128, 64, 16], F32)
    CT2 = S1.alloc("CT2", [128, 64, 16], F32)
    Cst = S1.alloc("Cst", [128, 8, 2, 64], F32)
    cre = I["c_re"].rearrange("d (gq g8) h p -> (g8 h) (d gq) p", g8=8)
    cim = I["c_im"].rearrange("d (gq g8) h p -> (g8 h) (d gq) p", g8=8)
    for which, CT in enumerate([CT1, CT2]):
        a_, b_ = (cre, cim) if which == 0 else (cim, cre)
        k.dma('sp', Cst[:, :, 0, :], a_, 'c0', [], ['Cst'])
        k.dma('sp', Cst[:, :, 1, :], b_, 'c0', [], ['Cst'])
        for blk in range(8):
            k.mm(PS(blk, 128), Cst[:, blk, :, :].rearrange("p a b -> p (a b)"), ident, True, True,
                 ['Cst', 'ident'], [PSR(blk)])
            k.cp('dve', CT[:, blk * 8:(blk + 1) * 8, :].rearrange("p a b -> p (a b)"), PS(blk, 128),
                 [PSR(blk)], ['CT%d' % which])
    dcol = S1.alloc("dcol", [128, 32], F32)
    for tau in range(8):
        k.dma('sp', dcol[tau * 16:(tau + 1) * 16, :], I["ssm_d"].rearrange("(g h) -> h g", h=16), 'c0', [], ['dcol'], slow=True)
    mkf = S1.alloc("mkf", [128, 128], F32)
    mkb = S1.alloc("mkb", [128, 128], F32)
    k.dma('sp', mkf, I["maskf"], 'c0', [], ['mkf'])
    k.dma('sp', mkb, I["maskb"], 'c0', [], ['mkb'])

    GB = 8
    sh4 = [128, GB, 8, 16]
    TW = {}
    for nm in ["WBt", "WBs", "Vt", "WC", "WC2"]:
        for d in range(2):
            TW[nm, d] = S1.alloc(f"{nm}{d}", sh4, F32)
    ta = S1.alloc("ta", sh4, F32)
    tb4 = S1.alloc("tb4", sh4, F32)
    wtile = [S1.alloc(f"wtile{i}", [128, 9, 128], BF16) for i in range(2)]
    kt1 = S1.alloc("kt1", [128, 128], F32)
    kt2 = S1.alloc("kt2", [128, 128], F32)
    for gb in range(32 // GB):
        for d in range(2):
            gs = slice(d * 32 + gb * GB, d * 32 + (gb + 1) * GB)

            def ER(s0):
                return bc(AR[:, gs, s0:s0 + 8].unsqueeze(3), sh4), bc(AI[:, gs, s0:s0 + 8].unsqueeze(3), sh4)

            def HB(t):
                return bc(t[:, gs, :].unsqueeze(2), sh4)
            rr = ['COST', 'SINT', 'Pm', 'Qm', 'CT0', 'CT1', 'sgn1', 'sgn2']
            er, ei = ER(0)
            o = TW["WBt", d]
            k.tt('dve', ta, er, HB(Pm), ALU.mult, rr, ['ta'])
            k.tt('dve', tb4, ei, HB(Qm), ALU.mult, rr, ['tb4'])
            k.stt('dve', o, tb4, sgn1[:, 0:1], ta, ALU.mult, ALU.add, ['ta', 'tb4'] + rr, ['WBt%d' % d])
            o = TW["WBs", d]
            k.tt('dve', ta, er, HB(Qm), ALU.mult, rr, ['ta'])
            k.tt('dve', tb4, ei, HB(Pm), ALU.mult, rr, ['tb4'])
            if d == 0:
                k.stt('dve', o, ta, sgn2[:, 0:1], tb4, ALU.mult, ALU.add, ['ta', 'tb4'] + rr, ['WBs%d' % d])
            else:
                k.stt('dve', o, ta, sgn1[:, 0:1], tb4, ALU.mult, ALU.subtract, ['ta', 'tb4'] + rr, ['WBs%d' % d])
            er, ei = ER(8)
            o = TW["Vt", d]
            k.tt('dve', ta, er, HB(Pm), ALU.mult, rr, ['ta'])
            k.tt('dve', tb4, ei, HB(Qm), ALU.mult, rr, ['tb4'])
            k.stt('dve', o, tb4, sgn1[:, 0:1], ta, ALU.mult, ALU.add, ['ta', 'tb4'] + rr, ['Vt%d' % d])
            er, ei = ER(16)
            o = TW["WC", d]
            k.tt('dve', ta, er, HB(CT1), ALU.mult, rr, ['ta'])
            k.tt('dve', tb4, ei, HB(CT2), ALU.mult, rr, ['tb4'])
            k.stt('dve', o, ta, sgn2[:, 0:1], tb4, ALU.mult, ALU.subtract, ['ta', 'tb4'] + rr, ['WC%d' % d])
            o = TW["WC2", d]
            k.tt('dve', ta, er, HB(CT2), ALU.mult, rr, ['ta'])
            k.tt('dve', tb4, ei, HB(CT1), ALU.mult, rr, ['tb4'])
            if d == 0:
                k.stt('dve', o, tb4, sgn1[:, 0:1], ta, ALU.mult, ALU.subtract, ['ta', 'tb4'] + rr, ['WC2%d' % d])
            else:
                k.stt('dve', o, tb4, sgn2[:, 0:1], ta, ALU.mult, ALU.add, ['ta', 'tb4'] + rr, ['WC2%d' % d])
        for gi in range(GB):
            g = gb * GB + gi
            wt = wtile[g % 2]
            wr = f'wtile{g % 2}'

            def V2(t):
                return t[:, gi, :, :].rearrange("p a b -> p (a b)")
            for d in range(2):
                k.mm(PS(0 + d, 128), V2(TW["WBt", d]), ident, True, True, ['WBt%d' % d, 'ident'], [PSR(0 + d)])
                k.cp('act', wt[:, d * 4 + 0, :], PS(0 + d, 128), [PSR(0 + d)], [wr])
                k.mm(PS(2 + d, 128), V2(TW["WBs", d]), ident, True, True, ['WBs%d' % d, 'ident'], [PSR(2 + d)])
                k.cp('act', wt[:, d * 4 + 1, :], PS(2 + d, 128), [PSR(2 + d)], [wr])
                k.cp('dve', wt[:, d * 4 + 2, :], V2(TW["WC", d]), ['WC%d' % d], [wr])
                k.cp('dve', wt[:, d * 4 + 3, :], V2(TW["WC2", d]), ['WC2%d' % d], [wr])
                k.mm(PS(4 + d, 128), V2(TW["Vt", d]), V2(TW["WC", d]), True, True, ['Vt%d' % d, 'WC%d' % d], [PSR(4 + d)])
            k.tt('dve', kt1, PS(4, 128), mkf, ALU.mult, [PSR(4), 'mkf'], ['kt1'])
            k.tt('dve', kt2, PS(5, 128), mkb, ALU.mult, [PSR(5), 'mkb'], ['kt2'])
            k.tt('dve', kt1, kt1, kt2, ALU.add, ['kt1', 'kt2'], ['kt1'])
            k.stt('dve', wt[:, 8, :], ident, dcol[:, g:g + 1], kt1, ALU.mult, ALU.add, ['kt1', 'ident', 'dcol'], [wr])
            k.dma('pool', ssmw[g], wt, wr, [wr], ['ssmw'])
    assert S1.cur <= DA.lo, (S1.cur, DA.lo)
    P.barrier_all()

    uid = [0]

    def XR(xr):
        return [f'{xr}.{c}' for c in range(8)]

    def ssq_step(xT, xr, sq, kc, n=512, bank_a=0):
        s_ = kc % 2
        if kc % 4 == 3:
            k.tt('dve', sq[s_][:, 0:n], xT[:, kc, 0:n], xT[:, kc, 0:n], ALU.mult, [f'{xr}.{kc}'], [f'sq{s_}'])
        else:
            k.act(sq[s_][:, 0:n], xT[:, kc, 0:n], AF.Square, [f'{xr}.{kc}'], [f'sq{s_}'])
        k.mm(PS(bank_a, n), ones_mat, sq[s_][:, 0:n], kc == 0, kc == 7, [f'sq{s_}', 'ones_mat'], [PSR(bank_a)])

    def norm_to_hT(xT, xr, hT, hr, sq, rbc, rrow, n=512, bank_a=0, bank_b=1, fused=False):
        if not fused:
            for kc in range(8):
                ssq_step(xT, xr, sq, kc, n, bank_a)
        k.act(rbc[:, 0:n], PS(bank_a, n), AF.Ln, [PSR(bank_a), 'epsc'], ['rbc'], scale=1.0 / D, bias=epsc[:, 0:1])
        k.act(rbc[:, 0:n], rbc[:, 0:n], AF.Exp, ['rbc'], ['rbc'], scale=-0.5)
        for kc in range(8):
            k.tt('dve', hT[:, kc, 0:n], xT[:, kc, 0:n], rbc[:, 0:n], ALU.mult, [f'{xr}.{kc}', 'rbc'], [f'{hr}{kc}'])

    def load_xT(xsrc, t0, xin, xT, xr, nblk=4):
        for b in range(nblk):
            s = b % len(xin)
            k.dma('sp', xin[s], xsrc[t0 + b * 128:t0 + (b + 1) * 128, :], f'xin{s}', [], [f'xin{s}'])
            for kc in range(8):
                k.mm(PS(kc)[:, b * 128:(b + 1) * 128], xin[s][:, kc * 128:(kc + 1) * 128], ident, True, True,
                     [f'xin{s}', 'ident'], [PSR(kc)])
        for kc in range(8):
            k.cp('act' if kc % 2 == 0 else 'dve', xT[:, kc, 0:nblk * 128], PS(kc, nblk * 128), [PSR(kc)], [f'{xr}.{kc}'])

    def ssm_phases(xsrc, L, LC, jobs=None, hi=None):
        nch = L // 8
        PA = Arena(nc, A0.cur, hi if hi is not None else A0.hi)
        U = PA.alloc("U", [128, 32, nch], BF16)
        AA = PA.sub()
        xin = [AA.alloc(f"xin{i}", [128, D], F32) for i in range(2)]
        xT = AA.alloc("xT", [128, 8, 512], F32)
        sq = [AA.alloc(f"sq{i}", [128, 512], BF16) for i in range(2)]
        rbc = AA.alloc("rbc", [128, 512], F32)
        rrow = AA.alloc("rrow", [1, 512], F32)
        hT1k = AA.alloc("hT1k", [128, 8, 1024], BF16)
        wa = AA.alloc("wa", [128, 8, 512], BF16)
        Zc = AA.alloc("Zc", [128, 32, 8, 16], BF16)
        k.dma('sp', wa, WS["w_in"][0], 'wa', ['WS_w_in'], ['wa'])
        for t1k in range(L // 1024):
            for half in range(2):
                t0 = t1k * 1024 + half * 512
                load_xT(xsrc, t0, xin, xT, 'xTa')
                norm_to_hT(xT, 'xTa', hT1k[:, :, half * 512:(half + 1) * 512], 'hT1k', sq, rbc, rrow)
            for tau in range(8):
                for kc in range(8):
                    k.mm(PS(tau), hT1k[:, kc, tau::8], wa[:, kc, :], kc == 0, kc == 7, [f'hT1k{kc}', 'wa'], [PSR(tau)])
                k.cp('act' if tau % 2 == 0 else 'dve', Zc[:, :, tau, :], PS(tau).rearrange("p (g h) -> p g h", h=16), [PSR(tau)], ['Zc'])
            for g in range(32):
                b = g // 4
                k.mm(PS(b)[:, (g % 4) * 128:(g % 4 + 1) * 128], Zc[:, g, :, :].rearrange("p a b -> p (a b)"), identb, True, True,
                     ['Zc', 'identb'], [PSR(b)])
                if g % 4 == 3:
                    k.cp('act' if b % 2 == 0 else 'dve', U[:, b * 4:(b + 1) * 4, t1k * 128:(t1k + 1) * 128],
                         PS(b).rearrange("p (g c) -> p g c", c=128), [PSR(b)], [f'U{b * 4 + q_}' for q_ in range(4)])
        P.barrier_all()
        BA = PA.sub()
        wset = [BA.alloc(f"wset{i}", [128, 9, 128], BF16) for i in range(2)]
        n1 = nch // 32
        tab = [BA.alloc(f"tab{i}", [128, 2, nch], F32) for i in range(2)]
        tw = BA.alloc("tw", [128, 2, nch], F32)
        tr = BA.alloc("tr", [128, 2, nch], F32)
        xt = [BA.alloc(f"xt{i}", [128, nch], F32) for i in range(2)]
        xt2 = BA.alloc("xt2", [128, nch], F32)
        St = BA.alloc("St", [128, nch], F32)
        P12 = [BA.alloc(f"P12{i}", [128, 2, nch], BF16) for i in range(2)]
        nb = (nch + 511) // 512
        units = [(g, d) for g in range(32) for d in range(2)]

        def wsel(g):
            return wset[g % 2], f'wset{g % 2}'

        def tabgen(g, d):
            gd = d * 32 + g
            w0 = tw[:, 0, :].rearrange("p (a b) -> p a b", b=32)
            k.ts('dve', w0, iota[:, :].unsqueeze(1).to_broadcast([128, n1, 32]), phi[:, gd:gd + 1], ALU.mult,
                 ['iota', 'phi'], ['tw'])
            k.stt('dve', w0, iota[:, 0:n1].unsqueeze(2).to_broadcast([128, n1, 32]), f1[:, gd:gd + 1], w0,
                  ALU.mult, ALU.add, ['iota', 'f1', 'tw'], ['tw'])
            k.ts('dve', tw[:, 1, :], tw[:, 0, :], 0.25, ALU.add, ['tw'], ['tw'])
            k.ts('dve', tr, tw, MAGIC, ALU.add, ['tw'], ['tr'], s2=MAGIC, op1=ALU.subtract)
            k.tt('dve', tr, tw, tr, ALU.subtract, ['tw', 'tr'], ['tr'])
            k.act(tab[d], tr, AF.Sin, ['tr'], [f'tab{d}'], scale=TWO_PI)

        def xmm(g, d):
            ws_, wsr = wsel(g)
            if d == 0:
                k.dma('sp', ws_, ssmw[g], wsr, ['ssmw'], [wsr])
            for bi in range(nb):
                cs = slice(bi * 512, min(nch, (bi + 1) * 512))
                n = cs.stop - cs.start
                k.mm(PS(bi, n), ws_[:, d * 4 + 0, :], U[:, g, cs], True, True, [wsr, f'U{g}'], [PSR(bi)])
                k.mm(PS(nb + bi, n), ws_[:, d * 4 + 1, :], U[:, g, cs], True, True, [wsr, f'U{g}'], [PSR(nb + bi)])

        def rot(g, d):
            tb_, tbr, x_, xr_ = tab[d], f'tab{d}', xt[d], f'xt{d}'
            for bi in range(nb):
                cs = slice(bi * 512, min(nch, (bi + 1) * 512))
                n = cs.stop - cs.start
                k.tt('dve', x_[:, cs], PS(bi, n), tb_[:, 1, cs], ALU.mult, [PSR(bi), tbr], [xr_])
                k.tt('dve', xt2[:, cs], PS(nb + bi, n), tb_[:, 0, cs], ALU.mult, [PSR(nb + bi), tbr], ['xt2'])
            k.tt('dve', x_, x_, xt2, ALU.add, [xr_, 'xt2'], [xr_])

        def scn(g, d):
            gd = d * 32 + g
            tb_, tbr, x_, xr_ = tab[d], f'tab{d}', xt[d], f'xt{d}'
            rb = rho8[:, gd:gd + 1].to_broadcast([128, nch])
            if d == 0:
                k.scan(St, rb, x_, [xr_, 'rho8'], ['St'])
            else:
                k.scan(St[:, ::-1], rb, x_[:, ::-1], [xr_, 'rho8'], ['St'])
            pp = P12[d]
            k.tt('dve', pp[:, 0, :], St, tb_[:, 1, :], ALU.mult, ['St', tbr], [f'P12{d}'])
            k.tt('dve', pp[:, 1, :], St, tb_[:, 0, :], ALU.mult, ['St', tbr], [f'P12{d}'])

        def ymm(g):
            ws_, wsr = wsel(g)
            for bi in range(nb):
                c0 = bi * 512
                c1 = min(nch, c0 + 512)
                yb = 4 + bi
                rdeps = [wsr, f'U{g}', 'P120', 'P121']
                k.mm(PS(yb, c1 - c0), ws_[:, 8, :], U[:, g, c0:c1], True, False, rdeps, [PSR(yb)])
                lo = max(c0, 1)
                for wi in (2, 3):
                    k.mm(psum[:, yb * 512 + (lo - c0):yb * 512 + (c1 - c0)], ws_[:, wi, :],
                         P12[0][:, wi - 2, lo - 1:c1 - 1], False, False, rdeps, [PSR(yb)])
                hi = min(c1, nch - 1)
                for wi in (6, 7):
                    k.mm(psum[:, yb * 512:yb * 512 + (hi - c0)], ws_[:, wi, :],
                         P12[1][:, wi - 6, c0 + 1:hi + 1], False, wi == 7, rdeps, [PSR(yb)])
                k.act(U[:, g, c0:c1], PS(yb, c1 - c0), AF.Gelu_apprx_tanh, [PSR(yb)], [f'U{g}'])

        tabgen(*units[0])
        for ui, (g, d) in enumerate(units):
            xmm(g, d)
            if ui + 1 < len(units):
                tabgen(*units[ui + 1])
            rot(g, d)
            scn(g, d)
            if d == 1:
                ymm(g)
            if jobs:
                jobs.pop(0)()
        while jobs:
            jobs.pop(0)()
        P.barrier_all()
        CA = PA.sub()
        Gc = CA.alloc("Gc", [128, 8, 512], BF16)
        gTt = [CA.alloc(f"gTt{i}", [128, 4, 1024], BF16) for i in range(2)]
        for t1k in range(LC // 1024):
            gt = gTt[t1k % 2]
            gtr = f'gTt{t1k % 2}'
            for g in range(32):
                b = g // 4
                k.mm(PS(b)[:, (g % 4) * 128:(g % 4 + 1) * 128], U[:, g, t1k * 128:(t1k + 1) * 128], identb, True, True,
                     [f'U{g}', 'identb'], [PSR(b)])
                if g % 4 == 3:
                    src = PS(b).rearrange("p (g t h) -> p g t h", g=4, t=8)
                    dst = Gc[:, :, b * 64:(b + 1) * 64].rearrange("p t (g h) -> p g t h", g=4)
                    k.cp('act' if b % 2 == 0 else 'dve', dst, src, [PSR(b)], ['Gc'])
            for j in range(4):
                for th in range(2):
                    b = j * 2 + th
                    for ti in range(4):
                        tau = th * 4 + ti
                        k.mm(PS(b)[:, ti * 128:(ti + 1) * 128], Gc[:, tau, j * 128:(j + 1) * 128], identb, True, True,
                             ['Gc', 'identb'], [PSR(b)])
                    src = PS(b).rearrange("p (t c) -> p t c", t=4)
                    dst = gt[:, j, :].rearrange("p (c t) -> p t c", t=8)[:, th * 4:(th + 1) * 4, :]
                    k.cp('act' if b % 2 == 0 else 'dve', dst, src, [PSR(b)], [gtr])
            k.dma('pool', gTs[:, :, t1k * 1024:(t1k + 1) * 1024].rearrange("j p t -> p j t"), gt, gtr, [gtr], ['gTs'])
        P.barrier_all()

    def main_seq(xsrc, ydst, L, OWN):
        MA = A0.sub()
        R = 5
        ring = [MA.alloc(f"ring{i}", [128, 4096], BF16) for i in range(R)]
        biasT = MA.alloc("biasT", [128, 16, 3, 128], BF16)
        xin = [MA.alloc(f"xin{i}", [128, D], F32) for i in range(4)]
        xouts = [xin[2], xin[3]]
        xb = [MA.alloc(f"xb{i}", [128, 8, 512], F32) for i in range(2)]
        hT = MA.alloc("hT", [128, 8, 512], BF16)
        sq = [MA.alloc(f"sq{i}", [128, 512], BF16) for i in range(2)]
        mst = MA.alloc("mst", [128, 3, 128], F32)
        rbc = MA.alloc("rbc", [128, 512], F32)
        rrow = MA.alloc("rrow", [1, 512], F32)
        gT = MA.alloc("gT", [128, 4, 512], BF16)
        sgate = [MA.alloc(f"sgate{i}", [128, 512], F32) for i in range(2)]
        qT = [MA.alloc(f"qT{i}", [128, 8, 512], BF16) for i in range(2)]
        NR = 12
        kT = MA.alloc("kT", [128, 2, NR * 128], BF16)
        Vr = MA.alloc("Vr", [128, NR, 4, 65], BF16)
        stat = MA.alloc("stat", [128, 64], F32)
        GX = MA.sub()
        ubT = GX.alloc("ubT", [128, 4, 512], BF16)
        vgs = [GX.alloc(f"vg{i}", [128, 512], F32) for i in range(2)]
        vsqs = [GX.alloc(f"vsq{i}", [128, 512], F32) for i in range(2)]
        vns = [GX.alloc(f"vn{i}", [128, 512], BF16) for i in range(2)]
        ybT = GX.alloc("ybT", [128, 4, 512], BF16)
        sg = GX.alloc("sg", [128, 512], F32)
        yaT = GX.alloc("yaT", [128, 4, 512], BF16)
        GXq = MA.sub()
        qk = GXq.alloc("qk", [128, 1280], F32)
        qsq = GXq.alloc("qsq", [128, 1280], F32)
        qn = GXq.alloc("qn", [128, 1280], BF16)
        MA.cur = max(GX.cur, GXq.cur)
        GY = MA.sub()
        actT = GY.alloc("actT", [128, NJ, 512], BF16)
        GYa = MA.sub()
        PT = [GYa.alloc(f"PT{i}", [128, 3, 512], BF16) for i in range(2)]
        on = GYa.alloc("on", [128, 1024], BF16)
        oT = GYa.alloc("oT", [128, 8, 512], BF16)
        MA.cur = max(GY.cur, GYa.cur)
        L0N = ['ubT', 'vg0', 'vg1', 'vsq0', 'vsq1', 'vn0', 'vn1', 'ybT', 'sg', 'yaT']
        QKN = ['qk', 'qsq', 'qn']
        ATN = ['PT0', 'PT1', 'on', 'oT']
        NBseq = L // 128

        mstage = mst
        k.dma('sp', mstage, I["maskc"], 'mstd', [], ['mst'])
        bst = xb[0].rearrange("p a b -> p (a b)")[:, 0:2048].rearrange("p (h q) -> p h q", h=16)
        for r in range(3):
            k.dma('sp', bst, I["biasg"][:, :, r, :], 'bst', [], XR('xb0'))
            k.tt('dve', biasT[:, :, r, :], bst, mstage[:, r, :].unsqueeze(1).to_broadcast([128, 16, 128]), ALU.add,
                 XR('xb0') + ['mst'], ['biasT'])
        k.memset('dve', Vr[:, :, :, 64:65], 1.0, ['Vr'])

        ridx = [0]

        def wload(src, dreg, view):
            s = ridx[0] % R
            ridx[0] += 1
            tot = int(np.prod(src.shape[1:]))
            dst = ring[s][:, 0:tot]
            if view is not None:
                dst = dst.rearrange(view[0], **view[1])
            k.dma('sp', dst, src, f'ring{s}', [dreg], [f'ring{s}'])
            return dst, f'ring{s}'

        def wpiece(nm, pc, KC=8, ncol=512):
            src = WS[nm][pc]
            if ncol != 512:
                src = src[:, :, 0:ncol]
            return wload(src, 'WS_' + nm, ("p (c n) -> p c n", dict(n=ncol)))

        def ffn(l, xT, xr, n):
            P.alias(['actT'], ATN)
            norm_to_hT(xT, xr, hT, 'hT', sq, rbc, rrow, n=n, fused=True)
            jcount = 0
            for pc in range(6):
                ncol = 512 if pc < 5 else 256
                wg_, wgr = wpiece(f"wg{l}", pc, ncol=ncol)
                wu_, wur = wpiece(f"wu{l}", pc, ncol=ncol)
                for jj in range(ncol // 128):
                    j = pc * 4 + jj
                    ba, bb = (2, 3) if jcount % 2 == 0 else (4, 5)
                    s = jcount % 2
                    jcount += 1
                    for kc in range(8):
                        k.mm(PS(ba, n), wg_[:, kc, jj * 128:(jj + 1) * 128], hT[:, kc, 0:n], kc == 0, kc == 7, [wgr, f'hT{kc}'], [PSR(ba)])
                    for kc in range(8):
                        k.mm(PS(bb, n), wu_[:, kc, jj * 128:(jj + 1) * 128], hT[:, kc, 0:n], kc == 0, kc == 7, [wur, f'hT{kc}'], [PSR(bb)])
                    k.act(sgate[s][:, 0:n], PS(ba, n), AF.Silu, [PSR(ba)], [f'sgate{s}'])
                    k.tt('dve', actT[:, j, 0:n], sgate[s][:, 0:n], PS(bb, n), ALU.mult, [f'sgate{s}', PSR(bb)], ['actT'])
            for fc in range(8):
                wd_, wdr = wload(WS[f"wd{l}"][fc], f'WS_wd{l}', ("p (j n) -> p j n", dict(n=128)))
                bo = 6 + fc % 2
                for j in range(NJ):
                    k.mm(PS(bo, n), wd_[:, j, :], actT[:, j, 0:n], j == 0, j == NJ - 1, [wdr, 'actT'], [PSR(bo)])
                k.tt('dve', xT[:, fc, 0:n], xT[:, fc, 0:n], PS(bo, n), ALU.add, [f'{xr}.{fc}', PSR(bo)], [f'{xr}.{fc}'])
                if l == 0 and fc >= 1:
                    ssq_step(xT, xr, sq, fc - 1, n)
            if l == 0:
                ssq_step(xT, xr, sq, 7, n)

        def L0(i, xT, xr, n):
            t0 = i * 512
            nblk = n // 128
            P.alias(L0N, QKN)
            load_xT(xsrc, t0, xin, xT, xr, nblk=nblk)
            k.dma('sp', gT[:, :, 0:n], gTs[:, :, t0:t0 + n].rearrange("j p t -> p j t"), 'gT', ['gTs'], ['gT'])
            norm_to_hT(xT, xr, hT, 'hT', sq, rbc, rrow, n=n)
            w1, w1r = wpiece("w_in", 1)
            for jj in range(4):
                bk = 2 + jj % 2
                for kc in range(8):
                    k.mm(PS(bk, n), w1[:, kc, jj * 128:(jj + 1) * 128], hT[:, kc, 0:n], kc == 0, kc == 7, [w1r, f'hT{kc}'], [PSR(bk)])
                k.act(ubT[:, jj, 0:n], PS(bk, n), AF.Gelu_apprx_tanh, [PSR(bk)], ['ubT'])
            w2, w2r = wpiece("w_in", 2)
            for b in range(nblk):
                bk = 4 + b
                ts_ = slice(b * 128, (b + 1) * 128)
                for kc in range(8):
                    k.mm(PS(bk), hT[:, kc, ts_], w2[:, kc, :], kc == 0, kc == 7, [w2r, f'hT{kc}'], [PSR(bk)])

            def chainA(b):
                bk = 4 + b
                q_ = b % 2
                st_ = stat[:, 48 + 4 * b:52 + 4 * b]
                k.act(vgs[q_], PS(bk), AF.Gelu_apprx_tanh, [PSR(bk)], [f'vg{q_}'])
                k.act(vsqs[q_], vgs[q_], AF.Square, [f'vg{q_}'], [f'vsq{q_}'])
                k.reduce(st_, vsqs[q_].rearrange("p (h d) -> p h d", h=4), [f'vsq{q_}'], [f'vst{b}'])

            def chainB(b):
                q_ = b % 2
                st_ = stat[:, 48 + 4 * b:52 + 4 * b]
                k.act(st_, st_, AF.Ln, [f'vst{b}', 'epsc'], [f'vst{b}'], scale=1.0 / 128, bias=epsc[:, 0:1])
                k.act(st_, st_, AF.Exp, [f'vst{b}'], [f'vst{b}'], scale=-0.5)
                for h in range(4):
                    hs = slice(h * 128, (h + 1) * 128)
                    k.stt('dve', vns[q_][:, hs], vgs[q_][:, hs], st_[:, h:h + 1], gmn[:, hs], ALU.mult, ALU.mult,
                          [f'vg{q_}', f'vst{b}', 'gmn'], [f'vn{q_}'])

            def mixed(b):
                bm = 4 + b
                q_ = b % 2
                ts_ = slice(b * 128, (b + 1) * 128)
                for h in range(4):
                    hs = slice(h * 128, (h + 1) * 128)
                    k.mm(PS(bm)[:, hs], vns[q_][:, hs], wsT[:, h, :], True, True, [f'vn{q_}', 'wsT'], [PSR(bm)])
                v3 = vsqs[q_].rearrange("p (h d) -> p h d", h=4)
                k.tt('dve', v3, PS(bm).rearrange("p (h d) -> p h d", h=4), bsT, ALU.add, [PSR(bm), 'bsT'], [f'vsq{q_}'])
                k.tt('dve', ybT[:, :, ts_], v3, ubT[:, :, ts_], ALU.mult, [f'vsq{q_}', 'ubT'], ['ybT'])

            for b0 in range(0, nblk, 2):
                bs = [b for b in (b0, b0 + 1) if b < nblk]
                for b in bs:
                    chainA(b)
                for b in bs:
                    chainB(b)
                for b in bs:
                    mixed(b)
            wl, wlr = wload(WS["glu_w"][0], 'WS_glu_w', ("p (c n) -> p c n", dict(n=512)))
            for fo in range(4):
                bk = 2 + fo % 2
                for kc in range(4):
                    k.mm(PS(bk, n), wl[:, kc, fo * 128:(fo + 1) * 128], gT[:, kc, 0:n], kc == 0, kc == 3, [wlr, 'gT'], [PSR(bk)])
                k.act(sg[:, 0:n], PS(bk, n), AF.Sigmoid, [PSR(bk), 'glub'], ['sg'], bias=glub[:, fo:fo + 1])
                k.tt('dve', yaT[:, fo, 0:n], gT[:, fo, 0:n], sg[:, 0:n], ALU.mult, ['gT', 'sg'], ['yaT'])
            for pc in range(2):
                wo_, wor = wpiece("w_out", pc)
                for fci in range(4):
                    fc = pc * 4 + fci
                    bo = 6 + fc % 2
                    for kc in range(8):
                        rhs = yaT[:, kc, 0:n] if kc < 4 else ybT[:, kc - 4, 0:n]
                        k.mm(PS(bo, n), wo_[:, kc, fci * 128:(fci + 1) * 128], rhs, kc == 0, kc == 7, [wor, 'yaT', 'ybT'], [PSR(bo)])
                    k.tt('dve', xT[:, fc, 0:n], xT[:, fc, 0:n], PS(bo, n), ALU.add, [f'{xr}.{fc}', PSR(bo)], [f'{xr}.{fc}'])
                    if fc >= 1:
                        ssq_step(xT, xr, sq, fc - 1, n)
            ssq_step(xT, xr, sq, 7, n)
            if debug and n == 512:
                k.dma('pool', dbg_x1[i], xT, 'dbg', XR(xr), [])
            ffn(0, xT, xr, n)
            if debug and n == 512:
                k.dma('pool', dbg_x2[i], xT, 'dbg', XR(xr), [])

        def QKV(i, xT, xr, n, qTs, qTr):
            nblk = n // 128
            P.alias(QKN, L0N)
            norm_to_hT(xT, xr, hT, 'hT', sq, rbc, rrow, n=n, fused=True)
            wq = [wpiece("w_qkv", pc) for pc in range(3)]

            def qmm(b):
                ts_ = slice(b * 128, (b + 1) * 128)
                for pc in range(3):
                    for kc in range(8):
                        k.mm(PS(1 + pc), hT[:, kc, ts_], wq[pc][0][:, kc, :], kc == 0, kc == 7, [wq[pc][1], f'hT{kc}'], [PSR(1 + pc)])

            def qchain(b):
                B = i * 4 + b
                slot = B % NR
                k.cp('act', qk[:, 0:512], PS(1), [PSR(1)], ['qk'])
                k.cp('dve', qk[:, 512:1024], PS(2), [PSR(2)], ['qk'])
                k.cp('act', qk[:, 1024:1280], PS(3)[:, 0:256], [PSR(3)], ['qk'])
                k.cp('act', Vr[:, slot, :, 0:64], PS(3)[:, 256:512].rearrange("p (v d) -> p v d", v=4), [PSR(3)], ['Vr'])
                k.act(qsq, qk, AF.Square, ['qk'], ['qsq'])
                k.reduce(stat[:, 8:28], qsq.rearrange("p (h d) -> p h d", d=64), ['qsq'], ['stat'])
                k.act(stat[:, 8:28], stat[:, 8:28], AF.Ln, ['stat', 'epsc'], ['stat'], scale=1.0 / 64, bias=epsc[:, 0:1])
                k.act(stat[:, 8:28], stat[:, 8:28], AF.Exp, ['stat'], ['stat'], scale=-0.5)
                for A in range(2):
                    i0_ = qk[:, A * 512:(A + 1) * 512].rearrange("p (b c d) -> p b c d", b=2, c=4)
                    i1_ = stat[:, 8 + A * 8:16 + A * 8].rearrange("p (b c) -> p b c", b=2).unsqueeze(3).to_broadcast([128, 2, 4, 64])
                    o_ = qn[:, A * 512:(A + 1) * 512].rearrange("p (c b d) -> p b c d", c=4, b=2)
                    k.tt('dve', o_, i0_, i1_, ALU.mult, ['qk', 'stat'], ['qn'])
                k.tt('dve', qn[:, 1024:1280].rearrange("p (h d) -> p h d", d=64), qk[:, 1024:1280].rearrange("p (h d) -> p h d", d=64),
                     stat[:, 24:28].unsqueeze(2).to_broadcast([128, 4, 64]), ALU.mult, ['qk', 'stat'], ['qn'])

            def qtr(b):
                B = i * 4 + b
                slot = B % NR
                ts_ = slice(b * 128, (b + 1) * 128)
                for half in range(2):
                    bk = 4 + half
                    for pi in range(4):
                        pr = half * 4 + pi
                        k.mm(PS(bk)[:, pi * 128:(pi + 1) * 128], qn[:, pr * 128:(pr + 1) * 128], identb, True, True, ['qn', 'identb'], [PSR(bk)])
                    k.cp('dve' if half == 0 else 'act', qTs[:, half * 4:(half + 1) * 4, ts_], PS(bk).rearrange("p (a t) -> p a t", a=4),
                         [PSR(bk)], [qTr])
                for kp in range(2):
                    k.mm(PS(6)[:, kp * 128:(kp + 1) * 128], qn[:, 1024 + kp * 128:1024 + (kp + 1) * 128], identb, True, True,
                         ['qn', 'identb'], [PSR(6)])
                k.ts('dve', kT[:, :, slot * 128:(slot + 1) * 128], PS(6)[:, 0:256].rearrange("p (a t) -> p a t", a=2), gqk[:, 0:1], ALU.mult,
                     [PSR(6), 'gqk'], ['kT'])

            qmm(0)
            for b in range(nblk):
                qchain(b)
                if b + 1 < nblk:
                    qmm(b + 1)
                qtr(b)

        def L1(i, xT, xr, qTs, qTr):
            P.alias(ATN, ['actT'])
            n = 512
            sbc = [0]

            def rs_of(qb):
                QB = i * 4 + qb
                return [r for r in range(3) if 0 <= QB - 1 + r < NBseq]

            def scores(qb, kv):
                QB = i * 4 + qb
                qs_ = slice(qb * 128, (qb + 1) * 128)
                hs = slice((kv % 2) * 64, (kv % 2) * 64 + 64)
                kp = kv // 2
                pt = PT[kv % 2]
                ptr = f'PT{kv % 2}'
                for r in rs_of(qb):
                    slot = (QB - 1 + r) % NR
                    sb = 1 + (sbc[0] % 3)
                    sbc[0] += 1
                    k.mm(PS(sb), kT[hs, kp, slot * 128:(slot + 1) * 128], qTs[hs, kp * 4:kp * 4 + 4, qs_], True, False,
                         ['kT', qTr], [PSR(sb)])
                    k.mm(PS(sb), identb, biasT[:, kv * 4:(kv + 1) * 4, r, :], False, True, ['identb', 'biasT'], [PSR(sb)])
                    k.act(pt[:, r, :], PS(sb), AF.Exp, [PSR(sb)], [ptr])

            def pv(qb, kv):
                QB = i * 4 + qb
                rs = rs_of(qb)
                pt = PT[kv % 2]
                ptr = f'PT{kv % 2}'
                for hh in range(4):
                    h = kv * 4 + hh
                    col = (h // 7) * 512 + (h % 7) * 65
                    for r in rs:
                        slot = (QB - 1 + r) % NR
                        k.mm(psum[:, 5 * 512 + col:5 * 512 + col + 65], pt[:, r, hh * 128:(hh + 1) * 128], Vr[:, slot, kv, :],
                             r == rs[0], r == rs[-1], [ptr, 'Vr'], [PSR(5 + h // 7)])

            def fin(qb):
                qs_ = slice(qb * 128, (qb + 1) * 128)
                for bg, (h0, nh) in enumerate([(0, 7), (7, 7), (14, 2)]):
                    ov = PS(5 + bg)[:, 0:nh * 65].rearrange("p (h e) -> p h e", e=65)
                    k.tt('dve', stat[:, 32 + h0:32 + h0 + nh].unsqueeze(2), ov[:, :, 64:65], esink[:, h0:h0 + nh].unsqueeze(2), ALU.add,
                         [PSR(5 + bg), 'esink'], ['stat2'])
                k.recip(stat[:, 32:48], stat[:, 32:48], ['stat2'], ['stat2'])
                for bg, (h0, nh) in enumerate([(0, 7), (7, 7), (14, 2)]):
                    ov = PS(5 + bg)[:, 0:nh * 65].rearrange("p (h e) -> p h e", e=65)
                    k.tt('dve', on[:, h0 * 64:(h0 + nh) * 64].rearrange("p (h d) -> p h d", d=64), ov[:, :, 0:64],
                         stat[:, 32 + h0:32 + h0 + nh].unsqueeze(2).to_broadcast([128, nh, 64]), ALU.mult, [PSR(5 + bg), 'stat2'], ['on'])
                for half in range(2):
                    bk = 0 if half == 0 else 4
                    for ci in range(4):
                        kc = half * 4 + ci
                        k.mm(PS(bk)[:, ci * 128:(ci + 1) * 128], on[:, kc * 128:(kc + 1) * 128], identb, True, True, ['on', 'identb'], [PSR(bk)])
                    k.cp('act' if half == 0 else 'dve', oT[:, half * 4:(half + 1) * 4, qs_], PS(bk).rearrange("p (a t) -> p a t", a=4),
                         [PSR(bk)], ['oT'])

            aunits = [(qb, kv) for qb in range(4) for kv in range(4)]
            scores(*aunits[0])
            for ui, (qb, kv) in enumerate(aunits):
                if ui + 1 < len(aunits):
                    scores(*aunits[ui + 1])
                pv(qb, kv)
                if kv == 3:
                    fin(qb)
            if debug:
                k.dma('pool', dbg_oT[i], oT, 'dbg', ['oT'], [])
            for pc in range(2):
                wo_, wor = wpiece("w_o", pc)
                for fci in range(4):
                    fc = pc * 4 + fci
                    bo = 6 + fc % 2
                    for kc in range(8):
                        k.mm(PS(bo, n), wo_[:, kc, fci * 128:(fci + 1) * 128], oT[:, kc, :], kc == 0, kc == 7, [wor, 'oT'], [PSR(bo)])
                    k.tt('dve', xT[:, fc, :], xT[:, fc, :], PS(bo, n), ALU.add, [f'{xr}.{fc}', PSR(bo)], [f'{xr}.{fc}'])
                    if fc >= 1:
                        ssq_step(xT, xr, sq, fc - 1, n)
            ssq_step(xT, xr, sq, 7, n)
            if debug:
                k.dma('pool', dbg_x3[i], xT, 'dbg', XR(xr), [])
            ffn(1, xT, xr, n)
            for b in range(4):
                ts_ = slice(b * 128, (b + 1) * 128)
                for kc in range(8):
                    bk = 1 + (b % 2) * 2 + kc // 4
                    k.mm(PS(bk)[:, (kc % 4) * 128:(kc % 4 + 1) * 128], xT[:, kc, ts_], ident, True, True, [f'{xr}.{kc}', 'ident'], [PSR(bk)])
                b0 = 1 + (b % 2) * 2
                xo, xor_ = xouts[b % 2], f'xin{2 + b % 2}'
                k.cp('act', xo[:, 0:512], PS(b0), [PSR(b0)], [xor_])
                k.cp('dve', xo[:, 512:1024], PS(b0 + 1), [PSR(b0 + 1)], [xor_])
                k.dma('pool', ydst[i * 512 + b * 128:i * 512 + (b + 1) * 128, :], xo, xor_, [xor_], [])

        nt = OWN // 512
        for i in range(nt):
            s = i % 2
            L0(i, xb[s], f'xb{s}', 512)
            QKV(i, xb[s], f'xb{s}', 512, qT[s], f'qT{s}')
            if i >= 1:
                L1(i - 1, xb[1 - s], f'xb{1 - s}', qT[1 - s], f'qT{1 - s}')
        if OWN < L:
            s = nt % 2
            L0(nt, xb[s], f'xb{s}', 128)
            QKV(nt, xb[s], f'xb{s}', 128, qT[s], f'qT{s}')
        s = (nt - 1) % 2
        L1(nt - 1, xb[s], f'xb{s}', qT[s], f'qT{s}')
        if debug:
            k.dma('pool', dbg_qT, qT[s], 'dbg', [f'qT{s}'], [])
            k.dma('pool', dbg_kT, kT, 'dbg', ['kT'], [])
            k.dma('pool', dbg_V, Vr, 'dbg', ['Vr'], [])
            k.dma('pool', dbg_bias, biasT, 'dbg', ['biasT'], [])
        P.barrier_all()

    ssm_phases(I["xs"], LS, LS, jobs=cast_jobs, hi=DA.lo)
    main_seq(I["xs"], ys, LS, LS)
    lc = ((OWNP + 128 + 1023) // 1024) * 1024 if OWNP < LP else LP
    ssm_phases(I["xp"], LP, min(lc, LP))
    main_seq(I["xp"], yp, LP, OWNP)
    block = stack.enter_context(nc.Block())
    P.emit(block)
    stack.close()
    return nc


def _rel_bucket_np(rel):
    half, max_exact = 16, 8
    ret = (rel > 0).astype(np.int32) * half
    n = np.abs(rel)
    nf = np.maximum(n, 1).astype(np.float32)
    large = max_exact + (np.log(nf / np.float32(max_exact)) / np.float32(np.log(128 / max_exact))
                         * np.float32(half - max_exact)).astype(np.int32)
    large = np.minimum(large, half - 1)
    return ret + np.where(n < max_exact, n, large)


def _consts():
    c = {}
    c["ident"] = np.eye(128, dtype=np.float32)
    tt = np.arange(128) // 16
    c["maskf"] = (tt[:, None] <= tt[None, :]).astype(np.float32)
    c["maskb"] = (tt[:, None] >= tt[None, :]).astype(np.float32)
    kexp = np.zeros((64, NSLOT), np.float32)
    for d in range(2):
        for tau in range(8):
            e = 7 - tau if d == 0 else tau
            kexp[d * 32:(d + 1) * 32, tau] = e
            kexp[d * 32:(d + 1) * 32, 8 + tau] = e - 8
            kexp[d * 32:(d + 1) * 32, 16 + tau] = tau + 1 if d == 0 else 8 - tau
    kexp[:, 24] = 1
    kexp[:, 25] = 8
    c["kexp"] = kexp.reshape(-1)
    c["iota32"] = np.arange(32, dtype=np.float32)
    c["sgn1"] = np.concatenate([-np.ones(64), np.ones(64)]).astype(np.float32)
    c["sgn2"] = -c["sgn1"]
    kk = np.arange(128)[:, None, None]
    r = np.arange(3)[None, :, None]
    qq = np.arange(128)[None, None, :]
    rel = (r - 1) * 128 + kk - qq
    c["maskc"] = np.where(np.abs(rel) <= 128, 0.0, -80.0).astype(np.float32)
    c["_rel"] = rel
    return c


def _core_inputs(inp, core, xs, xp, consts):
    rev = (core % 2 == 1)
    f = (lambda a: np.ascontiguousarray(a[::-1])) if rev else (lambda a: np.ascontiguousarray(a))
    m = {}
    m["xs"] = f(xs)
    m["xp"] = f(xp)
    m["norm_mix"] = inp["norm_mix"]
    m["norm_ffn"] = inp["norm_ffn"]
    m["w_in"] = inp["w_in_even"][0]
    m["glu_w"] = inp["glu_w"][0]
    m["glu_b"] = inp["glu_b"][0]
    m["w_out"] = inp["w_out_even"][0]
    m["w_qkv"] = inp["w_qkv"][0]
    m["w_o"] = inp["w_o"][0]
    for l in range(2):
        m[f"wg{l}"] = inp["ffn_w_gate"][l]
        m[f"wu{l}"] = inp["ffn_w_up"][l]
        m[f"wd{l}"] = inp["ffn_w_down"][l]
    dsw = (lambda a: np.ascontiguousarray(a[::-1])) if rev else (lambda a: a)
    m["lam_re"] = dsw(inp["ssm_lam_re"][0])
    m["lam_im"] = dsw(inp["ssm_lam_im"][0])
    m["log_dt"] = dsw(inp["ssm_log_dt"][0])
    m["b_re"] = dsw(inp["ssm_b_re"][0])
    m["b_im"] = dsw(inp["ssm_b_im"][0])
    m["c_re"] = dsw(inp["ssm_c_re"][0])
    m["c_im"] = dsw(inp["ssm_c_im"][0])
    m["ssm_d"] = inp["ssm_d"][0]
    m["gm_norm"] = inp["gm_norm"][0]
    ws, bs = inp["gm_w_s"][0], inp["gm_b_s"][0]
    if rev:
        ws, bs = ws[:, ::-1, ::-1], bs[:, ::-1]
    m["gm_w_s"] = np.ascontiguousarray(ws)
    m["gm_b_s"] = np.ascontiguousarray(bs)
    m["q_norm"] = inp["q_norm"][0]
    m["k_norm"] = inp["k_norm"][0]
    m["attn_sink"] = inp["attn_sink"][0]
    rel = consts["_rel"]
    bucket = _rel_bucket_np(-rel if rev else rel)
    bg = inp["rel_table"][bucket]
    m["biasg"] = np.ascontiguousarray(bg.transpose(0, 3, 1, 2))
    for kname in ["ident", "maskf", "maskb", "kexp", "iota32", "sgn1", "sgn2", "maskc"]:
        m[kname] = consts[kname]
    return {k_: np.ascontiguousarray(v, dtype=np.float32) for k_, v in m.items()}


_NC_CACHE = {}


def run_cores(inp, xs_list, xp_list, LS, LP, OWNP, ncores, debug=False):
    key = (LS, LP, OWNP, debug)
    if key not in _NC_CACHE:
        _NC_CACHE[key] = build(LS, LP, OWNP, debug=debug)
    nc = _NC_CACHE[key]
    consts = _consts()
    in_maps = [_core_inputs(inp, c, xs_list[c], xp_list[c], consts) for c in range(ncores)]
    res = run_bass_kernel_spmd(nc, in_maps, core_ids=list(range(ncores)))
    outs = []
    for c in range(ncores):
        r = res.results[c]
        ys_, yp_ = np.asarray(r["ys"]), np.asarray(r["yp"])
        if c % 2 == 1:
            ys_, yp_ = ys_[::-1], yp_[::-1]
        outs.append((ys_, yp_, r) if debug else (ys_, yp_))
    return outs


def kernel(**inputs):
    inp = {k_: np.asarray(v) for k_, v in inputs.items()}
    x_prompt, x_sample = inp["x_prompt"], inp["x_sample"]
    B, LP, _ = x_prompt.shape
    BS, LS, _ = x_sample.shape
    ncores = 8
    xs_list = [x_sample[c] for c in range(ncores)]
    xp_list = [x_prompt[c // 2] for c in range(ncores)]
    outs = run_cores(inp, xs_list, xp_list, LS, LP, LP // 2, ncores)
    y_sample = np.stack([outs[c][0] for c in range(ncores)], axis=0).astype(np.float32)
    y_prompt = np.zeros_like(x_prompt, dtype=np.float32)
    H = LP // 2
    for c in range(ncores):
        b = c // 2
        if c % 2 == 0:
            y_prompt[b, 0:H] = outs[c][1]
        else:
            y_prompt[b, H:LP] = outs[c][1]
    return (y_prompt, y_sample)
```

```python
import contextlib
import numpy as np
import ml_dtypes
import concourse.bass as bass
import concourse.mybir as mybir
from concourse.bass_utils import run_bass_kernel_spmd

F32 = mybir.dt.float32
BF16 = mybir.dt.bfloat16
AF = mybir.ActivationFunctionType
ALU = mybir.AluOpType
AX = mybir.AxisListType

D = 1024
FF = 2816
NJ = FF // 128
EPS = 1e-6
MAGIC = 12582912.0
TWO_PI = float(2 * np.pi)
ENGS = ['pe', 'act', 'dve', 'pool', 'sp']
NSLOT = 27


class Prog:
    def __init__(self, nc, stack):
        self.nc = nc
        self.stack = stack
        self.ops = {e: [] for e in ENGS}
        self.cnt = {e: 0 for e in ENGS}
        self.seen = {e: {} for e in ENGS}
        self.reg = {}
        self.dmasem = {}
        self.sems = {}

    def sem(self, key):
        if key not in self.sems:
            self.sems[key] = self.stack.enter_context(self.nc.semaphore("s_" + str(key)))
        return self.sems[key]

    def _deps(self, eng, reads, writes):
        need = {}

        def add(k, v):
            if v > need.get(k, 0):
                need[k] = v
        for r in reads:
            st = self.reg.get(r)
            if st and st['w']:
                add(*st['w'])
        for r in writes:
            st = self.reg.get(r)
            if st:
                if st['w']:
                    add(*st['w'])
                for k, v in st['r'].items():
                    add(k, v)
        seen = self.seen[eng]
        waits = []
        for k, v in need.items():
            if eng == 'pe' and k == 'pe':
                continue
            if k.startswith('d_'):
                v = self.dmasem[k]
            if seen.get(k, 0) < v:
                seen[k] = v
                waits.append((k, v))
        return waits

    def _commit(self, reads, writes, tok):
        k, v = tok
        for r in reads:
            st = self.reg.setdefault(r, {'w': None, 'r': {}})
            if st['r'].get(k, 0) < v:
                st['r'][k] = v
        for r in writes:
            self.reg[r] = {'w': tok, 'r': {}}

    def op(self, eng, fn, reads=(), writes=()):
        waits = self._deps(eng, reads, writes)
        self.cnt[eng] += 1
        tok = (eng, self.cnt[eng])
        self.ops[eng].append((waits, fn, (eng, 1)))
        self._commit(reads, writes, tok)

    def dma(self, eng, fn, semname, reads=(), writes=()):
        waits = self._deps(eng, reads, writes)
        key = 'd_' + semname
        c = self.dmasem.get(key, 0) + 16
        self.dmasem[key] = c
        self.ops[eng].append((waits, fn, (key, 16)))
        self._commit(reads, writes, (key, c))

    def alias(self, new, olds):
        merged = {}
        for o in olds:
            st = self.reg.get(o)
            if not st:
                continue
            if st['w']:
                k, v = st['w']
                merged[k] = max(merged.get(k, 0), v)
            for k, v in st['r'].items():
                merged[k] = max(merged.get(k, 0), v)
        for n in new:
            st = self.reg.get(n)
            m2 = dict(merged)
            if st:
                if st['w']:
                    k, v = st['w']
                    m2[k] = max(m2.get(k, 0), v)
                for k, v in st['r'].items():
                    m2[k] = max(m2.get(k, 0), v)
            self.reg[n] = {'w': None, 'r': m2}

    def barrier_all(self):
        for e in ENGS:
            waits = []
            for k, v in list(self.cnt.items()):
                if v > 0 and self.seen[e].get(k, 0) < v and not (e == 'pe' and k == 'pe'):
                    self.seen[e][k] = v
                    waits.append((k, v))
            for k, v in self.dmasem.items():
                if self.seen[e].get(k, 0) < v:
                    self.seen[e][k] = v
                    waits.append((k, v))
            if waits:
                self.ops[e].append((waits, None, None))
        self.reg = {}

    def emit(self, block):
        engobj = {'pe': 'tensor', 'act': 'scalar', 'dve': 'vector', 'pool': 'gpsimd', 'sp': 'sync'}
        for e in ENGS:
            self.sem(e)
        for k in self.dmasem:
            self.sem(k)
        for e in ENGS:
            ops = self.ops[e]

            def body(eng, ops=ops):
                for waits, fn, inc in ops:
                    for k, v in waits:
                        eng.wait_ge(self.sem(k), v)
                    if fn is not None:
                        ins = fn(eng)
                        ins.then_inc(self.sem(inc[0]), inc[1])
            getattr(block, engobj[e])(body)


class Arena:
    def __init__(self, nc, lo, hi):
        self.nc, self.lo, self.hi, self.cur = nc, lo, hi, lo
        self.n = 0

    def alloc(self, name, shape, dt):
        size = int(np.prod(shape[1:])) * (4 if dt == F32 else 2)
        off = (self.cur + 63) // 64 * 64
        assert off + size <= self.hi, f"SBUF arena overflow at {name}: need {off + size - self.hi} more bytes"
        self.cur = off + size
        self.n += 1
        return self.nc.alloc_sbuf_tensor_at(f"{name}_{self.n}_{off}", list(shape), dt, offset=off).ap()

    def sub(self):
        return Arena(self.nc, self.cur, self.hi)


class K:
    def __init__(self, P):
        self.P = P

    def tt(self, eng, out, a, b, op, r, w):
        self.P.op(eng, lambda e: e.tensor_tensor(out=out, in0=a, in1=b, op=op), r, w)

    def ts(self, eng, out, a, s1, op0, r, w, s2=None, op1=None):
        if op1 is None:
            self.P.op(eng, lambda e: e.tensor_scalar(out=out, in0=a, scalar1=s1, scalar2=None, op0=op0), r, w)
        else:
            self.P.op(eng, lambda e: e.tensor_scalar(out=out, in0=a, scalar1=s1, scalar2=s2, op0=op0, op1=op1), r, w)

    def stt(self, eng, out, a, scalar, b, op0, op1, r, w):
        self.P.op(eng, lambda e: e.scalar_tensor_tensor(out=out, in0=a, scalar=scalar, in1=b, op0=op0, op1=op1), r, w)

    def act(self, out, in_, func, r, w, scale=1.0, bias=None, accum=None):
        def f(e):
            kw = {}
            if bias is not None:
                kw['bias'] = bias
            if accum is not None:
                kw['accum_out'] = accum
            return e.activation(out=out, in_=in_, func=func, scale=scale, **kw)
        self.P.op('act', f, r, w)

    def cp(self, eng, out, in_, r, w):
        if eng == 'act':
            self.P.op('act', lambda e: e.activation(out=out, in_=in_, func=AF.Copy), r, w)
        else:
            self.P.op(eng, lambda e: e.tensor_copy(out=out, in_=in_), r, w)

    def mm(self, out, lhsT, rhs, start, stop, r, w):
        self.P.op('pe', lambda e: e.matmul(out, lhsT=lhsT, rhs=rhs, start=start, stop=stop), r, w)

    def dma(self, q, out, in_, sem, r, w, slow=False):
        if sem == 'c0':
            sem = 'c_' + w[0]
        if slow:
            self.P.dma(q, lambda e: e.dma_start(out=out, in_=in_, allow_slow_non_contiguous=True), sem, r, w)
        else:
            self.P.dma(q, lambda e: e.dma_start(out=out, in_=in_), sem, r, w)

    def memset(self, eng, out, val, w):
        self.P.op(eng, lambda e: e.memset(out, val), (), w)

    def recip(self, out, in_, r, w):
        self.P.op('dve', lambda e: e.reciprocal(out=out, in_=in_), r, w)

    def reduce(self, out, in_, r, w):
        self.P.op('dve', lambda e: e.tensor_reduce(out=out, in_=in_, axis=AX.X, op=ALU.add), r, w)

    def scan(self, out, d0, d1, r, w):
        self.P.op('dve', lambda e: e.tensor_tensor_scan(out=out, data0=d0, data1=d1, initial=0.0,
                                                       op0=ALU.mult, op1=ALU.add), r, w)


WPIECES = [
    ("w_in", 1024, 1536, "nm0"), ("glu_w", 512, 512, None), ("w_out", 1024, 1024, None),
    ("w_qkv", 1024, 1536, "nm1"), ("w_o", 1024, 1024, None),
    ("wg0", 1024, FF, "nf0"), ("wu0", 1024, FF, "nf0"), ("wg1", 1024, FF, "nf1"), ("wu1", 1024, FF, "nf1"),
]


def build(LS, LP, OWNP, debug=False):
    assert LS % 1024 == 0 and LP % 1024 == 0 and OWNP % 512 == 0
    nc = bass.Bass("TRN2", target_bir_lowering=False)
    stack = contextlib.ExitStack()
    P = Prog(nc, stack)
    k = K(P)

    def din(name, shape):
        return nc.dram_tensor(name, list(shape), F32, kind="ExternalInput").ap()

    def dscr(name, shape, dt=BF16):
        return nc.dram_tensor(name, list(shape), dt).ap()

    I = {}
    for nm, shp in [("xs", [LS, D]), ("xp", [LP, D]), ("norm_mix", [2, D]), ("norm_ffn", [2, D]),
                    ("w_in", [D, 1536]), ("glu_w", [512, 512]), ("glu_b", [512]), ("w_out", [D, D]),
                    ("w_qkv", [D, 1536]), ("w_o", [D, D]),
                    ("wg0", [D, FF]), ("wu0", [D, FF]), ("wd0", [FF, D]),
                    ("wg1", [D, FF]), ("wu1", [D, FF]), ("wd1", [FF, D]),
                    ("lam_re", [2, 32, 64]), ("lam_im", [2, 32, 64]), ("log_dt", [2, 32]),
                    ("b_re", [2, 32, 64, 16]), ("b_im", [2, 32, 64, 16]),
                    ("c_re", [2, 32, 16, 64]), ("c_im", [2, 32, 16, 64]), ("ssm_d", [512]),
                    ("gm_norm", [4, 128]), ("gm_w_s", [4, 128, 128]), ("gm_b_s", [4, 128]),
                    ("q_norm", [64]), ("k_norm", [64]), ("attn_sink", [16]),
                    ("biasg", [128, 16, 3, 128]), ("maskc", [128, 3, 128]),
                    ("ident", [128, 128]), ("maskf", [128, 128]), ("maskb", [128, 128]),
                    ("kexp", [64 * NSLOT]), ("iota32", [32]), ("sgn1", [128]), ("sgn2", [128])]:
        I[nm] = din(nm, shp)
    ys = nc.dram_tensor("ys", [LS, D], F32, kind="ExternalOutput").ap()
    yp = nc.dram_tensor("yp", [OWNP, D], F32, kind="ExternalOutput").ap()
    dbg = {}

    WS = {}
    for nm, Kd, N, g in WPIECES:
        WS[nm] = dscr("s_" + nm, [(N + 511) // 512, 128, Kd // 128, 512])
    WS["wd0"] = dscr("s_wd0", [8, 128, NJ, 128])
    WS["wd1"] = dscr("s_wd1", [8, 128, NJ, 128])
    ssmw = dscr("s_ssmw", [32, 128, 9, 128])
    Lmax = max(LS, LP)
    if debug:
        gTs = nc.dram_tensor("dbg_gT", [4, 128, Lmax], BF16, kind="ExternalOutput").ap()
        dbg_x2 = nc.dram_tensor("dbg_x2", [Lmax // 512, 128, 8, 512], F32, kind="ExternalOutput").ap()
        dbg_x3 = nc.dram_tensor("dbg_x3", [Lmax // 512, 128, 8, 512], F32, kind="ExternalOutput").ap()
        dbg_oT = nc.dram_tensor("dbg_oT", [Lmax // 512, 128, 8, 512], BF16, kind="ExternalOutput").ap()
        dbg_qT = nc.dram_tensor("dbg_qT", [128, 8, 512], BF16, kind="ExternalOutput").ap()
        dbg_kT = nc.dram_tensor("dbg_kT", [128, 2, 12 * 128], BF16, kind="ExternalOutput").ap()
        dbg_V = nc.dram_tensor("dbg_V", [128, 12, 4, 65], BF16, kind="ExternalOutput").ap()
        dbg_bias = nc.dram_tensor("dbg_bias", [128, 16, 3, 128], BF16, kind="ExternalOutput").ap()
        dbg_x1 = nc.dram_tensor("dbg_x1", [Lmax // 512, 128, 8, 512], F32, kind="ExternalOutput").ap()
    else:
        gTs = dscr("s_gT", [4, 128, Lmax])

    base = (nc.sbuf_base + 63) // 64 * 64
    top = nc.sbuf_top // 64 * 64
    A0 = Arena(nc, base, top)
    psum = nc.alloc_psum_tensor("ps", [128, 4096], F32).ap()

    def PS(b, n=512):
        return psum[:, b * 512:b * 512 + n]

    def PSR(b):
        return f"ps{b}"

    ident = A0.alloc("ident", [128, 128], F32)
    identb = A0.alloc("identb", [128, 128], BF16)
    ones_col = A0.alloc("ones_col", [128, 1], BF16)
    ones_row = A0.alloc("ones_row", [1, 128], F32)
    rho8 = A0.alloc("rho8", [128, 64], F32)
    phi = A0.alloc("phi", [128, 64], F32)
    f1 = A0.alloc("f1", [128, 64], F32)
    iota = A0.alloc("iota", [128, 32], F32)
    glub = A0.alloc("glub", [128, 4], F32)
    gmn = A0.alloc("gmn", [128, 512], F32)
    bsT = A0.alloc("bsT", [128, 4, 128], F32)
    wsT = A0.alloc("wsT", [128, 4, 128], BF16)
    esink = A0.alloc("esink", [128, 16], F32)
    gqk = A0.alloc("gqk", [128, 1], F32)
    sgn1 = A0.alloc("sgn1", [128, 1], F32)
    sgn2 = A0.alloc("sgn2", [128, 1], F32)
    epsc = A0.alloc("epsc", [128, 1], F32)
    ccol = A0.alloc("ccol", [128, 3], F32)

    k.dma('sp', ident, I["ident"], 'c0', [], ['ident'])
    k.cp('dve', identb, ident, ['ident'], ['identb'])
    k.memset('dve', ones_col, 1.0, ['ones_col'])
    k.memset('dve', ones_row, 1.0, ['ones_row'])
    k.memset('dve', epsc, EPS, ['epsc'])
    k.memset('dve', ccol[:, 0:1], MAGIC, ['ccol'])
    k.memset('dve', ccol[:, 1:2], -MAGIC, ['ccol'])
    k.memset('dve', ccol[:, 2:3], 0.25, ['ccol'])
    k.dma('sp', iota, I["iota32"].partition_broadcast(128), 'c0', [], ['iota'])
    k.dma('sp', sgn1, I["sgn1"].rearrange("(p o) -> p o", o=1), 'c0', [], ['sgn1'])
    k.dma('sp', sgn2, I["sgn2"].rearrange("(p o) -> p o", o=1), 'c0', [], ['sgn2'])
    k.dma('sp', glub, I["glu_b"].rearrange("(c p) -> p c", p=128), 'c0', [], ['glub'], slow=True)
    k.dma('sp', gmn, I["gm_norm"].rearrange("h d -> (h d)").partition_broadcast(128), 'c0', [], ['gmn'])
    k.dma('sp', bsT, I["gm_b_s"].rearrange("h i -> (h i)").partition_broadcast(128), 'c0', [], ['bsT'])
    k.dma('sp', esink, I["attn_sink"].partition_broadcast(128), 'c0', [], ['esink'])
    k.act(esink, esink, AF.Exp, ['esink'], ['esink'])

    gains = A0.alloc("gains", [128, 4, 8], F32)
    S1 = A0.sub()
    tq = S1.alloc("tq", [128, 2], F32)
    for half in range(2):
        k.dma('sp', tq[half * 64:(half + 1) * 64, 0:1], I["q_norm"].rearrange("(p o) -> p o", o=1), 'c0', [], ['tq'])
        k.dma('sp', tq[half * 64:(half + 1) * 64, 1:2], I["k_norm"].rearrange("(p o) -> p o", o=1), 'c0', [], ['tq'])
    k.stt('dve', gqk, tq[:, 0:1], 0.125, tq[:, 1:2], ALU.mult, ALU.mult, ['tq'], ['gqk'])
    wst = S1.alloc("wst", [128, 4, 128], F32)
    k.dma('sp', wst, I["gm_w_s"].rearrange("h i j -> i h j"), 'c0', [], ['wst'])
    for h in range(4):
        k.mm(PS(h, 128), wst[:, h, :], ident, True, True, ['wst', 'ident'], [PSR(h)])
        k.cp('dve', wsT[:, h, :], PS(h, 128), [PSR(h)], ['wsT'])

    for gi, (src, row) in enumerate([("norm_mix", 0), ("norm_mix", 1), ("norm_ffn", 0), ("norm_ffn", 1)]):
        k.dma('sp', gains[:, gi, :], I[src][row].rearrange("(c p) -> p c", p=128), 'c0', [], ['gains'], slow=True)
    gidx = {"nm0": 0, "nm1": 1, "nf0": 2, "nf1": 3}
    top_hi = top
    DEF_BYTES = 2 * (8 * 512 * 4) + 2 * (8 * 512 * 2) + 256
    DA = Arena(nc, (top_hi - DEF_BYTES) // 64 * 64, top_hi)
    stg = [DA.alloc(f"stg{i}", [128, 8, 512], F32) for i in range(2)]
    stb = [DA.alloc(f"stb{i}", [128, 8, 512], BF16) for i in range(2)]
    cnt = [0]
    cast_jobs = []

    def mk_piece_job(nm, Kd, N, g, pc, engs):
        def job():
            KC = Kd // 128
            src = I[nm].rearrange("(c p) n -> p c n", p=128)
            n0 = pc * 512
            nn = min(512, N - n0)
            s_ = cnt[0] % 2
            cnt[0] += 1
            k.dma('sp', stg[s_][:, 0:KC, 0:nn], src[:, :, n0:n0 + nn], f'stg{s_}', [], [f'stg{s_}'])
            for kc in range(KC):
                eng = engs[(cnt[0] + kc) % len(engs)]
                if g is None:
                    k.cp(eng, stb[s_][:, kc, 0:nn], stg[s_][:, kc, 0:nn], [f'stg{s_}'], [f'stb{s_}'])
                elif eng == 'act':
                    k.act(stb[s_][:, kc, 0:nn], stg[s_][:, kc, 0:nn], AF.Copy, [f'stg{s_}', 'gains'], [f'stb{s_}'],
                          scale=gains[:, gidx[g], kc:kc + 1])
                else:
                    k.ts(eng, stb[s_][:, kc, 0:nn], stg[s_][:, kc, 0:nn], gains[:, gidx[g], kc:kc + 1], ALU.mult,
                         [f'stg{s_}', 'gains'], [f'stb{s_}'])
            k.dma('pool', WS[nm][pc][:, :, 0:nn], stb[s_][:, 0:KC, 0:nn], f'stb{s_}', [f'stb{s_}'], ['WS_' + nm])
        return job

    def mk_wd_job(l, fc, jh, engs):
        def job():
            src = I[f"wd{l}"].rearrange("(j p) n -> p j n", p=128)
            s_ = cnt[0] % 2
            cnt[0] += 1
            stgv = stg[s_].rearrange("p a b -> p (a b)")[:, 0:11 * 128].rearrange("p (j n) -> p j n", n=128)
            stbv = stb[s_].rearrange("p a b -> p (a b)")[:, 0:11 * 128].rearrange("p (j n) -> p j n", n=128)
            k.dma('sp', stgv, src[:, jh * 11:(jh + 1) * 11, fc * 128:(fc + 1) * 128], f'stg{s_}', [], [f'stg{s_}'])
            k.cp(engs[cnt[0] % len(engs)], stbv, stgv, [f'stg{s_}'], [f'stb{s_}'])
            k.dma('pool', WS[f"wd{l}"][fc][:, jh * 11:(jh + 1) * 11, :], stbv, f'stb{s_}', [f'stb{s_}'], [f'WS_wd{l}'])
        return job

    for nm, Kd, N, g in WPIECES:
        for pc in range((N + 511) // 512):
            if nm == "w_in":
                mk_piece_job(nm, Kd, N, g, pc, ['dve', 'act'])()
            else:
                cast_jobs.append(mk_piece_job(nm, Kd, N, g, pc, ['act']))
    for l in range(2):
        for fc in range(8):
            for jh in range(2):
                cast_jobs.append(mk_wd_job(l, fc, jh, ['act']))

    def bc(ap, shape):
        return ap.to_broadcast(shape)

    kx = S1.alloc("kx", [128, 64, NSLOT], F32)
    k.dma('sp', kx, I["kexp"].partition_broadcast(128), 'c0', [], ['kx'])
    lr = S1.alloc("lr", [128, 64], F32)
    li = S1.alloc("li", [128, 64], F32)
    dtb = S1.alloc("dtb", [128, 64], F32)
    for half in range(2):
        k.dma('sp', lr[half * 64:(half + 1) * 64, :], I["lam_re"].rearrange("d g p -> p (d g)"), 'c0', [], ['lr'], slow=True)
        k.dma('sp', li[half * 64:(half + 1) * 64, :], I["lam_im"].rearrange("d g p -> p (d g)"), 'c0', [], ['li'], slow=True)
    k.dma('sp', dtb, I["log_dt"].rearrange("d g -> (d g)").partition_broadcast(128), 'c0', [], ['dtb'])
    k.act(dtb, dtb, AF.Exp, ['dtb'], ['dtb'])
    lrdt = S1.alloc("lrdt", [128, 64], F32)
    lidt = S1.alloc("lidt", [128, 64], F32)
    k.tt('dve', lrdt, lr, dtb, ALU.mult, ['lr', 'dtb'], ['lrdt'])
    k.stt('dve', lidt, li, 1.0 / TWO_PI, dtb, ALU.mult, ALU.mult, ['li', 'dtb'], ['lidt'])
    MAG = S1.alloc("MAG", [128, 64, NSLOT], F32)
    Wt = S1.alloc("Wt", [128, 64, NSLOT], F32)
    Rt = S1.alloc("Rt", [128, 64, NSLOT], F32)
    SINT = S1.alloc("SINT", [128, 64, NSLOT], F32)
    COST = S1.alloc("COST", [128, 64, NSLOT], F32)
    sh3 = [128, 64, NSLOT]
    k.tt('dve', MAG, kx, bc(lrdt.unsqueeze(2), sh3), ALU.mult, ['kx', 'lrdt'], ['MAG'])
    k.act(MAG, MAG, AF.Exp, ['MAG'], ['MAG'])
    k.tt('dve', Wt, kx, bc(lidt.unsqueeze(2), sh3), ALU.mult, ['kx', 'lidt'], ['Wt'])
    k.ts('dve', Rt, Wt, MAGIC, ALU.add, ['Wt'], ['Rt'], s2=MAGIC, op1=ALU.subtract)
    k.tt('dve', SINT, Wt, Rt, ALU.subtract, ['Wt', 'Rt'], ['SINT'])
    k.cp('dve', phi, SINT[:, :, 25], ['SINT'], ['phi'])
    k.ts('dve', f1, phi, 32.0, ALU.mult, ['phi'], ['f1'])
    f1r = S1.alloc("f1r", [128, 64], F32)
    k.ts('dve', f1r, f1, MAGIC, ALU.add, ['f1'], ['f1r'], s2=MAGIC, op1=ALU.subtract)
    k.tt('dve', f1, f1, f1r, ALU.subtract, ['f1', 'f1r'], ['f1'])
    k.act(SINT, SINT, AF.Sin, ['SINT'], ['SINT'], scale=TWO_PI)
    k.ts('dve', Wt, Wt, 0.25, ALU.add, ['Wt'], ['Wt'])
    k.ts('dve', Rt, Wt, MAGIC, ALU.add, ['Wt'], ['Rt'], s2=MAGIC, op1=ALU.subtract)
    k.tt('dve', COST, Wt, Rt, ALU.subtract, ['Wt', 'Rt'], ['COST'])
    k.act(COST, COST, AF.Sin, ['COST'], ['COST'], scale=TWO_PI)
    AR, AI = COST, SINT
    k.tt('dve', AR, MAG, COST, ALU.mult, ['MAG', 'COST'], ['COST'])
    k.tt('dve', AI, MAG, SINT, ALU.mult, ['MAG', 'SINT'], ['SINT'])
    k.cp('dve', rho8, MAG[:, :, 25], ['MAG'], ['rho8'])
    den = S1.alloc("den", [128, 64], F32)
    t1 = S1.alloc("t1", [128, 64], F32)
    t2 = S1.alloc("t2", [128, 64], F32)
    arm1 = S1.alloc("arm1", [128, 64], F32)
    zr = S1.alloc("zr", [128, 64], F32)
    zi = S1.alloc("zi", [128, 64], F32)
    k.tt('dve', den, lr, lr, ALU.mult, ['lr'], ['den'])
    k.tt('dve', t1, li, li, ALU.mult, ['li'], ['t1'])
    k.tt('dve', den, den, t1, ALU.add, ['den', 't1'], ['den'])
    k.recip(den, den, ['den'], ['den'])
    k.ts('dve', arm1, AR[:, :, 24], -1.0, ALU.add, ['COST'], ['arm1'])
    k.tt('dve', t1, arm1, lr, ALU.mult, ['arm1', 'lr'], ['t1'])
    k.tt('dve', t2, AI[:, :, 24], li, ALU.mult, ['SINT', 'li'], ['t2'])
    k.tt('dve', t1, t1, t2, ALU.add, ['t1', 't2'], ['t1'])
    k.tt('dve', zr, t1, den, ALU.mult, ['t1', 'den'], ['zr'])
    k.tt('dve', t1, AI[:, :, 24], lr, ALU.mult, ['SINT', 'lr'], ['t1'])
    k.tt('dve', t2, arm1, li, ALU.mult, ['arm1', 'li'], ['t2'])
    k.tt('dve', t1, t1, t2, ALU.subtract, ['t1', 't2'], ['t1'])
    k.tt('dve', zi, t1, den, ALU.mult, ['t1', 'den'], ['zi'])
    P0 = S1.alloc("P0", [128, 64, 16], F32)
    Q0 = S1.alloc("Q0", [128, 64, 16], F32)
    Pm = S1.alloc("Pm", [128, 64, 16], F32)
    Qm = S1.alloc("Qm", [128, 64, 16], F32)
    tb = S1.alloc("tb", [128, 64, 16], F32)
    bre = I["b_re"].rearrange("d g p h -> p (d g) h")
    bim = I["b_im"].rearrange("d g p h -> p (d g) h")
    k.dma('sp', P0[0:64], bre, 'c0', [], ['P0'])
    k.dma('sp', P0[64:128], bim, 'c0', [], ['P0'])
    k.dma('sp', Q0[0:64], bim, 'c0', [], ['Q0'])
    k.dma('sp', Q0[64:128], bre, 'c0', [], ['Q0'])
    sh = [128, 64, 16]
    zrb, zib = bc(zr.unsqueeze(2), sh), bc(zi.unsqueeze(2), sh)
    k.tt('dve', tb, Q0, zib, ALU.mult, ['Q0', 'zi'], ['tb'])
    k.tt('dve', Pm, P0, zrb, ALU.mult, ['P0', 'zr'], ['Pm'])
    k.stt('dve', Pm, tb, sgn1[:, 0:1], Pm, ALU.mult, ALU.add, ['tb', 'Pm', 'sgn1'], ['Pm'])
    k.tt('dve', tb, P0, zib, ALU.mult, ['P0', 'zi'], ['tb'])
    k.tt('dve', Qm, Q0, zrb, ALU.mult, ['Q0', 'zr'], ['Qm'])
    k.stt('dve', Qm, tb, sgn2[:, 0:1], Qm, ALU.mult, ALU.add, ['tb', 'Qm', 'sgn2'], ['Qm'])
    CT1 = S1.alloc("CT1", [128, 64, 16], F32)
    CT2 = S1.alloc("CT2", [128, 64, 16], F32)
    Cst = S1.alloc("Cst", [128, 8, 2, 64], F32)
    cre = I["c_re"].rearrange("d (gq g8) h p -> (g8 h) (d gq) p", g8=8)
    cim = I["c_im"].rearrange("d (gq g8) h p -> (g8 h) (d gq) p", g8=8)
    for which, CT in enumerate([CT1, CT2]):
        a_, b_ = (cre, cim) if which == 0 else (cim, cre)
        k.dma('sp', Cst[:, :, 0, :], a_, 'c0', [], ['Cst'])
        k.dma('sp', Cst[:, :, 1, :], b_, 'c0', [], ['Cst'])
        for blk in range(8):
            k.mm(PS(blk, 128), Cst[:, blk, :, :].rearrange("p a b -> p (a b)"), ident, True, True,
                 ['Cst', 'ident'], [PSR(blk)])
            k.cp('dve', CT[:, blk * 8:(blk + 1) * 8, :].rearrange("p a b -> p (a b)"), PS(blk, 128),
                 [PSR(blk)], ['CT%d' % which])
    dcol = S1.alloc("dcol", [128, 32], F32)
    for tau in range(8):
        k.dma('sp', dcol[tau * 16:(tau + 1) * 16, :], I["ssm_d"].rearrange("(g h) -> h g", h=16), 'c0', [], ['dcol'], slow=True)
    mkf = S1.alloc("mkf", [128, 128], F32)
    mkb = S1.alloc("mkb", [128, 128], F32)
    k.dma('sp', mkf, I["maskf"], 'c0', [], ['mkf'])
    k.dma('sp', mkb, I["maskb"], 'c0', [], ['mkb'])

    GB = 8
    sh4 = [128, GB, 8, 16]
    TW = {}
    for nm in ["WBt", "WBs", "Vt", "WC", "WC2"]:
        for d in range(2):
            TW[nm, d] = S1.alloc(f"{nm}{d}", sh4, F32)
    ta = S1.alloc("ta", sh4, F32)
    tb4 = S1.alloc("tb4", sh4, F32)
    wtile = [S1.alloc(f"wtile{i}", [128, 9, 128], BF16) for i in range(2)]
    kt1 = S1.alloc("kt1", [128, 128], F32)
    kt2 = S1.alloc("kt2", [128, 128], F32)
    for gb in range(32 // GB):
        for d in range(2):
            gs = slice(d * 32 + gb * GB, d * 32 + (gb + 1) * GB)

            def ER(s0):
                return bc(AR[:, gs, s0:s0 + 8].unsqueeze(3), sh4), bc(AI[:, gs, s0:s0 + 8].unsqueeze(3), sh4)

            def HB(t):
                return bc(t[:, gs, :].unsqueeze(2), sh4)
            rr = ['COST', 'SINT', 'Pm', 'Qm', 'CT0', 'CT1', 'sgn1', 'sgn2']
            er, ei = ER(0)
            o = TW["WBt", d]
            k.tt('dve', ta, er, HB(Pm), ALU.mult, rr, ['ta'])
            k.tt('dve', tb4, ei, HB(Qm), ALU.mult, rr, ['tb4'])
            k.stt('dve', o, tb4, sgn1[:, 0:1], ta, ALU.mult, ALU.add, ['ta', 'tb4'] + rr, ['WBt%d' % d])
            o = TW["WBs", d]
            k.tt('dve', ta, er, HB(Qm), ALU.mult, rr, ['ta'])
            k.tt('dve', tb4, ei, HB(Pm), ALU.mult, rr, ['tb4'])
            if d == 0:
                k.stt('dve', o, ta, sgn2[:, 0:1], tb4, ALU.mult, ALU.add, ['ta', 'tb4'] + rr, ['WBs%d' % d])
            else:
                k.stt('dve', o, ta, sgn1[:, 0:1], tb4, ALU.mult, ALU.subtract, ['ta', 'tb4'] + rr, ['WBs%d' % d])
            er, ei = ER(8)
            o = TW["Vt", d]
            k.tt('dve', ta, er, HB(Pm), ALU.mult, rr, ['ta'])
            k.tt('dve', tb4, ei, HB(Qm), ALU.mult, rr, ['tb4'])
            k.stt('dve', o, tb4, sgn1[:, 0:1], ta, ALU.mult, ALU.add, ['ta', 'tb4'] + rr, ['Vt%d' % d])
            er, ei = ER(16)
            o = TW["WC", d]
            k.tt('dve', ta, er, HB(CT1), ALU.mult, rr, ['ta'])
            k.tt('dve', tb4, ei, HB(CT2), ALU.mult, rr, ['tb4'])
            k.stt('dve', o, ta, sgn2[:, 0:1], tb4, ALU.mult, ALU.subtract, ['ta', 'tb4'] + rr, ['WC%d' % d])
            o = TW["WC2", d]
            k.tt('dve', ta, er, HB(CT2), ALU.mult, rr, ['ta'])
            k.tt('dve', tb4, ei, HB(CT1), ALU.mult, rr, ['tb4'])
            if d == 0:
                k.stt('dve', o, tb4, sgn1[:, 0:1], ta, ALU.mult, ALU.subtract, ['ta', 'tb4'] + rr, ['WC2%d' % d])
            else:
                k.stt('dve', o, tb4, sgn2[:, 0:1], ta, ALU.mult, ALU.add, ['ta', 'tb4'] + rr, ['WC2%d' % d])
        for gi in range(GB):
            g = gb * GB + gi
            wt = wtile[g % 2]
            wr = f'wtile{g % 2}'

            def V2(t):
                return t[:, gi, :, :].rearrange("p a b -> p (a b)")
            for d in range(2):
                k.mm(PS(0 + d, 128), V2(TW["WBt", d]), ident, True, True, ['WBt%d' % d, 'ident'], [PSR(0 + d)])
                k.cp('act', wt[:, d * 4 + 0, :], PS(0 + d, 128), [PSR(0 + d)], [wr])
                k.mm(PS(2 + d, 128), V2(TW["WBs", d]), ident, True, True, ['WBs%d' % d, 'ident'], [PSR(2 + d)])
                k.cp('act', wt[:, d * 4 + 1, :], PS(2 + d, 128), [PSR(2 + d)], [wr])
                k.cp('dve', wt[:, d * 4 + 2, :], V2(TW["WC", d]), ['WC%d' % d], [wr])
                k.cp('dve', wt[:, d * 4 + 3, :], V2(TW["WC2", d]), ['WC2%d' % d], [wr])
                k.mm(PS(4 + d, 128), V2(TW["Vt", d]), V2(TW["WC", d]), True, True, ['Vt%d' % d, 'WC%d' % d], [PSR(4 + d)])
            k.tt('dve', kt1, PS(4, 128), mkf, ALU.mult, [PSR(4), 'mkf'], ['kt1'])
            k.tt('dve', kt2, PS(5, 128), mkb, ALU.mult, [PSR(5), 'mkb'], ['kt2'])
            k.tt('dve', kt1, kt1, kt2, ALU.add, ['kt1', 'kt2'], ['kt1'])
            k.stt('dve', wt[:, 8, :], ident, dcol[:, g:g + 1], kt1, ALU.mult, ALU.add, ['kt1', 'ident', 'dcol'], [wr])
            k.dma('pool', ssmw[g], wt, wr, [wr], ['ssmw'])
    assert S1.cur <= DA.lo, (S1.cur, DA.lo)
    P.barrier_all()

    uid = [0]

    def XR(xr):
        return [f'{xr}.{c}' for c in range(8)]

    def ssq_step(xT, xr, sq, kc, n=512, bank_a=0):
        s_ = kc % 2
        if kc % 4 == 3:
            k.tt('dve', sq[s_][:, 0:n], xT[:, kc, 0:n], xT[:, kc, 0:n], ALU.mult, [f'{xr}.{kc}'], [f'sq{s_}'])
        else:
            k.act(sq[s_][:, 0:n], xT[:, kc, 0:n], AF.Square, [f'{xr}.{kc}'], [f'sq{s_}'])
        k.mm(PS(bank_a, n)[0:1, :], ones_col, sq[s_][:, 0:n], kc == 0, kc == 7, [f'sq{s_}', 'ones_col'], [PSR(bank_a)])

    def norm_to_hT(xT, xr, hT, hr, sq, rbc, rrow, n=512, bank_a=0, bank_b=1, fused=False):
        if not fused:
            for kc in range(8):
                ssq_step(xT, xr, sq, kc, n, bank_a)
        k.act(rrow[:, 0:n], PS(bank_a, n)[0:1, :], AF.Ln, [PSR(bank_a), 'epsc'], ['rrow'], scale=1.0 / D, bias=epsc[0:1, 0:1])
        k.act(rrow[:, 0:n], rrow[:, 0:n], AF.Exp, ['rrow'], ['rrow'], scale=-0.5)
        k.mm(PS(bank_b, n), ones_row, rrow[:, 0:n], True, True, ['rrow', 'ones_row'], [PSR(bank_b)])
        for kc in range(8):
            k.tt('dve', hT[:, kc, 0:n], xT[:, kc, 0:n], PS(bank_b, n), ALU.mult, [f'{xr}.{kc}', PSR(bank_b)], [f'{hr}{kc}'])

    def load_xT(xsrc, t0, xin, xT, xr, nblk=4):
        for b in range(nblk):
            s = b % len(xin)
            k.dma('sp', xin[s], xsrc[t0 + b * 128:t0 + (b + 1) * 128, :], f'xin{s}', [], [f'xin{s}'])
            for kc in range(8):
                k.mm(PS(kc)[:, b * 128:(b + 1) * 128], xin[s][:, kc * 128:(kc + 1) * 128], ident, True, True,
                     [f'xin{s}', 'ident'], [PSR(kc)])
        for kc in range(8):
            k.cp('act' if kc % 2 == 0 else 'dve', xT[:, kc, 0:nblk * 128], PS(kc, nblk * 128), [PSR(kc)], [f'{xr}.{kc}'])

    def ssm_phases(xsrc, L, LC, jobs=None, hi=None):
        nch = L // 8
        PA = Arena(nc, A0.cur, hi if hi is not None else A0.hi)
        U = PA.alloc("U", [128, 32, nch], BF16)
        AA = PA.sub()
        xin = [AA.alloc(f"xin{i}", [128, D], F32) for i in range(2)]
        xT = AA.alloc("xT", [128, 8, 512], F32)
        sq = [AA.alloc(f"sq{i}", [128, 512], BF16) for i in range(2)]
        rbc = AA.alloc("rbc", [128, 512], F32)
        rrow = AA.alloc("rrow", [1, 512], F32)
        hT1k = AA.alloc("hT1k", [128, 8, 1024], BF16)
        wa = AA.alloc("wa", [128, 8, 512], BF16)
        Zc = AA.alloc("Zc", [128, 32, 8, 16], BF16)
        k.dma('sp', wa, WS["w_in"][0], 'wa', ['WS_w_in'], ['wa'])
        for t1k in range(L // 1024):
            for half in range(2):
                t0 = t1k * 1024 + half * 512
                load_xT(xsrc, t0, xin, xT, 'xTa')
                norm_to_hT(xT, 'xTa', hT1k[:, :, half * 512:(half + 1) * 512], 'hT1k', sq, rbc, rrow)
            for tau in range(8):
                for kc in range(8):
                    k.mm(PS(tau), hT1k[:, kc, tau::8], wa[:, kc, :], kc == 0, kc == 7, [f'hT1k{kc}', 'wa'], [PSR(tau)])
                k.cp('act' if tau % 2 == 0 else 'dve', Zc[:, :, tau, :], PS(tau).rearrange("p (g h) -> p g h", h=16), [PSR(tau)], ['Zc'])
            for g in range(32):
                b = g // 4
                k.mm(PS(b)[:, (g % 4) * 128:(g % 4 + 1) * 128], Zc[:, g, :, :].rearrange("p a b -> p (a b)"), identb, True, True,
                     ['Zc', 'identb'], [PSR(b)])
                if g % 4 == 3:
                    k.cp('act' if b % 2 == 0 else 'dve', U[:, b * 4:(b + 1) * 4, t1k * 128:(t1k + 1) * 128],
                         PS(b).rearrange("p (g c) -> p g c", c=128), [PSR(b)], [f'U{b * 4 + q_}' for q_ in range(4)])
        P.barrier_all()
        BA = PA.sub()
        wset = [BA.alloc(f"wset{i}", [128, 9, 128], BF16) for i in range(2)]
        n1 = nch // 32
        tab = [BA.alloc(f"tab{i}", [128, 2, nch], F32) for i in range(2)]
        tw = BA.alloc("tw", [128, 2, nch], F32)
        tr = BA.alloc("tr", [128, 2, nch], F32)
        xt = [BA.alloc(f"xt{i}", [128, nch], F32) for i in range(2)]
        xt2 = BA.alloc("xt2", [128, nch], F32)
        St = BA.alloc("St", [128, nch], F32)
        P12 = [BA.alloc(f"P12{i}", [128, 2, nch], BF16) for i in range(2)]
        nb = (nch + 511) // 512
        units = [(g, d) for g in range(32) for d in range(2)]

        def wsel(g):
            return wset[g % 2], f'wset{g % 2}'

        def tabgen(g, d):
            gd = d * 32 + g
            w0 = tw[:, 0, :].rearrange("p (a b) -> p a b", b=32)
            k.ts('dve', w0, iota[:, :].unsqueeze(1).to_broadcast([128, n1, 32]), phi[:, gd:gd + 1], ALU.mult,
                 ['iota', 'phi'], ['tw'])
            k.stt('dve', w0, iota[:, 0:n1].unsqueeze(2).to_broadcast([128, n1, 32]), f1[:, gd:gd + 1], w0,
                  ALU.mult, ALU.add, ['iota', 'f1', 'tw'], ['tw'])
            k.ts('dve', tw[:, 1, :], tw[:, 0, :], 0.25, ALU.add, ['tw'], ['tw'])
            k.ts('dve', tr, tw, MAGIC, ALU.add, ['tw'], ['tr'], s2=MAGIC, op1=ALU.subtract)
            k.tt('dve', tr, tw, tr, ALU.subtract, ['tw', 'tr'], ['tr'])
            k.act(tab[d], tr, AF.Sin, ['tr'], [f'tab{d}'], scale=TWO_PI)

        def xmm(g, d):
            ws_, wsr = wsel(g)
            if d == 0:
                k.dma('sp', ws_, ssmw[g], wsr, ['ssmw'], [wsr])
            for bi in range(nb):
                cs = slice(bi * 512, min(nch, (bi + 1) * 512))
                n = cs.stop - cs.start
                k.mm(PS(bi, n), ws_[:, d * 4 + 0, :], U[:, g, cs], True, True, [wsr, f'U{g}'], [PSR(bi)])
                k.mm(PS(nb + bi, n), ws_[:, d * 4 + 1, :], U[:, g, cs], True, True, [wsr, f'U{g}'], [PSR(nb + bi)])

        def rot(g, d):
            tb_, tbr, x_, xr_ = tab[d], f'tab{d}', xt[d], f'xt{d}'
            for bi in range(nb):
                cs = slice(bi * 512, min(nch, (bi + 1) * 512))
                n = cs.stop - cs.start
                k.tt('dve', x_[:, cs], PS(bi, n), tb_[:, 1, cs], ALU.mult, [PSR(bi), tbr], [xr_])
                k.tt('dve', xt2[:, cs], PS(nb + bi, n), tb_[:, 0, cs], ALU.mult, [PSR(nb + bi), tbr], ['xt2'])
            k.tt('dve', x_, x_, xt2, ALU.add, [xr_, 'xt2'], [xr_])

        def scn(g, d):
            gd = d * 32 + g
            tb_, tbr, x_, xr_ = tab[d], f'tab{d}', xt[d], f'xt{d}'
            rb = rho8[:, gd:gd + 1].to_broadcast([128, nch])
            if d == 0:
                k.scan(St, rb, x_, [xr_, 'rho8'], ['St'])
            else:
                k.scan(St[:, ::-1], rb, x_[:, ::-1], [xr_, 'rho8'], ['St'])
            pp = P12[d]
            k.tt('dve', pp[:, 0, :], St, tb_[:, 1, :], ALU.mult, ['St', tbr], [f'P12{d}'])
            k.tt('dve', pp[:, 1, :], St, tb_[:, 0, :], ALU.mult, ['St', tbr], [f'P12{d}'])

        def ymm(g):
            ws_, wsr = wsel(g)
            for bi in range(nb):
                c0 = bi * 512
                c1 = min(nch, c0 + 512)
                yb = 4 + bi
                rdeps = [wsr, f'U{g}', 'P120', 'P121']
                k.mm(PS(yb, c1 - c0), ws_[:, 8, :], U[:, g, c0:c1], True, False, rdeps, [PSR(yb)])
                lo = max(c0, 1)
                for wi in (2, 3):
                    k.mm(psum[:, yb * 512 + (lo - c0):yb * 512 + (c1 - c0)], ws_[:, wi, :],
                         P12[0][:, wi - 2, lo - 1:c1 - 1], False, False, rdeps, [PSR(yb)])
                hi = min(c1, nch - 1)
                for wi in (6, 7):
                    k.mm(psum[:, yb * 512:yb * 512 + (hi - c0)], ws_[:, wi, :],
                         P12[1][:, wi - 6, c0 + 1:hi + 1], False, wi == 7, rdeps, [PSR(yb)])
                k.act(U[:, g, c0:c1], PS(yb, c1 - c0), AF.Gelu_apprx_tanh, [PSR(yb)], [f'U{g}'])

        tabgen(*units[0])
        for ui, (g, d) in enumerate(units):
            xmm(g, d)
            if ui + 1 < len(units):
                tabgen(*units[ui + 1])
            rot(g, d)
            scn(g, d)
            if d == 1:
                ymm(g)
            if jobs:
                jobs.pop(0)()
        while jobs:
            jobs.pop(0)()
        P.barrier_all()
        CA = PA.sub()
        Gc = CA.alloc("Gc", [128, 8, 512], BF16)
        gTt = [CA.alloc(f"gTt{i}", [128, 4, 1024], BF16) for i in range(2)]
        for t1k in range(LC // 1024):
            gt = gTt[t1k % 2]
            gtr = f'gTt{t1k % 2}'
            for g in range(32):
                b = g // 4
                k.mm(PS(b)[:, (g % 4) * 128:(g % 4 + 1) * 128], U[:, g, t1k * 128:(t1k + 1) * 128], identb, True, True,
                     [f'U{g}', 'identb'], [PSR(b)])
                if g % 4 == 3:
                    src = PS(b).rearrange("p (g t h) -> p g t h", g=4, t=8)
                    dst = Gc[:, :, b * 64:(b + 1) * 64].rearrange("p t (g h) -> p g t h", g=4)
                    k.cp('act' if b % 2 == 0 else 'dve', dst, src, [PSR(b)], ['Gc'])
            for j in range(4):
                for th in range(2):
                    b = j * 2 + th
                    for ti in range(4):
                        tau = th * 4 + ti
                        k.mm(PS(b)[:, ti * 128:(ti + 1) * 128], Gc[:, tau, j * 128:(j + 1) * 128], identb, True, True,
                             ['Gc', 'identb'], [PSR(b)])
                    src = PS(b).rearrange("p (t c) -> p t c", t=4)
                    dst = gt[:, j, :].rearrange("p (c t) -> p t c", t=8)[:, th * 4:(th + 1) * 4, :]
                    k.cp('act' if b % 2 == 0 else 'dve', dst, src, [PSR(b)], [gtr])
            k.dma('pool', gTs[:, :, t1k * 1024:(t1k + 1) * 1024].rearrange("j p t -> p j t"), gt, gtr, [gtr], ['gTs'])
        P.barrier_all()

    def main_seq(xsrc, ydst, L, OWN):
        MA = A0.sub()
        R = 5
        ring = [MA.alloc(f"ring{i}", [128, 4096], BF16) for i in range(R)]
        biasT = MA.alloc("biasT", [128, 16, 3, 128], BF16)
        xin = [MA.alloc(f"xin{i}", [128, D], F32) for i in range(4)]
        xouts = [xin[2], xin[3]]
        xb = [MA.alloc(f"xb{i}", [128, 8, 512], F32) for i in range(2)]
        hT = MA.alloc("hT", [128, 8, 512], BF16)
        sq = [MA.alloc(f"sq{i}", [128, 512], BF16) for i in range(2)]
        mst = MA.alloc("mst", [128, 3, 128], F32)
        rbc = None
        rrow = MA.alloc("rrow", [1, 512], F32)
        gT = MA.alloc("gT", [128, 4, 512], BF16)
        sgate = [MA.alloc(f"sgate{i}", [128, 512], F32) for i in range(2)]
        qT = [MA.alloc(f"qT{i}", [128, 8, 512], BF16) for i in range(2)]
        NR = 12
        kT = MA.alloc("kT", [128, 2, NR * 128], BF16)
        Vr = MA.alloc("Vr", [128, NR, 4, 65], BF16)
        stat = MA.alloc("stat", [128, 64], F32)
        GX = MA.sub()
        ubT = GX.alloc("ubT", [128, 4, 512], BF16)
        vgs = [GX.alloc(f"vg{i}", [128, 512], F32) for i in range(2)]
        vsqs = [GX.alloc(f"vsq{i}", [128, 512], F32) for i in range(2)]
        vns = [GX.alloc(f"vn{i}", [128, 512], BF16) for i in range(2)]
        ybT = GX.alloc("ybT", [128, 4, 512], BF16)
        sg = GX.alloc("sg", [128, 512], F32)
        yaT = GX.alloc("yaT", [128, 4, 512], BF16)
        GXq = MA.sub()
        qk = GXq.alloc("qk", [128, 1280], F32)
        qsq = GXq.alloc("qsq", [128, 1280], F32)
        qn = GXq.alloc("qn", [128, 1280], BF16)
        MA.cur = max(GX.cur, GXq.cur)
        GY = MA.sub()
        actT = GY.alloc("actT", [128, NJ, 512], BF16)
        GYa = MA.sub()
        PT = [GYa.alloc(f"PT{i}", [128, 3, 512], BF16) for i in range(2)]
        on = GYa.alloc("on", [128, 1024], BF16)
        ssb = [GYa.alloc(f"ssb{i}", [128, 512], F32) for i in range(2)]
        oT = GYa.alloc("oT", [128, 8, 512], BF16)
        MA.cur = max(GY.cur, GYa.cur)
        L0N = ['ubT', 'vg0', 'vg1', 'vsq0', 'vsq1', 'vn0', 'vn1', 'ybT', 'sg', 'yaT']
        QKN = ['qk', 'qsq', 'qn']
        ATN = ['PT0', 'PT1', 'on', 'oT', 'ssb0', 'ssb1']
        NBseq = L // 128

        mstage = mst
        k.dma('sp', mstage, I["maskc"], 'mstd', [], ['mst'])
        bst = xb[0].rearrange("p a b -> p (a b)")[:, 0:2048].rearrange("p (h q) -> p h q", h=16)
        for r in range(3):
            k.dma('sp', bst, I["biasg"][:, :, r, :], 'bst', [], XR('xb0'))
            k.tt('dve', biasT[:, :, r, :], bst, mstage[:, r, :].unsqueeze(1).to_broadcast([128, 16, 128]), ALU.add,
                 XR('xb0') + ['mst'], ['biasT'])
        k.memset('dve', Vr[:, :, :, 64:65], 1.0, ['Vr'])

        ridx = [0]

        def wload(src, dreg, view):
            s = ridx[0] % R
            ridx[0] += 1
            tot = int(np.prod(src.shape[1:]))
            dst = ring[s][:, 0:tot]
            if view is not None:
                dst = dst.rearrange(view[0], **view[1])
            k.dma('sp', dst, src, f'ring{s}', [dreg], [f'ring{s}'])
            return dst, f'ring{s}'

        def wpiece(nm, pc, KC=8, ncol=512):
            src = WS[nm][pc]
            if ncol != 512:
                src = src[:, :, 0:ncol]
            return wload(src, 'WS_' + nm, ("p (c n) -> p c n", dict(n=ncol)))

        def ffn(l, xT, xr, n):
            P.alias(['actT'], ATN)
            norm_to_hT(xT, xr, hT, 'hT', sq, rbc, rrow, n=n, fused=True)
            jcount = 0
            for pc in range(6):
                ncol = 512 if pc < 5 else 256
                wg_, wgr = wpiece(f"wg{l}", pc, ncol=ncol)
                wu_, wur = wpiece(f"wu{l}", pc, ncol=ncol)
                for jj in range(ncol // 128):
                    j = pc * 4 + jj
                    ba, bb = (2, 3) if jcount % 2 == 0 else (4, 5)
                    s = jcount % 2
                    jcount += 1
                    for kc in range(8):
                        k.mm(PS(ba, n), wg_[:, kc, jj * 128:(jj + 1) * 128], hT[:, kc, 0:n], kc == 0, kc == 7, [wgr, f'hT{kc}'], [PSR(ba)])
                    for kc in range(8):
                        k.mm(PS(bb, n), wu_[:, kc, jj * 128:(jj + 1) * 128], hT[:, kc, 0:n], kc == 0, kc == 7, [wur, f'hT{kc}'], [PSR(bb)])
                    k.act(sgate[s][:, 0:n], PS(ba, n), AF.Silu, [PSR(ba)], [f'sgate{s}'])
                    k.tt('dve', actT[:, j, 0:n], sgate[s][:, 0:n], PS(bb, n), ALU.mult, [f'sgate{s}', PSR(bb)], ['actT'])
            for fc in range(8):
                wd_, wdr = wload(WS[f"wd{l}"][fc], f'WS_wd{l}', ("p (j n) -> p j n", dict(n=128)))
                bo = 6 + fc % 2
                for j in range(NJ):
                    k.mm(PS(bo, n), wd_[:, j, :], actT[:, j, 0:n], j == 0, j == NJ - 1, [wdr, 'actT'], [PSR(bo)])
                k.tt('dve', xT[:, fc, 0:n], xT[:, fc, 0:n], PS(bo, n), ALU.add, [f'{xr}.{fc}', PSR(bo)], [f'{xr}.{fc}'])
                if l == 0 and fc >= 1:
                    ssq_step(xT, xr, sq, fc - 1, n)
            if l == 0:
                ssq_step(xT, xr, sq, 7, n)

        def L0(i, xT, xr, n):
            t0 = i * 512
            nblk = n // 128
            P.alias(L0N, QKN)
            load_xT(xsrc, t0, xin, xT, xr, nblk=nblk)
            k.dma('sp', gT[:, :, 0:n], gTs[:, :, t0:t0 + n].rearrange("j p t -> p j t"), 'gT', ['gTs'], ['gT'])
            norm_to_hT(xT, xr, hT, 'hT', sq, rbc, rrow, n=n)
            w1, w1r = wpiece("w_in", 1)
            for jj in range(4):
                bk = 2 + jj % 2
                for kc in range(8):
                    k.mm(PS(bk, n), w1[:, kc, jj * 128:(jj + 1) * 128], hT[:, kc, 0:n], kc == 0, kc == 7, [w1r, f'hT{kc}'], [PSR(bk)])
                k.act(ubT[:, jj, 0:n], PS(bk, n), AF.Gelu_apprx_tanh, [PSR(bk)], ['ubT'])
            w2, w2r = wpiece("w_in", 2)
            for b in range(nblk):
                bk = 4 + b
                ts_ = slice(b * 128, (b + 1) * 128)
                for kc in range(8):
                    k.mm(PS(bk), hT[:, kc, ts_], w2[:, kc, :], kc == 0, kc == 7, [w2r, f'hT{kc}'], [PSR(bk)])

            def chainA(b):
                bk = 4 + b
                q_ = b % 2
                st_ = stat[:, 48 + 4 * b:52 + 4 * b]
                k.act(vgs[q_], PS(bk), AF.Gelu_apprx_tanh, [PSR(bk)], [f'vg{q_}'])
                k.act(vsqs[q_], vgs[q_], AF.Square, [f'vg{q_}'], [f'vsq{q_}'])
                k.reduce(st_, vsqs[q_].rearrange("p (h d) -> p h d", h=4), [f'vsq{q_}'], [f'vst{b}'])

            def chainB(b):
                q_ = b % 2
                st_ = stat[:, 48 + 4 * b:52 + 4 * b]
                k.act(st_, st_, AF.Ln, [f'vst{b}', 'epsc'], [f'vst{b}'], scale=1.0 / 128, bias=epsc[:, 0:1])
                k.act(st_, st_, AF.Exp, [f'vst{b}'], [f'vst{b}'], scale=-0.5)
                for h in range(4):
                    hs = slice(h * 128, (h + 1) * 128)
                    k.stt('dve', vns[q_][:, hs], vgs[q_][:, hs], st_[:, h:h + 1], gmn[:, hs], ALU.mult, ALU.mult,
                          [f'vg{q_}', f'vst{b}', 'gmn'], [f'vn{q_}'])

            def mixed(b):
                bm = 4 + b
                q_ = b % 2
                ts_ = slice(b * 128, (b + 1) * 128)
                for h in range(4):
                    hs = slice(h * 128, (h + 1) * 128)
                    k.mm(PS(bm)[:, hs], vns[q_][:, hs], wsT[:, h, :], True, True, [f'vn{q_}', 'wsT'], [PSR(bm)])
                v3 = vsqs[q_].rearrange("p (h d) -> p h d", h=4)
                k.tt('dve', v3, PS(bm).rearrange("p (h d) -> p h d", h=4), bsT, ALU.add, [PSR(bm), 'bsT'], [f'vsq{q_}'])
                k.tt('dve', ybT[:, :, ts_], v3, ubT[:, :, ts_], ALU.mult, [f'vsq{q_}', 'ubT'], ['ybT'])

            for b0 in range(0, nblk, 2):
                bs = [b for b in (b0, b0 + 1) if b < nblk]
                for b in bs:
                    chainA(b)
                for b in bs:
                    chainB(b)
                for b in bs:
                    mixed(b)
            wl, wlr = wload(WS["glu_w"][0], 'WS_glu_w', ("p (c n) -> p c n", dict(n=512)))
            for fo in range(4):
                bk = 2 + fo % 2
                for kc in range(4):
                    k.mm(PS(bk, n), wl[:, kc, fo * 128:(fo + 1) * 128], gT[:, kc, 0:n], kc == 0, kc == 3, [wlr, 'gT'], [PSR(bk)])
                k.act(sg[:, 0:n], PS(bk, n), AF.Sigmoid, [PSR(bk), 'glub'], ['sg'], bias=glub[:, fo:fo + 1])
                k.tt('dve', yaT[:, fo, 0:n], gT[:, fo, 0:n], sg[:, 0:n], ALU.mult, ['gT', 'sg'], ['yaT'])
            for pc in range(2):
                wo_, wor = wpiece("w_out", pc)
                for fci in range(4):
                    fc = pc * 4 + fci
                    bo = 6 + fc % 2
                    for kc in range(8):
                        rhs = yaT[:, kc, 0:n] if kc < 4 else ybT[:, kc - 4, 0:n]
                        k.mm(PS(bo, n), wo_[:, kc, fci * 128:(fci + 1) * 128], rhs, kc == 0, kc == 7, [wor, 'yaT', 'ybT'], [PSR(bo)])
                    k.tt('dve', xT[:, fc, 0:n], xT[:, fc, 0:n], PS(bo, n), ALU.add, [f'{xr}.{fc}', PSR(bo)], [f'{xr}.{fc}'])
                    if fc >= 1:
                        ssq_step(xT, xr, sq, fc - 1, n)
            ssq_step(xT, xr, sq, 7, n)
            if debug and n == 512:
                k.dma('pool', dbg_x1[i], xT, 'dbg', XR(xr), [])
            ffn(0, xT, xr, n)
            if debug and n == 512:
                k.dma('pool', dbg_x2[i], xT, 'dbg', XR(xr), [])

        def QKV(i, xT, xr, n, qTs, qTr):
            nblk = n // 128
            P.alias(QKN, L0N)
            norm_to_hT(xT, xr, hT, 'hT', sq, rbc, rrow, n=n, fused=True)
            wq = [wpiece("w_qkv", pc) for pc in range(3)]

            def qmm(b):
                ts_ = slice(b * 128, (b + 1) * 128)
                for pc in range(3):
                    for kc in range(8):
                        k.mm(PS(1 + pc), hT[:, kc, ts_], wq[pc][0][:, kc, :], kc == 0, kc == 7, [wq[pc][1], f'hT{kc}'], [PSR(1 + pc)])

            def qchain(b):
                B = i * 4 + b
                slot = B % NR
                k.cp('act', qk[:, 0:512], PS(1), [PSR(1)], ['qk'])
                k.cp('act', qk[:, 512:1024], PS(2), [PSR(2)], ['qk'])
                k.cp('act', qk[:, 1024:1280], PS(3)[:, 0:256], [PSR(3)], ['qk'])
                k.cp('act', Vr[:, slot, :, 0:64], PS(3)[:, 256:512].rearrange("p (v d) -> p v d", v=4), [PSR(3)], ['Vr'])
                k.act(qsq, qk, AF.Square, ['qk'], ['qsq'])
                k.reduce(stat[:, 8:28], qsq.rearrange("p (h d) -> p h d", d=64), ['qsq'], ['stat'])
                k.act(stat[:, 8:28], stat[:, 8:28], AF.Ln, ['stat', 'epsc'], ['stat'], scale=1.0 / 64, bias=epsc[:, 0:1])
                k.act(stat[:, 8:28], stat[:, 8:28], AF.Exp, ['stat'], ['stat'], scale=-0.5)
                for A in range(2):
                    i0_ = qk[:, A * 512:(A + 1) * 512].rearrange("p (b c d) -> p b c d", b=2, c=4)
                    i1_ = stat[:, 8 + A * 8:16 + A * 8].rearrange("p (b c) -> p b c", b=2).unsqueeze(3).to_broadcast([128, 2, 4, 64])
                    o_ = qn[:, A * 512:(A + 1) * 512].rearrange("p (c b d) -> p b c d", c=4, b=2)
                    k.tt('dve', o_, i0_, i1_, ALU.mult, ['qk', 'stat'], ['qn'])
                k.tt('dve', qn[:, 1024:1280].rearrange("p (h d) -> p h d", d=64), qk[:, 1024:1280].rearrange("p (h d) -> p h d", d=64),
                     stat[:, 24:28].unsqueeze(2).to_broadcast([128, 4, 64]), ALU.mult, ['qk', 'stat'], ['qn'])

            def qtr(b):
                B = i * 4 + b
                slot = B % NR
                ts_ = slice(b * 128, (b + 1) * 128)
                for half in range(2):
                    bk = 4 + half
                    for pi in range(4):
                        pr = half * 4 + pi
                        k.mm(PS(bk)[:, pi * 128:(pi + 1) * 128], qn[:, pr * 128:(pr + 1) * 128], identb, True, True, ['qn', 'identb'], [PSR(bk)])
                    k.cp('dve' if half == 0 else 'act', qTs[:, half * 4:(half + 1) * 4, ts_], PS(bk).rearrange("p (a t) -> p a t", a=4),
                         [PSR(bk)], [qTr])
                for kp in range(2):
                    k.mm(PS(6)[:, kp * 128:(kp + 1) * 128], qn[:, 1024 + kp * 128:1024 + (kp + 1) * 128], identb, True, True,
                         ['qn', 'identb'], [PSR(6)])
                k.ts('dve', kT[:, :, slot * 128:(slot + 1) * 128], PS(6)[:, 0:256].rearrange("p (a t) -> p a t", a=2), gqk[:, 0:1], ALU.mult,
                     [PSR(6), 'gqk'], ['kT'])

            qmm(0)
            for b in range(nblk):
                qchain(b)
                if b + 1 < nblk:
                    qmm(b + 1)
                qtr(b)

        def L1(i, xT, xr, qTs, qTr):
            P.alias(ATN, ['actT'])
            n = 512
            sbc = [0]

            def rs_of(qb):
                QB = i * 4 + qb
                return [r for r in range(3) if 0 <= QB - 1 + r < NBseq]

            def scores(qb, kv):
                QB = i * 4 + qb
                qs_ = slice(qb * 128, (qb + 1) * 128)
                hs = slice((kv % 2) * 64, (kv % 2) * 64 + 64)
                kp = kv // 2
                pt = PT[kv % 2]
                ptr = f'PT{kv % 2}'
                for r in rs_of(qb):
                    slot = (QB - 1 + r) % NR
                    sb = 1 + (sbc[0] % 3)
                    sbc[0] += 1
                    k.mm(PS(sb), kT[hs, kp, slot * 128:(slot + 1) * 128], qTs[hs, kp * 4:kp * 4 + 4, qs_], True, True,
                         ['kT', qTr], [PSR(sb)])
                    sq_ = (sbc[0] - 1) % 2
                    k.tt('dve', ssb[sq_].rearrange("p (h q) -> p h q", h=4), PS(sb).rearrange("p (h q) -> p h q", h=4),
                         biasT[:, kv * 4:(kv + 1) * 4, r, :], ALU.add, [PSR(sb), 'biasT'], [f'ssb{sq_}'])
                    k.act(pt[:, r, :], ssb[sq_], AF.Exp, [f'ssb{sq_}'], [ptr])

            def pv(qb, kv):
                QB = i * 4 + qb
                rs = rs_of(qb)
                pt = PT[kv % 2]
                ptr = f'PT{kv % 2}'
                for hh in range(4):
                    h = kv * 4 + hh
                    col = (h // 7) * 512 + (h % 7) * 65
                    for r in rs:
                        slot = (QB - 1 + r) % NR
                        k.mm(psum[:, 5 * 512 + col:5 * 512 + col + 65], pt[:, r, hh * 128:(hh + 1) * 128], Vr[:, slot, kv, :],
                             r == rs[0], r == rs[-1], [ptr, 'Vr'], [PSR(5 + h // 7)])

            def fin(qb):
                qs_ = slice(qb * 128, (qb + 1) * 128)
                for bg, (h0, nh) in enumerate([(0, 7), (7, 7), (14, 2)]):
                    ov = PS(5 + bg)[:, 0:nh * 65].rearrange("p (h e) -> p h e", e=65)
                    k.tt('dve', stat[:, 32 + h0:32 + h0 + nh].unsqueeze(2), ov[:, :, 64:65], esink[:, h0:h0 + nh].unsqueeze(2), ALU.add,
                         [PSR(5 + bg), 'esink'], ['stat2'])
                k.recip(stat[:, 32:48], stat[:, 32:48], ['stat2'], ['stat2'])
                for bg, (h0, nh) in enumerate([(0, 7), (7, 7), (14, 2)]):
                    ov = PS(5 + bg)[:, 0:nh * 65].rearrange("p (h e) -> p h e", e=65)
                    k.tt('dve', on[:, h0 * 64:(h0 + nh) * 64].rearrange("p (h d) -> p h d", d=64), ov[:, :, 0:64],
                         stat[:, 32 + h0:32 + h0 + nh].unsqueeze(2).to_broadcast([128, nh, 64]), ALU.mult, [PSR(5 + bg), 'stat2'], ['on'])
                for half in range(2):
                    bk = 0 if half == 0 else 4
                    for ci in range(4):
                        kc = half * 4 + ci
                        k.mm(PS(bk)[:, ci * 128:(ci + 1) * 128], on[:, kc * 128:(kc + 1) * 128], identb, True, True, ['on', 'identb'], [PSR(bk)])
                    k.cp('act' if half == 0 else 'dve', oT[:, half * 4:(half + 1) * 4, qs_], PS(bk).rearrange("p (a t) -> p a t", a=4),
                         [PSR(bk)], ['oT'])

            aunits = [(qb, kv) for qb in range(4) for kv in range(4)]
            scores(*aunits[0])
            for ui, (qb, kv) in enumerate(aunits):
                if ui + 1 < len(aunits):
                    scores(*aunits[ui + 1])
                pv(qb, kv)
                if kv == 3:
                    fin(qb)
            if debug:
                k.dma('pool', dbg_oT[i], oT, 'dbg', ['oT'], [])
            for pc in range(2):
                wo_, wor = wpiece("w_o", pc)
                for fci in range(4):
                    fc = pc * 4 + fci
                    bo = 6 + fc % 2
                    for kc in range(8):
                        k.mm(PS(bo, n), wo_[:, kc, fci * 128:(fci + 1) * 128], oT[:, kc, :], kc == 0, kc == 7, [wor, 'oT'], [PSR(bo)])
                    k.tt('dve', xT[:, fc, :], xT[:, fc, :], PS(bo, n), ALU.add, [f'{xr}.{fc}', PSR(bo)], [f'{xr}.{fc}'])
                    if fc >= 1:
                        ssq_step(xT, xr, sq, fc - 1, n)
            ssq_step(xT, xr, sq, 7, n)
            if debug:
                k.dma('pool', dbg_x3[i], xT, 'dbg', XR(xr), [])
            ffn(1, xT, xr, n)
            for b in range(4):
                ts_ = slice(b * 128, (b + 1) * 128)
                for kc in range(8):
                    bk = 1 + (b % 2) * 2 + kc // 4
                    k.mm(PS(bk)[:, (kc % 4) * 128:(kc % 4 + 1) * 128], xT[:, kc, ts_], ident, True, True, [f'{xr}.{kc}', 'ident'], [PSR(bk)])
                b0 = 1 + (b % 2) * 2
                xo, xor_ = xouts[b % 2], f'xin{2 + b % 2}'
                k.cp('act', xo[:, 0:512], PS(b0), [PSR(b0)], [xor_])
                k.cp('dve', xo[:, 512:1024], PS(b0 + 1), [PSR(b0 + 1)], [xor_])
                k.dma('pool', ydst[i * 512 + b * 128:i * 512 + (b + 1) * 128, :], xo, xor_, [xor_], [])

        nt = OWN // 512
        for i in range(nt):
            s = i % 2
            L0(i, xb[s], f'xb{s}', 512)
            QKV(i, xb[s], f'xb{s}', 512, qT[s], f'qT{s}')
            if i >= 1:
                L1(i - 1, xb[1 - s], f'xb{1 - s}', qT[1 - s], f'qT{1 - s}')
        if OWN < L:
            s = nt % 2
            L0(nt, xb[s], f'xb{s}', 128)
            QKV(nt, xb[s], f'xb{s}', 128, qT[s], f'qT{s}')
        s = (nt - 1) % 2
        L1(nt - 1, xb[s], f'xb{s}', qT[s], f'qT{s}')
        if debug:
            k.dma('pool', dbg_qT, qT[s], 'dbg', [f'qT{s}'], [])
            k.dma('pool', dbg_kT, kT, 'dbg', ['kT'], [])
            k.dma('pool', dbg_V, Vr, 'dbg', ['Vr'], [])
            k.dma('pool', dbg_bias, biasT, 'dbg', ['biasT'], [])
        P.barrier_all()

    ssm_phases(I["xs"], LS, LS, jobs=cast_jobs, hi=DA.lo)
    main_seq(I["xs"], ys, LS, LS)
    lc = ((OWNP + 128 + 1023) // 1024) * 1024 if OWNP < LP else LP
    ssm_phases(I["xp"], LP, min(lc, LP))
    main_seq(I["xp"], yp, LP, OWNP)
    block = stack.enter_context(nc.Block())
    P.emit(block)
    stack.close()
    return nc


def _rel_bucket_np(rel):
    half, max_exact = 16, 8
    ret = (rel > 0).astype(np.int32) * half
    n = np.abs(rel)
    nf = np.maximum(n, 1).astype(np.float32)
    large = max_exact + (np.log(nf / np.float32(max_exact)) / np.float32(np.log(128 / max_exact))
                         * np.float32(half - max_exact)).astype(np.int32)
    large = np.minimum(large, half - 1)
    return ret + np.where(n < max_exact, n, large)


def _consts():
    c = {}
    c["ident"] = np.eye(128, dtype=np.float32)
    tt = np.arange(128) // 16
    c["maskf"] = (tt[:, None] <= tt[None, :]).astype(np.float32)
    c["maskb"] = (tt[:, None] >= tt[None, :]).astype(np.float32)
    kexp = np.zeros((64, NSLOT), np.float32)
    for d in range(2):
        for tau in range(8):
            e = 7 - tau if d == 0 else tau
            kexp[d * 32:(d + 1) * 32, tau] = e
            kexp[d * 32:(d + 1) * 32, 8 + tau] = e - 8
            kexp[d * 32:(d + 1) * 32, 16 + tau] = tau + 1 if d == 0 else 8 - tau
    kexp[:, 24] = 1
    kexp[:, 25] = 8
    c["kexp"] = kexp.reshape(-1)
    c["iota32"] = np.arange(32, dtype=np.float32)
    c["sgn1"] = np.concatenate([-np.ones(64), np.ones(64)]).astype(np.float32)
    c["sgn2"] = -c["sgn1"]
    kk = np.arange(128)[:, None, None]
    r = np.arange(3)[None, :, None]
    qq = np.arange(128)[None, None, :]
    rel = (r - 1) * 128 + kk - qq
    c["maskc"] = np.where(np.abs(rel) <= 128, 0.0, -80.0).astype(np.float32)
    c["_rel"] = rel
    return c


def _core_inputs(inp, core, xs, xp, consts):
    rev = (core % 2 == 1)
    f = (lambda a: np.ascontiguousarray(a[::-1])) if rev else (lambda a: np.ascontiguousarray(a))
    m = {}
    m["xs"] = f(xs)
    m["xp"] = f(xp)
    m["norm_mix"] = inp["norm_mix"]
    m["norm_ffn"] = inp["norm_ffn"]
    m["w_in"] = inp["w_in_even"][0]
    m["glu_w"] = inp["glu_w"][0]
    m["glu_b"] = inp["glu_b"][0]
    m["w_out"] = inp["w_out_even"][0]
    m["w_qkv"] = inp["w_qkv"][0]
    m["w_o"] = inp["w_o"][0]
    for l in range(2):
        m[f"wg{l}"] = inp["ffn_w_gate"][l]
        m[f"wu{l}"] = inp["ffn_w_up"][l]
        m[f"wd{l}"] = inp["ffn_w_down"][l]
    dsw = (lambda a: np.ascontiguousarray(a[::-1])) if rev else (lambda a: a)
    m["lam_re"] = dsw(inp["ssm_lam_re"][0])
    m["lam_im"] = dsw(inp["ssm_lam_im"][0])
    m["log_dt"] = dsw(inp["ssm_log_dt"][0])
    m["b_re"] = dsw(inp["ssm_b_re"][0])
    m["b_im"] = dsw(inp["ssm_b_im"][0])
    m["c_re"] = dsw(inp["ssm_c_re"][0])
    m["c_im"] = dsw(inp["ssm_c_im"][0])
    m["ssm_d"] = inp["ssm_d"][0]
    m["gm_norm"] = inp["gm_norm"][0]
    ws, bs = inp["gm_w_s"][0], inp["gm_b_s"][0]
    if rev:
        ws, bs = ws[:, ::-1, ::-1], bs[:, ::-1]
    m["gm_w_s"] = np.ascontiguousarray(ws)
    m["gm_b_s"] = np.ascontiguousarray(bs)
    m["q_norm"] = inp["q_norm"][0]
    m["k_norm"] = inp["k_norm"][0]
    m["attn_sink"] = inp["attn_sink"][0]
    rel = consts["_rel"]
    bucket = _rel_bucket_np(-rel if rev else rel)
    bg = inp["rel_table"][bucket]
    m["biasg"] = np.ascontiguousarray(bg.transpose(0, 3, 1, 2))
    for kname in ["ident", "maskf", "maskb", "kexp", "iota32", "sgn1", "sgn2", "maskc"]:
        m[kname] = consts[kname]
    return {k_: np.ascontiguousarray(v, dtype=np.float32) for k_, v in m.items()}


_NC_CACHE = {}


def run_cores(inp, xs_list, xp_list, LS, LP, OWNP, ncores, debug=False):
    key = (LS, LP, OWNP, debug)
    if key not in _NC_CACHE:
        _NC_CACHE[key] = build(LS, LP, OWNP, debug=debug)
    nc = _NC_CACHE[key]
    consts = _consts()
    in_maps = [_core_inputs(inp, c, xs_list[c], xp_list[c], consts) for c in range(ncores)]
    res = run_bass_kernel_spmd(nc, in_maps, core_ids=list(range(ncores)))
    outs = []
    for c in range(ncores):
        r = res.results[c]
        ys_, yp_ = np.asarray(r["ys"]), np.asarray(r["yp"])
        if c % 2 == 1:
            ys_, yp_ = ys_[::-1], yp_[::-1]
        outs.append((ys_, yp_, r) if debug else (ys_, yp_))
    return outs


def kernel(**inputs):
    inp = {k_: np.asarray(v) for k_, v in inputs.items()}
    x_prompt, x_sample = inp["x_prompt"], inp["x_sample"]
    B, LP, _ = x_prompt.shape
    BS, LS, _ = x_sample.shape
    ncores = 8
    xs_list = [x_sample[c] for c in range(ncores)]
    xp_list = [x_prompt[c // 2] for c in range(ncores)]
    outs = run_cores(inp, xs_list, xp_list, LS, LP, LP // 2, ncores)
    y_sample = np.stack([outs[c][0] for c in range(ncores)], axis=0).astype(np.float32)
    y_prompt = np.zeros_like(x_prompt, dtype=np.float32)
    H = LP // 2
    for c in range(ncores):
        b = c // 2
        if c % 2 == 0:
            y_prompt[b, 0:H] = outs[c][1]
        else:
            y_prompt[b, H:LP] = outs[c][1]
    return (y_prompt, y_sample)
```
